# Optimizing a Trainium2 kernel written in Bass

```python
import jax, jax.numpy as jnp
from jax import lax
import numpy as np

D_MODEL = 2048
BATCH = 2
SEQ = 8192
DEPTH = 4

CHUNK = 64
D_FF = 5632
NORM_EPS = 1e-6

GDN_HEADS = 8
GDN_DK = 128
GDN_DV = 128
GDN_CONV = 4
GDN_QK = GDN_HEADS * GDN_DK
GDN_VW = GDN_HEADS * GDN_DV

RW_HEADS = 16
RW_HEAD = 64
RW_W = RW_HEADS * RW_HEAD
RW_DECAY_LORA = 96
RW_AAA_LORA = 96
RW_GATE_LORA = 256
RW_GN_EPS = 64e-5
RW_SLAB = 3 * RW_W + RW_DECAY_LORA + RW_AAA_LORA + RW_GATE_LORA

IN_SPLITS = (GDN_QK, GDN_QK, GDN_VW, GDN_VW, GDN_HEADS, GDN_HEADS, RW_SLAB, D_MODEL, D_MODEL)
N_IN = 2 * GDN_QK + 2 * GDN_VW + 2 * GDN_HEADS + RW_SLAB + 2 * D_MODEL

kernel_name = "hybrid_gdn_rwkv7_macaron"


def split_cols(t, sizes):
    out = []
    off = 0
    for s in sizes:
        out.append(t[..., off:off + s])
        off += s
    return out


def rmsnorm(x, gain):
    xf = x.astype(jnp.float32)
    y = xf * lax.rsqrt(jnp.mean(xf * xf, axis=-1, keepdims=True) + NORM_EPS)
    return (y * gain.astype(jnp.float32)).astype(x.dtype)


def l2norm(t, eps=1e-6):
    tf = t.astype(jnp.float32)
    return tf * lax.rsqrt(jnp.sum(tf * tf, axis=-1, keepdims=True) + eps)


def swiglu(h, w_gate, w_up, w_down):
    return (jax.nn.silu(h @ w_gate) * (h @ w_up)) @ w_down


def causal_dwconv(x, w):
    K = w.shape[0]
    xp = jnp.pad(x, ((0, 0), (K - 1, 0), (0, 0)))
    return lax.conv_general_dilated(xp, w[:, None, :].astype(x.dtype), window_strides=(1,), padding='VALID',
                                    dimension_numbers=('NWC', 'WIO', 'NWC'), feature_group_count=x.shape[-1])


def token_shift(x):
    return jnp.pad(x, ((0, 0), (1, 0), (0, 0)))[:, :-1]


def gdn_chunked(q, k, v, g, beta):
    B, S, H, DK = q.shape
    DV = v.shape[-1]
    N = S // CHUNK

    def blk(t):
        t = t.reshape((B, N, CHUNK, H) + t.shape[3:])
        return jnp.moveaxis(t, 3, 2)

    q, k, v, g, beta = blk(q), blk(k), blk(v), blk(g), blk(beta)
    gc = jnp.cumsum(g, axis=-1)
    incl = jnp.tril(jnp.ones((CHUNK, CHUNK), bool))
    strict = jnp.tril(jnp.ones((CHUNK, CHUNK), bool), -1)
    diff = gc[..., :, None] - gc[..., None, :]
    gamma = jnp.exp(jnp.where(incl, diff, -jnp.inf))
    kb = k * beta[..., None]
    A = jnp.where(strict, jnp.einsum('bnhid,bnhjd->bnhij', kb, k) * gamma, 0.0)
    eye = jnp.eye(CHUNK, dtype=jnp.float32)
    T = lax.linalg.triangular_solve(A + eye, jnp.broadcast_to(eye, A.shape), left_side=True,
                                    lower=True, unit_diagonal=True)
    U = jnp.einsum('bnhij,bnhjv->bnhiv', T, v * beta[..., None])
    W = jnp.einsum('bnhij,bnhjk->bnhik', T, kb * jnp.exp(gc)[..., None])
    Aqk = jnp.einsum('bnhid,bnhjd->bnhij', q, k) * gamma
    qg = q * jnp.exp(gc)[..., None]
    kg = k * jnp.exp(gc[..., -1:] - gc)[..., None]
    glast = jnp.exp(gc[..., -1])

    def step(state, inp):
        u_c, w_c, aqk_c, qg_c, kg_c, gl_c = inp
        v_new = u_c - jnp.einsum('bhck,bhkv->bhcv', w_c, state)
        o = jnp.einsum('bhck,bhkv->bhcv', qg_c, state) + jnp.einsum('bhij,bhjv->bhiv', aqk_c, v_new)
        state = state * gl_c[..., None, None] + jnp.einsum('bhck,bhcv->bhkv', kg_c, v_new)
        return state, o

    xs = tuple(jnp.moveaxis(t, 1, 0) for t in (U, W, Aqk, qg, kg, glast))
    _, o = lax.scan(step, jnp.zeros((B, H, DK, DV), jnp.float32), xs)
    return jnp.transpose(o, (1, 0, 3, 2, 4)).reshape(B, S, H, DV)


def rwkv7_scan(r, w, k, v, kk, a):
    B, S, H, N = r.shape

    def step(state, inp):
        r_t, w_t, k_t, v_t, kk_t, a_t = inp
        sk = jnp.einsum('bhvk,bhk->bhv', state, kk_t)
        state = (state * w_t[:, :, None, :] - sk[..., None] * (kk_t * a_t)[:, :, None, :]
                 + v_t[..., None] * k_t[:, :, None, :])
        return state, jnp.einsum('bhvk,bhk->bhv', state, r_t)

    xs = tuple(jnp.moveaxis(t, 1, 0) for t in (r, w, k, v, kk, a))
    _, y = lax.scan(step, jnp.zeros((B, H, N, N), jnp.float32), xs)
    return jnp.moveaxis(y, 0, 1)


def gdn_branch(q, k, v, z, a, b, conv_w, a_log, dt_bias, out_gain):
    B, S, _ = q.shape
    qkv = jax.nn.silu(causal_dwconv(jnp.concatenate([q, k, v], axis=-1), conv_w))
    q, k, v = split_cols(qkv, (GDN_QK, GDN_QK, GDN_VW))
    q = l2norm(q.reshape(B, S, GDN_HEADS, GDN_DK)) * (GDN_DK ** -0.5)
    k = l2norm(k.reshape(B, S, GDN_HEADS, GDN_DK))
    v = v.reshape(B, S, GDN_HEADS, GDN_DV).astype(jnp.float32)
    beta = jax.nn.sigmoid(b.astype(jnp.float32))
    g = -jnp.exp(a_log.astype(jnp.float32)) * jax.nn.softplus(a.astype(jnp.float32) + dt_bias.astype(jnp.float32))
    o = gdn_chunked(q, k, v, g, beta)
    o = rmsnorm(o, out_gain) * jax.nn.silu(z.reshape(B, S, GDN_HEADS, GDN_DV).astype(jnp.float32))
    return o.reshape(B, S, GDN_VW).astype(z.dtype)


def rwkv_branch(slab, mu, w0, w_up, a0, a_up, g_up, k_k, k_a, r_k, ln_w, ln_b):
    B, S, _ = slab.shape
    slab = slab + (token_shift(slab) - slab) * mu
    r, k, v, wl, al, gl = split_cols(slab, (RW_W, RW_W, RW_W, RW_DECAY_LORA, RW_AAA_LORA, RW_GATE_LORA))
    w = -jax.nn.softplus(-(w0 + jnp.tanh(wl) @ w_up)) - 0.5
    decay = jnp.exp(-jnp.exp(w.astype(jnp.float32)))
    a = jax.nn.sigmoid(a0 + al @ a_up)
    g = jax.nn.sigmoid(gl) @ g_up
    hs = (B, S, RW_HEADS, RW_HEAD)
    kk = l2norm((k * k_k).reshape(hs))
    k = k * (1.0 + (a - 1.0) * k_a)
    r4 = r.reshape(hs).astype(jnp.float32)
    k4 = k.reshape(hs).astype(jnp.float32)
    v4 = v.reshape(hs).astype(jnp.float32)
    y = rwkv7_scan(r4, decay.reshape(hs), k4, v4, kk, a.reshape(hs).astype(jnp.float32))
    mean = jnp.mean(y, axis=-1, keepdims=True)
    var = jnp.mean(jnp.square(y - mean), axis=-1, keepdims=True)
    y = ((y - mean) * lax.rsqrt(var + RW_GN_EPS)).reshape(B, S, RW_W)
    y = y * ln_w.astype(jnp.float32) + ln_b.astype(jnp.float32)
    bonus = jnp.sum(r4 * k4 * r_k.astype(jnp.float32), axis=-1, keepdims=True) * v4
    y = (y + bonus.reshape(B, S, RW_W)) * g.astype(jnp.float32)
    return y.astype(slab.dtype)


def setup_inputs(seed: int = 0) -> dict:
    key = jax.random.key(seed)
    ks = iter(jax.random.split(key, 48))
    L, D = DEPTH, D_MODEL

    def nrm(shape, scale):
        return jax.random.normal(next(ks), shape, jnp.float32) * scale

    def unif(shape, lo, hi):
        return jax.random.uniform(next(ks), shape, jnp.float32, minval=lo, maxval=hi)

    def gain(shape):
        return 1.0 + nrm(shape, 0.02)

    dt = jnp.exp(unif((L, GDN_HEADS), float(np.log(1e-3)), float(np.log(1e-1))))
    return {
        "x": nrm((BATCH, SEQ, D), 1.0),
        "ffn1_norm": gain((L, D)),
        "ffn1_w_gate": nrm((L, D, D_FF), D ** -0.5),
        "ffn1_w_up": nrm((L, D, D_FF), D ** -0.5),
        "ffn1_w_down": nrm((L, D_FF, D), D_FF ** -0.5),
        "mix_norm": gain((L, D)),
        "w_in": nrm((L, D, N_IN), D ** -0.5),
        "gdn_conv": nrm((L, GDN_CONV, 2 * GDN_QK + GDN_VW), GDN_CONV ** -0.5),
        "gdn_a_log": jnp.log(unif((L, GDN_HEADS), 1.0, 16.0)),
        "gdn_dt_bias": dt + jnp.log(-jnp.expm1(-dt)),
        "gdn_out_norm": gain((L, GDN_DV)),
        "rw_mu": unif((L, RW_SLAB), 0.0, 1.0),
        "rw_w0": unif((L, RW_W), -6.0, 1.0),
        "rw_w_up": nrm((L, RW_DECAY_LORA, RW_W), 0.5 * RW_DECAY_LORA ** -0.5),
        "rw_a0": nrm((L, RW_W), 0.1),
        "rw_a_up": nrm((L, RW_AAA_LORA, RW_W), 0.5 * RW_AAA_LORA ** -0.5),
        "rw_g_up": nrm((L, RW_GATE_LORA, RW_W), RW_GATE_LORA ** -0.5),
        "rw_k_k": 0.85 + nrm((L, RW_W), 0.02),
        "rw_k_a": 1.0 + nrm((L, RW_W), 0.02),
        "rw_r_k": nrm((L, RW_HEADS, RW_HEAD), 0.1),
        "rw_ln_w": gain((L, RW_W)),
        "rw_ln_b": nrm((L, RW_W), 0.02),
        "w_branch_a": nrm((L, GDN_VW, D), GDN_VW ** -0.5),
        "w_branch_b": nrm((L, RW_W, D), RW_W ** -0.5),
        "w_out": nrm((L, D, D), D ** -0.5),
        "ffn2_norm": gain((L, D)),
        "ffn2_w_gate": nrm((L, D, D_FF), D ** -0.5),
        "ffn2_w_up": nrm((L, D, D_FF), D ** -0.5),
        "ffn2_w_down": nrm((L, D_FF, D), D_FF ** -0.5),
        "final_norm": gain((D,)),
    }


def reference(x, ffn1_norm, ffn1_w_gate, ffn1_w_up, ffn1_w_down, mix_norm, w_in, gdn_conv, gdn_a_log,
              gdn_dt_bias, gdn_out_norm, rw_mu, rw_w0, rw_w_up, rw_a0, rw_a_up, rw_g_up, rw_k_k, rw_k_a,
              rw_r_k, rw_ln_w, rw_ln_b, w_branch_a, w_branch_b, w_out, ffn2_norm, ffn2_w_gate, ffn2_w_up,
              ffn2_w_down, final_norm):
    h = x
    for l in range(DEPTH):
        h = h + 0.5 * swiglu(rmsnorm(h, ffn1_norm[l]), ffn1_w_gate[l], ffn1_w_up[l], ffn1_w_down[l])
        u = rmsnorm(h, mix_norm[l])
        p = u @ w_in[l]
        q, k, v, z, a, b, slab, gate_a, gate_b = split_cols(p, IN_SPLITS)
        o_a = gdn_branch(q, k, v, z, a, b, gdn_conv[l], gdn_a_log[l], gdn_dt_bias[l], gdn_out_norm[l])
        o_b = rwkv_branch(slab, rw_mu[l], rw_w0[l], rw_w_up[l], rw_a0[l], rw_a_up[l], rw_g_up[l],
                          rw_k_k[l], rw_k_a[l], rw_r_k[l], rw_ln_w[l], rw_ln_b[l])
        y = jax.nn.sigmoid(gate_a) * (o_a @ w_branch_a[l]) + jax.nn.sigmoid(gate_b) * (o_b @ w_branch_b[l])
        h = h + y @ w_out[l]
        h = h + 0.5 * swiglu(rmsnorm(h, ffn2_norm[l]), ffn2_w_gate[l], ffn2_w_up[l], ffn2_w_down[l])
    return rmsnorm(h, final_norm)
```

```python
import numpy as np
import ml_dtypes
from contextlib import ExitStack
import concourse.bass as bass
import concourse.mybir as mybir
from concourse.bass_utils import run_bass_kernel_spmd

F32 = mybir.dt.float32
BF16 = mybir.dt.bfloat16
AF = mybir.ActivationFunctionType
ALU = mybir.AluOpType
AX = mybir.AxisListType

SEM_LIM = 30000


class Sched:
    def __init__(self, nc, same_engine_sync=True):
        self.nc = nc
        self.ops = []
        self.lastw = {}
        self.readers = {}
        self.last_dma = {}
        self.exclusive = set()
        self.same_engine_sync = same_engine_sync

    def add(self, eng, fn, reads=(), writes=(), dma_key=None):
        idx = len(self.ops)
        deps = set()
        for r in reads:
            w = self.lastw.get(r)
            if w is not None:
                deps.add(w)
            if r in self.exclusive:
                for k_, x in self.readers.get(r, {}).items():
                    if k_[0] != eng:
                        deps.add(x)
        for w_ in writes:
            w = self.lastw.get(w_)
            if w is not None:
                deps.add(w)
            for x in self.readers.get(w_, {}).values():
                deps.add(x)
        if dma_key is not None:
            p = self.last_dma.get(dma_key)
            if p is not None:
                deps.add(p)
            self.last_dma[dma_key] = idx
        rk = (eng, idx) if dma_key is not None else (eng, -1)
        for r in reads:
            self.readers.setdefault(r, {})[rk] = idx
        for w_ in writes:
            self.lastw[w_] = idx
            self.readers[w_] = {}
        deps.discard(idx)
        self.ops.append(dict(eng=eng, fn=fn, deps=deps, dma_key=dma_key))
        return idx

    def emit(self, final_wait_keys=()):
        nc = self.nc
        ops = self.ops
        needed = set()
        for o in ops:
            eff = set()
            for j in o['deps']:
                d = ops[j]
                if d['dma_key'] is None and d['eng'] == o['eng'] and (o['eng'] == 'pe' or not self.same_engine_sync):
                    continue
                eff.add(j)
            o['deps'] = eff
            needed |= eff
        final_ops = [self.last_dma[k] for k in final_wait_keys if k in self.last_dma]
        needed |= set(final_ops)
        eng_count = {}
        dma_count = {}
        sem_names = set()
        for i, o in enumerate(ops):
            o['sig'] = None
            if i not in needed:
                continue
            if o['dma_key'] is not None:
                k = o['dma_key']
                dma_count[k] = dma_count.get(k, 0) + 1
                o['sig'] = ('d_%s' % (k,), 16 * dma_count[k], 16)
            else:
                e = o['eng']
                c = eng_count.get(e, 0)
                eng_count[e] = c + 1
                o['sig'] = ('e_%s_%d' % (e, c // SEM_LIM), (c % SEM_LIM) + 1, 1)
            sem_names.add(o['sig'][0])
        engs = ['sp', 'act', 'dve', 'pe', 'pool']
        per_eng = {e: [] for e in engs}
        for i, o in enumerate(ops):
            per_eng[o['eng']].append(i)
        with ExitStack() as es:
            sems = {}
            for n in sorted(sem_names):
                sems[n] = es.enter_context(nc.semaphore(n))
            block = es.enter_context(nc.Block())

            def run_engine(ename, eobj):
                waited = {}
                for i in per_eng[ename]:
                    o = ops[i]
                    for j in sorted(o['deps']):
                        d = ops[j]
                        if d['dma_key'] is None and d['eng'] == ename:
                            if ename == 'pe' or not self.same_engine_sync:
                                continue
                        sname, val, _ = d['sig']
                        if waited.get(sname, 0) >= val:
                            continue
                        eobj.wait_ge(sems[sname], val)
                        waited[sname] = val
                    ins = o['fn'](eobj)
                    if o['sig'] is not None:
                        ins.then_inc(sems[o['sig'][0]], o['sig'][2])
                if ename == 'sp':
                    for j in final_ops:
                        sname, val, _ = ops[j]['sig']
                        if waited.get(sname, 0) >= val:
                            continue
                        eobj.wait_ge(sems[sname], val)
                        waited[sname] = val

            @block.sync
            def _(e):
                run_engine('sp', e)

            @block.scalar
            def _(e):
                run_engine('act', e)

            @block.vector
            def _(e):
                run_engine('dve', e)

            @block.tensor
            def _(e):
                run_engine('pe', e)

            @block.gpsimd
            def _(e):
                run_engine('pool', e)
        return dict(n_ops=len(ops), eng_count=eng_count, n_sems=len(sem_names))


class Ctx:
    def __init__(self, name="k"):
        self.nc = bass.Bass("TRN2", target_bir_lowering=False)
        self.S = Sched(self.nc)
        self.es = ExitStack()
        self.psum = []
        self.psum_i = 0
        self.uid = 0

    def dram(self, name, shape, dtype, kind="Internal"):
        return self.nc.dram_tensor(name, list(shape), dtype, kind=kind).ap()

    def sb(self, name, shape, dtype):
        return self.es.enter_context(self.nc.sbuf_tensor(name, list(shape), dtype))

    def init_psum(self, n=8):
        for i in range(n):
            t = self.es.enter_context(self.nc.psum_tensor("ps%d" % i, [128, 512], F32))
            self.psum.append((t, 'ps%d' % i))
            self.S.exclusive.add('ps%d' % i)

    def next_psum(self):
        t = self.psum[self.psum_i % len(self.psum)]
        self.psum_i += 1
        return t

    def dma(self, eng, out, in_, reads, writes, key):
        return self.S.add(eng, lambda e: e.dma_start(out=out, in_=in_), reads, writes, dma_key=key)

    def mm(self, out, lhsT, rhs, start, stop, reads, writes):
        return self.S.add('pe', lambda e: e.matmul(out, lhsT, rhs, start=start, stop=stop), reads, writes)

    def act(self, out, in_, func, reads, writes, bias=None, scale=None):
        kw = {}
        if bias is not None:
            kw['bias'] = bias
        if scale is not None:
            kw['scale'] = scale
        return self.S.add('act', lambda e: e.activation(out=out, in_=in_, func=func, **kw), reads, writes)

    def tt(self, eng, out, in0, in1, op, reads, writes):
        return self.S.add(eng, lambda e: e.tensor_tensor(out=out, in0=in0, in1=in1, op=op), reads, writes)

    def stt(self, out, in0, scalar, in1, op0, op1, reads, writes, eng='dve'):
        return self.S.add(eng, lambda e: e.scalar_tensor_tensor(out=out, in0=in0, scalar=scalar, in1=in1,
                                                                op0=op0, op1=op1), reads, writes)

    def ts(self, eng, out, in0, s1, s2, op0, op1, reads, writes):
        if s2 is None:
            return self.S.add(eng, lambda e: e.tensor_scalar(out=out, in0=in0, scalar1=s1, scalar2=None, op0=op0),
                              reads, writes)
        return self.S.add(eng, lambda e: e.tensor_scalar(out=out, in0=in0, scalar1=s1, scalar2=s2, op0=op0, op1=op1),
                          reads, writes)

    def copy(self, eng, out, in_, reads, writes):
        if eng == 'act':
            return self.S.add('act', lambda e: e.copy(out=out, in_=in_), reads, writes)
        return self.S.add(eng, lambda e: e.tensor_copy(out=out, in_=in_), reads, writes)

    def memset(self, eng, ap, val, writes):
        return self.S.add(eng, lambda e: e.memset(ap, val), (), writes)


NORM_EPS = 1e-6


def emit_rmsnorm(c, h_sb, hres, g_sb, gres, out_sb, outres, KC, NT, D, ones_bf, sq_bufs, rstd_sb):
    ps, psr = c.next_psum()
    for kc in range(KC):
        sq, sqr = sq_bufs[kc % len(sq_bufs)]
        c.act(sq[:, :NT], h_sb[:, kc, :], AF.Square, [(hres, kc)], [sqr])
        c.mm(ps[:, :NT], ones_bf[:, :], sq[:, :NT], kc == 0, kc == KC - 1, [sqr, 'ones'], [psr])
    c.act(rstd_sb[:, :NT], ps[:, :NT], AF.Ln, [psr], ['rstd'], bias=c.eps_ap, scale=1.0 / D)
    c.act(rstd_sb[:, :NT], rstd_sb[:, :NT], AF.Exp, ['rstd'], ['rstd'], scale=-0.5)
    for kc in range(KC):
        c.stt(out_sb[:, kc, :], h_sb[:, kc, :], g_sb[:, kc:kc + 1], rstd_sb[:, :NT], ALU.mult, ALU.mult,
              [(hres, kc), 'rstd', gres], [(outres, kc)])


def emit_ffn(c, n_sb, nres, h_sb, hres, hid_sb, wg_d, wu_d, wd_d, wres, KC, NT, DFF, wslots, wdslots, sg_bufs):
    NF = DFF // 128
    FS = 256 if DFF % 256 == 0 else 128
    NJ = FS // 128
    wgv = wg_d.rearrange("(kc p) f -> p kc f", p=128)
    wuv = wu_d.rearrange("(kc p) f -> p kc f", p=128)
    for fs in range(DFF // FS):
        (wg_sb, wgr), (wu_sb, wur) = wslots[fs % len(wslots)]
        c.dma('sp', wg_sb[:, :, :FS], wgv[:, :, fs * FS:(fs + 1) * FS], [wres[0]], [wgr], key=wgr)
        c.dma('sp', wu_sb[:, :, :FS], wuv[:, :, fs * FS:(fs + 1) * FS], [wres[1]], [wur], key=wur)
        for j in range(NJ):
            f = fs * NJ + j
            pg, pgr = c.next_psum()
            pu, pur = c.next_psum()
            for kc in range(KC):
                c.mm(pg[:, :NT], wg_sb[:, kc, j * 128:(j + 1) * 128], n_sb[:, kc, :], kc == 0, kc == KC - 1,
                     [wgr, (nres, kc)], [pgr])
            for kc in range(KC):
                c.mm(pu[:, :NT], wu_sb[:, kc, j * 128:(j + 1) * 128], n_sb[:, kc, :], kc == 0, kc == KC - 1,
                     [wur, (nres, kc)], [pur])
            sg, sgr = sg_bufs[f % len(sg_bufs)]
            c.act(sg[:, :NT], pg[:, :NT], AF.Silu, [pgr], [sgr])
            c.tt('dve', hid_sb[:, f, :], sg[:, :NT], pu[:, :NT], ALU.mult, [sgr, pur], [('hid', f)])
    NDC = min(4, KC)
    FG = 11 if NF % 11 == 0 else (4 if NF % 4 == 0 else 1)
    wdv = wd_d.rearrange("(f p) d -> p f d", p=128)
    li = 0
    for p0 in range(0, KC, NDC):
        accs = [c.next_psum() for _ in range(NDC)]
        for fg in range(NF // FG):
            wd_sb, wdr = wdslots[li % len(wdslots)]
            li += 1
            c.dma('sp', wd_sb[:, :FG, :NDC * 128], wdv[:, fg * FG:(fg + 1) * FG, p0 * 128:(p0 + NDC) * 128],
                  [wres[2]], [wdr], key=wdr)
            for fi in range(FG):
                f = fg * FG + fi
                for di in range(NDC):
                    ps, psr = accs[di]
                    c.mm(ps[:, :NT], wd_sb[:, fi, di * 128:(di + 1) * 128], hid_sb[:, f, :], f == 0, f == NF - 1,
                         [wdr, ('hid', f)], [psr])
        for di in range(NDC):
            ps, psr = accs[di]
            dc = p0 + di
            c.stt(h_sb[:, dc, :], ps[:, :NT], 0.5, h_sb[:, dc, :], ALU.mult, ALU.add, [psr, (hres, dc)], [(hres, dc)])


def cast_dram(c, dst, src, rows_per=256, sres=None, dres=None):
    R = src.shape[0]
    for r0 in range(0, R, rows_per):
        r1 = min(R, r0 + rows_per)
        c.dma('pool', dst[r0:r1, :], src[r0:r1, :], [], [dres], key=dres + "_%d" % (r0 // rows_per % 4))


def build_k1(D, DFF, NFM, NAB, T, NT=512):
    c = Ctx()
    nc = c.nc
    KC = D // 128
    NF = DFF // 128
    hT = c.dram("hT", [D, T], F32, "ExternalInput")
    g1 = c.dram("g1", [128, KC], F32, "ExternalInput")
    g2 = c.dram("g2", [128, KC], F32, "ExternalInput")
    wg = c.dram("wg", [D, DFF], F32, "ExternalInput")
    wu = c.dram("wu", [D, DFF], F32, "ExternalInput")
    wd = c.dram("wd", [DFF, D], F32, "ExternalInput")
    wfm = c.dram("wfm", [D, NFM], F32, "ExternalInput")
    wab = c.dram("wab", [D, NAB], F32, "ExternalInput")
    h1T = c.dram("h1T", [D, T], F32, "ExternalOutput")
    PT = c.dram("PT", [NFM, T], BF16, "ExternalOutput")
    ab = c.dram("ab", [T, NAB], F32, "ExternalOutput")
    wg_b = c.dram("wg_b", [D, DFF], BF16)
    wu_b = c.dram("wu_b", [D, DFF], BF16)
    wd_b = c.dram("wd_b", [DFF, D], BF16)
    wfm_b = c.dram("wfm_b", [D, NFM], BF16)
    c.init_psum(8)
    h_sb = c.sb("h_sb", [128, KC, NT], F32)
    n_sb = c.sb("n_sb", [128, KC, NT], BF16)
    hid_sb = c.sb("hid_sb", [128, NF, NT], BF16)
    g1_sb = c.sb("g1_sb", [128, KC], F32)
    g2_sb = c.sb("g2_sb", [128, KC], F32)
    ones_bf = c.sb("ones_bf", [128, 128], BF16)
    eps_sb = c.sb("eps_sb", [128, 1], F32)
    rstd_sb = c.sb("rstd_sb", [128, NT], F32)
    sq_bufs = [(c.sb("sq%d" % i, [128, NT], BF16), "sq%d" % i) for i in range(2)]
    sg_bufs = [(c.sb("sg%d" % i, [128, NT], F32), "sg%d" % i) for i in range(2)]
    wslots = [((c.sb("wga%d" % i, [128, KC, 256], BF16), "wga%d" % i),
               (c.sb("wua%d" % i, [128, KC, 256], BF16), "wua%d" % i)) for i in range(2)]
    wdslots = [(c.sb("wds%d" % i, [128, 11, 512], BF16), "wds%d" % i) for i in range(2)]
    wab_sb = c.sb("wab_sb", [128, KC, NAB], F32)
    wab_bf = c.sb("wab_bf", [128, KC, NAB], BF16)
    pt_bufs = [(c.sb("pt%d" % i, [128, 2, NT], BF16), "pt%d" % i) for i in range(2)]
    ab_bufs = [(c.sb("abs%d" % i, [128, NAB], F32), "abs%d" % i) for i in range(2)]
    c.eps_ap = eps_sb[:, 0:1]

    c.memset('pool', ones_bf[:, :], 1.0, ['ones'])
    c.memset('pool', eps_sb[:, :], NORM_EPS, ['eps'])
    c.dma('sp', g1_sb[:, :], g1[:, :], [], ['g1'], key='g1')
    c.dma('sp', g2_sb[:, :], g2[:, :], [], ['g2'], key='g2')
    c.dma('sp', wab_sb[:, :, :], wab.rearrange("(kc p) n -> p kc n", p=128), [], ['wab'], key='wab')
    c.copy('dve', wab_bf[:, :, :], wab_sb[:, :, :], ['wab'], ['wabb'])
    cast_dram(c, wg_b, wg, dres='wg_b')
    cast_dram(c, wu_b, wu, dres='wu_b')
    cast_dram(c, wd_b, wd, dres='wd_b')
    cast_dram(c, wfm_b, wfm, dres='wfm_b')

    hv = hT.rearrange("(kc p) t -> p kc t", p=128)
    h1v = h1T.rearrange("(kc p) t -> p kc t", p=128)
    PTv = PT.rearrange("(cc p) t -> p cc t", p=128)
    wfv = wfm_b.rearrange("(kc p) f -> p kc f", p=128)
    NCH = NFM // 128
    CS = 2
    for tt in range(T // NT):
        tsl = slice(tt * NT, (tt + 1) * NT)
        c.dma('sp', h_sb[:, :, :], hv[:, :, tsl], [], [('h', kc) for kc in range(KC)], key='hload')
        emit_rmsnorm(c, h_sb, 'h', g1_sb, 'g1', n_sb, 'n', KC, NT, D, ones_bf, sq_bufs, rstd_sb)
        emit_ffn(c, n_sb, 'n', h_sb, 'h', hid_sb, wg_b, wu_b, wd_b, ('wg_b', 'wu_b', 'wd_b'), KC, NT, DFF,
                 wslots, wdslots, sg_bufs)
        c.dma('sp', h1v[:, :, tsl], h_sb[:, :, :], [('h', kc) for kc in range(KC)], ['h1T'], key='hstore')
        emit_rmsnorm(c, h_sb, 'h', g2_sb, 'g2', n_sb, 'n', KC, NT, D, ones_bf, sq_bufs, rstd_sb)
        si = 0
        for cs in range(0, NCH, CS):
            ncs = min(CS, NCH - cs)
            (w_sb, wr), _ = wslots[si % len(wslots)]
            pt_sb, ptr = pt_bufs[si % len(pt_bufs)]
            si += 1
            c.dma('sp', w_sb[:, :, :ncs * 128], wfv[:, :, cs * 128:(cs + ncs) * 128], ['wfm_b'], [wr], key=wr)
            for j in range(ncs):
                ps, psr = c.next_psum()
                for kc in range(KC):
                    c.mm(ps[:, :NT], w_sb[:, kc, j * 128:(j + 1) * 128], n_sb[:, kc, :], kc == 0, kc == KC - 1,
                         [wr, ('n', kc)], [psr])
                if j % 2 == 0:
                    c.copy('act', pt_sb[:, j, :], ps[:, :NT], [psr], [(ptr, j)])
                else:
                    c.copy('dve', pt_sb[:, j, :], ps[:, :NT], [psr], [(ptr, j)])
            c.dma('sp', PTv[:, cs:cs + ncs, tsl], pt_sb[:, :ncs, :], [(ptr, j) for j in range(ncs)], ['PT'],
                  key='ptstore%d' % (si % 2))
        for tk in range(NT // 128):
            ps, psr = c.next_psum()
            for kc in range(KC):
                c.mm(ps[:, :NAB], n_sb[:, kc, tk * 128:(tk + 1) * 128], wab_bf[:, kc, :], kc == 0, kc == KC - 1,
                     ['wabb', ('n', kc)], [psr])
            ab_sb, abr = ab_bufs[tk % 2]
            c.copy('dve', ab_sb[:, :], ps[:, :NAB], [psr], [abr])
            c.dma('sp', ab[tt * NT + tk * 128: tt * NT + (tk + 1) * 128, :], ab_sb[:, :], [abr], ['ab'],
                  key='abstore%d' % (tk % 2))
    info = c.S.emit(final_wait_keys=['hstore', 'ptstore0', 'ptstore1', 'abstore0', 'abstore1'])
    c.es.close()
    return nc, info


NEG = -30000.0
import os as _os
DBG_STOP = int(_os.environ.get('K2_STOP', '0'))
CST_NAMES = ['ident', 'ones', 'ltri', 'bd', 'c0', 'c1', 'ms', 'msT', 'miT', 'm_s', 'mT_s', 'mT_i']


def k2_consts():
    i = np.arange(128)
    same = (i[:, None] // 64) == (i[None, :] // 64)
    P, Fr = i[:, None], i[None, :]
    t = {}
    t['ident'] = np.eye(128)
    t['ones'] = np.ones((128, 128))
    t['ltri'] = same & (P <= Fr)
    t['bd'] = same
    t['c0'] = np.broadcast_to(P < 64, (128, 128))
    t['c1'] = np.broadcast_to(P >= 64, (128, 128))
    t['ms'] = np.where(same & (P > Fr), 0.0, NEG)
    t['msT'] = np.where(same & (Fr > P), 0.0, NEG)
    t['miT'] = np.where(same & (Fr >= P), 0.0, NEG)
    t['m_s'] = same & (Fr < P)
    t['mT_s'] = same & (P < Fr)
    t['mT_i'] = same & (P <= Fr)
    return np.ascontiguousarray(np.concatenate([np.asarray(t[n], np.float32) for n in CST_NAMES], axis=1))


class K2:
    def __init__(self, T2, TT=256, do_gdn=True, do_rwkv=True):
        self.T2, self.TT = T2, TT
        self.NS = TT // 128
        c = self.c = Ctx()
        self.do_gdn, self.do_rwkv = do_gdn, do_rwkv
        NS = self.NS
        self.cst_d = c.dram("cst", [128, 128 * len(CST_NAMES)], F32, "ExternalInput")
        c.init_psum(8)
        cst = self.cst = c.sb("cst_sb", [128, 128 * len(CST_NAMES)], F32)
        c.dma('sp', cst[:, :], self.cst_d[:, :], [], ['cst'], key='cst')
        self.cf = {n: cst[:, i * 128:(i + 1) * 128] for i, n in enumerate(CST_NAMES)}
        self.ident_bf = c.sb("ident_bf", [128, 128], BF16)
        self.ones_bf = c.sb("ones_bf", [128, 128], BF16)
        c.copy('dve', self.ident_bf[:, :], self.cf['ident'], ['cst'], ['cstb'])
        c.copy('dve', self.ones_bf[:, :], self.cf['ones'], ['cst'], ['cstb'])
        self.eps_sb = c.sb("eps_sb", [128, 1], F32)
        self.one_sb = c.sb("one_sb", [128, 1], F32)
        c.memset('pool', self.eps_sb[:, :], 1e-6, ['eps'])
        c.memset('pool', self.one_sb[:, :], 1.0, ['one'])
        self.final_keys = []
        NCH = NS * 4
        self.invP = [[c.sb("invP%d_%d" % (ch, p), [128, 128], F32) for p in range(2)] for ch in range(NCH)]
        self.invQ = [[c.sb("invQ%d_%d" % (ch, p), [128, 128], F32) for p in range(2)] for ch in range(NCH)]
        self.invT = [[c.sb("invT%d_%d" % (ch, p), [128, 128], F32) for p in range(2)] for ch in range(NCH)]
        if do_gdn:
            self.init_gdn()
        if do_rwkv:
            self.init_rwkv()
        for tt in range(T2 // TT):
            if do_gdn:
                self.gdn_tile(tt)
            if do_rwkv:
                self.rwkv_tile(tt)
        self.info = c.S.emit(final_wait_keys=self.final_keys)
        c.es.close()

    def tr_batch(self, items):
        c = self.c
        for i0 in range(0, len(items), 4):
            ps, res = c.next_psum()
            grp = items[i0:i0 + 4]
            for i, (in_ap, rd, ev) in enumerate(grp):
                c.mm(ps[:, i * 128:(i + 1) * 128], in_ap, self.ident_bf[:, :], True, True, list(rd) + ['cstb'], [res])
            for i, (in_ap, rd, ev) in enumerate(grp):
                ev(ps[:, i * 128:(i + 1) * 128], res)

    def init_gdn(self):
        c, NS, TT, T2 = self.c, self.NS, self.TT, self.T2
        HA = self.HA = 2
        self.xg = c.dram("xg", [6 * 128, T2], BF16, "ExternalInput")
        self.abd = c.dram("abd", [T2, 4], F32, "ExternalInput")
        self.gpar_d = c.dram("gpar", [128, 24 + 4 + 128], F32, "ExternalInput")
        self.oaT = c.dram("oaT", [HA * 128, T2], BF16, "ExternalOutput")
        gp = self.gpar = c.sb("gpar_sb", [128, 156], F32)
        c.dma('sp', gp[:, :], self.gpar_d[:, :], [], ['gpar'], key='gpar')
        self.cw = lambda ci, j: gp[:, ci * 4 + j: ci * 4 + j + 1]
        self.gain_rep = gp[:, 28:156]
        self.dtb4 = c.sb("dtb4", [128, NS, 2], F32)
        self.nea4 = c.sb("nea4", [128, NS, 2], F32)
        nea = c.sb("nea", [128, 2], F32)
        c.act(nea[:, :], gp[:, 24:26], AF.Exp, ['gpar'], ['nea'])
        for s in range(NS):
            c.copy('dve', self.dtb4[:, s, :], gp[:, 26:28], ['gpar'], [('dtb4', s)])
            c.ts('dve', self.nea4[:, s, :], nea[:, :], -1.0, None, ALU.mult, None, ['nea'], [('nea4', s)])
        self.graw = [[c.sb("graw%d_%d" % (p, ci), [128, 3 + TT], BF16) for ci in range(6)] for p in range(2)]
        self.gacc = [c.sb("gacc%d" % i, [128, TT], F32) for i in range(2)]
        self.gxc = [c.sb("gxc%d" % ci, [128, TT], BF16) for ci in range(6)]
        self.gsq = [c.sb("gsq%d" % i, [128, TT], BF16) for i in range(2)]
        self.abt = c.sb("abt", [128, NS, 4], F32)
        self.lnr = c.sb("lnr", [128, NS, 4], F32)
        names = ['gx', 'ge', 'gg', 'glnb', 'gc', 'glrow', 'ga1', 'gb1', 'ga2', 'gns1', 'gs2', 'gs3', 'gbeta',
                 'gl0', 'gl1']
        self.gcol = {n: c.sb(n, [128, NS, 2], F32) for n in names}
        self.k_tok = [[c.sb("ktok%d_%d" % (s, h), [128, 128], BF16) for h in range(2)] for s in range(NS)]
        self.vb_tok = [[c.sb("vbtok%d_%d" % (s, h), [128, 128], F32) for h in range(2)] for s in range(NS)]
        self.AqkT = [[c.sb("aqkT%d_%d" % (s, h), [128, 128], BF16) for h in range(2)] for s in range(NS)]
        self.TTb = [[c.sb("TTb%d_%d" % (s, h), [128, 128], BF16) for h in range(2)] for s in range(NS)]
        self.o_tok = [[c.sb("otok%d_%d" % (s, h), [128, 128], F32) for h in range(2)] for s in range(NS)]
        self.diag = [c.sb("diag%d" % i, [128, 128], F32) for i in range(3)]
        self.Emat = [c.sb("emat%d" % i, [128, 128], F32) for i in range(3)]
        self.S_f = [c.sb("S_f%d" % h, [128, 128], F32) for h in range(2)]
        self.S_b = [c.sb("S_b%d" % h, [128, 128], BF16) for h in range(2)]
        for h in range(2):
            c.memset('pool', self.S_f[h][:, :], 0.0, [('S_f', h)])
            c.memset('pool', self.S_b[h][:, :], 0.0, [('S_b', h)])
        self.R_bf = [c.sb("R_bf%d" % h, [128, 128], BF16) for h in range(2)]
        self.x2s = [c.sb("x2s%d" % h, [128, 128], F32) for h in range(2)]
        self.vn = [c.sb("vn%d" % h, [128, 128], BF16) for h in range(2)]
        self.vs = [c.sb("vs%d" % h, [128, 128], BF16) for h in range(2)]
        self.on_bf = [c.sb("on_bf%d" % i, [128, 128], BF16) for i in range(NS)]
        self.ojunk = c.sb("ojunk", [128, 128], F32)
        self.oss = c.sb("oss", [128, 2], F32)
        self.ostage = [[c.sb("ostage%d_%d" % (p, h), [128, TT], BF16) for h in range(2)] for p in range(2)]
        self.final_keys += ['oast0', 'oast1']

    def gdn_tile(self, tt):
        c, NS, TT = self.c, self.NS, self.TT
        cf = self.cf
        par = tt % 2
        t0 = tt * TT
        G = self.gcol
        for ci in range(6):
            raw = self.graw[par][ci]
            rr = ('graw', par, ci)
            if tt == 0:
                c.memset('pool', raw[:, 0:3], 0.0, [rr])
                c.dma('sp', raw[:, 3:3 + TT], self.xg[ci * 128:(ci + 1) * 128, 0:TT], [], [rr], key='graw%d_%d' % (par, ci))
            else:
                c.dma('sp', raw[:, :], self.xg[ci * 128:(ci + 1) * 128, t0 - 3:t0 + TT], [], [rr],
                      key='graw%d_%d' % (par, ci))
            acc = self.gacc[ci % 2]
            ar = ('gacc', ci % 2)
            c.ts('dve', acc[:, :], raw[:, 3:3 + TT], self.cw(ci, 3), None, ALU.mult, None, [rr, 'gpar'], [ar])
            for j in (2, 1, 0):
                c.stt(acc[:, :], raw[:, j:j + TT], self.cw(ci, j), acc[:, :], ALU.mult, ALU.add, [rr, 'gpar', ar], [ar])
            c.act(self.gxc[ci][:, :], acc[:, :], AF.Silu, [ar], [('gxc', ci)])
        if DBG_STOP == 1:
            return
        ps_ss, ps_ssr = c.next_psum()
        for ci in range(4):
            sq = self.gsq[ci % 2]
            sr = ('gsq', ci % 2)
            c.act(sq[:, :], self.gxc[ci][:, :], AF.Square, [('gxc', ci)], [sr])
            for s in range(NS):
                c.mm(ps_ss[:, s * 4 + ci: s * 4 + ci + 1], sq[:, s * 128:(s + 1) * 128], self.ones_bf[:, 0:1], True, True,
                     [sr, 'cstb'], [ps_ssr])
        lnr = self.lnr
        c.act(lnr[:, :, :].rearrange("p s c -> p (s c)"), ps_ss[:, 0:NS * 4], AF.Ln, [ps_ssr, 'eps'], ['lnr'],
              bias=self.eps_sb[:, 0:1])
        c.ts('dve', lnr[:, :, :], lnr[:, :, :], -0.5, None, ALU.mult, None, ['lnr'], ['lnr'])
        if DBG_STOP == 2:
            return
        abt = self.abt
        c.dma('sp', abt[:, :, :], self.abd[t0:t0 + TT, :].rearrange("(s p) c -> p s c", p=128), [], ['abt'], key='abt')
        c.tt('dve', G['gx'][:, :, :], abt[:, :, 0:2], self.dtb4[:, :, :], ALU.add, ['abt'] + [('dtb4', s) for s in range(NS)], ['gx'])
        c.act(G['ge'][:, :, :], G['gx'][:, :, :], AF.Exp, ['gx'], ['ge'])
        c.act(G['ge'][:, :, :], G['ge'][:, :, :], AF.Ln, ['ge', 'one'], ['ge'], bias=self.one_sb[:, 0:1])
        c.tt('dve', G['gg'][:, :, :], G['ge'][:, :, :], self.nea4[:, :, :], ALU.mult, ['ge'] + [('nea4', s) for s in range(NS)], ['gg'])
        c.act(G['glnb'][:, :, :], abt[:, :, 2:4], AF.Exp, ['abt'], ['glnb'], scale=-1.0)
        c.act(G['glnb'][:, :, :], G['glnb'][:, :, :], AF.Ln, ['glnb', 'one'], ['glnb'], bias=self.one_sb[:, 0:1])
        ggf = G['gg'][:, :, :].rearrange("p s c -> p (s c)")
        NC2 = NS * 2
        ps_g, ps_gr = c.next_psum()
        c.mm(ps_g[:, 0:NC2], cf['ltri'], ggf, True, True, ['cst', 'gg'], [ps_gr])
        c.mm(ps_g[:, 16:16 + NC2], cf['bd'], ggf, True, True, ['cst', 'gg'], [ps_gr])
        c.mm(ps_g[:, 32:32 + NC2], cf['c0'], ggf, True, True, ['cst', 'gg'], [ps_gr])
        c.mm(ps_g[:, 48:48 + NC2], cf['c1'], ggf, True, True, ['cst', 'gg'], [ps_gr])
        fl = lambda n: G[n][:, :, :].rearrange("p s c -> p (s c)")
        c.copy('dve', fl('gc'), ps_g[:, 0:NC2], [ps_gr], ['gc'])
        c.copy('dve', fl('glrow'), ps_g[:, 16:16 + NC2], [ps_gr], ['glrow'])
        c.act(fl('gl0'), ps_g[:, 32:32 + NC2], AF.Exp, [ps_gr], ['gl0'])
        c.act(fl('gl1'), ps_g[:, 48:48 + NC2], AF.Exp, [ps_gr], ['gl1'])
        lnrq, lnrk = lnr[:, :, 0:2], lnr[:, :, 2:4]
        A3 = lambda n: G[n][:, :, :]
        c.tt('dve', A3('ga1'), A3('gc'), A3('glnb'), ALU.subtract, ['gc', 'glnb'], ['ga1'])
        c.tt('dve', A3('ga1'), A3('ga1'), lnrk, ALU.add, ['ga1', 'lnr'], ['ga1'])
        c.tt('dve', A3('gb1'), lnrk, A3('gc'), ALU.subtract, ['gc', 'lnr'], ['gb1'])
        c.stt(A3('ga2'), A3('gc'), float(np.log(128.0 ** -0.5)), lnrq, ALU.add, ALU.add, ['gc', 'lnr'], ['ga2'])
        c.act(A3('gns1'), A3('ga1'), AF.Exp, ['ga1'], ['gns1'])
        c.ts('dve', A3('gns1'), A3('gns1'), -1.0, None, ALU.mult, None, ['gns1'], ['gns1'])
        c.act(A3('gs2'), A3('ga2'), AF.Exp, ['ga2'], ['gs2'])
        c.tt('dve', A3('gs3'), A3('gb1'), A3('glrow'), ALU.add, ['gb1', 'glrow'], ['gs3'])
        c.act(A3('gs3'), A3('gs3'), AF.Exp, ['gs3'], ['gs3'])
        c.act(A3('gbeta'), A3('glnb'), AF.Exp, ['glnb'], ['gbeta'], scale=-1.0)
        if DBG_STOP == 3:
            return
        items = []
        for s in range(NS):
            for h in range(2):
                items.append((self.gxc[2 + h][:, s * 128:(s + 1) * 128], [('gxc', 2 + h)],
                              lambda pt, res, s=s, h=h: c.copy('act', self.k_tok[s][h][:, :], pt, [res], [('ktok', s, h)])))
                items.append((self.gxc[4 + h][:, s * 128:(s + 1) * 128], [('gxc', 4 + h)],
                              lambda pt, res, s=s, h=h: c.ts('dve', self.vb_tok[s][h][:, :], pt, G['gbeta'][:, s, h:h + 1], None,
                                                              ALU.mult, None, [res, 'gbeta'], [('vbtok', s, h)])))
        self.tr_batch(items)
        if DBG_STOP == 4:
            return
        for s in range(NS):
            for h in range(2):
                ch = s * 2 + h
                ssl = slice(s * 128, (s + 1) * 128)
                kc_, qc_ = self.gxc[2 + h], self.gxc[h]
                psK, psKr = c.next_psum()
                c.mm(psK[:, 0:128], kc_[:, ssl], kc_[:, ssl], True, True, [('gxc', 2 + h)], [psKr])
                c.mm(psK[:, 128:256], kc_[:, ssl], qc_[:, ssl], True, True, [('gxc', 2 + h), ('gxc', h)], [psKr])
                cols = [G['gb1'][:, s, h:h + 1], G['ga1'][:, s, h:h + 1], G['ga2'][:, s, h:h + 1]]
                coln = ['gb1', 'ga1', 'ga2']
                for i3 in range(3):
                    c.ts('pool', self.diag[i3][:, :], cf['ident'], cols[i3], None, ALU.mult, None, ['cst', coln[i3]],
                         [('diag', i3)])
                psE, psEr = c.next_psum()
                masks = ['ms', 'msT', 'miT']
                for i3 in range(3):
                    o_ = psE[:, i3 * 128:(i3 + 1) * 128]
                    c.mm(o_, cf['ones'], self.diag[i3][:, :], True, False, ['cst', ('diag', i3)], [psEr])
                    c.mm(o_, cf['ident'], cf[masks[i3]], False, True, ['cst'], [psEr])
                biases = [cols[1], cols[0], cols[0]]
                biasn = ['ga1', 'gb1', 'gb1']
                for i3 in range(3):
                    c.act(self.Emat[i3][:, :], psE[:, i3 * 128:(i3 + 1) * 128], AF.Exp, [psEr, biasn[i3]], [('emat', i3)],
                          bias=biases[i3])
                P0, Q0, T0 = self.invP[ch][0], self.invQ[ch][0], self.invT[ch][0]
                c.tt('dve', P0[:, :], self.Emat[0][:, :], psK[:, 0:128], ALU.mult, [('emat', 0), psKr], [('invP', ch, 0)])
                c.tt('dve', Q0[:, :], self.Emat[1][:, :], psK[:, 0:128], ALU.mult, [('emat', 1), psKr], [('invQ', ch, 0)])
                c.tt('dve', self.AqkT[s][h][:, :], self.Emat[2][:, :], psK[:, 128:256], ALU.mult, [('emat', 2), psKr],
                     [('aqkT', s, h)])
                c.stt(T0[:, :], Q0[:, :], -1.0, cf['ident'], ALU.mult, ALU.add, [('invQ', ch, 0), 'cst'], [('invT', ch, 0)],
                      eng='dve')
        if DBG_STOP == 5:
            return
        self.emit_inverse(NS * 2, [(s, h) for s in range(NS) for h in range(2)], self.TTb, 'TTb')
        if DBG_STOP == 6:
            return
        for ck in range(2 * NS):
            s, half = ck // 2, ck % 2
            rows = slice(half * 64, half * 64 + 64)
            tcols = slice(s * 128 + half * 64, s * 128 + half * 64 + 64)
            for h in range(2):
                psX, psXr = c.next_psum()
                c.mm(psX[rows, 0:128], self.gxc[2 + h][:, tcols], self.S_b[h][:, :], True, True, [('gxc', 2 + h), ('S_b', h)], [psXr])
                c.mm(psX[rows, 128:256], self.gxc[h][:, tcols], self.S_b[h][:, :], True, True, [('gxc', h), ('S_b', h)], [psXr])
                c.stt(self.R_bf[h][rows, :], psX[rows, 0:128], G['gns1'][rows, s, h:h + 1], self.vb_tok[s][h][rows, :],
                      ALU.mult, ALU.add, [psXr, 'gns1', ('vbtok', s, h)], [('R_bf', h)])
                c.act(self.x2s[h][rows, :], psX[rows, 128:256], AF.Identity, [psXr, 'gs2'], [('x2s', h)],
                      scale=G['gs2'][rows, s, h:h + 1])
                psV, psVr = c.next_psum()
                c.mm(psV[rows, 0:128], self.TTb[s][h][rows, half * 64:half * 64 + 64], self.R_bf[h][rows, :], True, True,
                     [('TTb', s, h), ('R_bf', h)], [psVr])
                c.copy('dve', self.vn[h][rows, :], psV[rows, 0:128], [psVr], [('vn', h)])
                c.act(self.vs[h][rows, :], psV[rows, 0:128], AF.Identity, [psVr, 'gs3'], [('vs', h)],
                      scale=G['gs3'][rows, s, h:h + 1])
                c.mm(psV[rows, 128:256], self.AqkT[s][h][rows, half * 64:half * 64 + 64], self.vn[h][rows, :], True, True,
                     [('aqkT', s, h), ('vn', h)], [psVr])
                c.tt('dve', self.o_tok[s][h][rows, :], psV[rows, 128:256], self.x2s[h][rows, :], ALU.add,
                     [psVr, ('x2s', h)], [('otok', s, h)])
                psS, psSr = c.next_psum()
                c.mm(psS[:, 0:128], self.k_tok[s][h][rows, :], self.vs[h][rows, :], True, True,
                     [('ktok', s, h), ('vs', h)], [psSr])
                gl = G['gl0'] if half == 0 else G['gl1']
                c.stt(self.S_f[h][:, :], self.S_f[h][:, :], gl[:, s, h:h + 1], psS[:, 0:128], ALU.mult, ALU.add,
                      [('S_f', h), 'gl0', 'gl1', psSr], [('S_f', h)])
                c.copy('act', self.S_b[h][:, :], self.S_f[h][:, :], [('S_f', h)], [('S_b', h)])
        if DBG_STOP == 7:
            return
        for h in range(2):
            st = self.ostage[par][h]
            items = []
            for s in range(NS):
                o = self.o_tok[s][h]
                c.S.add('act', lambda e, o=o, h=h: e.activation(out=self.ojunk[:, :], in_=o[:, :], func=AF.Square,
                                                                 accum_out=self.oss[:, h:h + 1]),
                        [('otok', s, h)], ['ojunk', ('oss', h)])
                c.act(self.oss[:, h:h + 1], self.oss[:, h:h + 1], AF.Ln, [('oss', h), 'eps'], [('oss', h)],
                      bias=self.eps_sb[:, 0:1], scale=1.0 / 128)
                c.act(self.oss[:, h:h + 1], self.oss[:, h:h + 1], AF.Exp, [('oss', h)], [('oss', h)], scale=-0.5)
                onb = self.on_bf[s]
                onr = ('on_bf', s)
                c.stt(onb[:, :], o[:, :], self.oss[:, h:h + 1], self.gain_rep, ALU.mult, ALU.mult,
                      [('otok', s, h), ('oss', h), 'gpar'], [onr])
                items.append((onb[:, :], [onr],
                              lambda pt, res, s=s, st=st, h=h: c.copy('act', st[:, s * 128:(s + 1) * 128], pt, [res],
                                                                      [('ostage', par, h)])))
            self.tr_batch(items)
            c.dma('sp', self.oaT[h * 128:(h + 1) * 128, tt * TT:(tt + 1) * TT], st[:, :], [('ostage', par, h)], ['oaT'],
                  key='oast%d' % par)

    def emit_inverse(self, NCH, chains, outs, outname, plus=False):
        c = self.c
        cur = 0
        _nlev = int(_os.environ.get('K2_NLEV', '5'))
        _nch = int(_os.environ.get('K2_NCH', str(NCH)))
        for lev in range(1, 1 + _nlev):
            nxt = 1 - cur
            for ch in range(_nch):
                Pc, Qc, Tc = self.invP[ch][cur], self.invQ[ch][cur], self.invT[ch][cur]
                Pn, Qn, Tn = self.invP[ch][nxt], self.invQ[ch][nxt], self.invT[ch][nxt]
                ps, psr = c.next_psum()
                c.mm(ps[:, 0:128], Qc[:, :], Pc[:, :], True, True, [('invQ', ch, cur), ('invP', ch, cur)], [psr])
                if lev < 5:
                    c.mm(ps[:, 128:256], Pc[:, :], Qc[:, :], True, True, [('invQ', ch, cur), ('invP', ch, cur)], [psr])
                c.copy('act', Pn[:, :], ps[:, 0:128], [psr], [('invP', ch, nxt)])
                if lev < 5:
                    c.copy('dve', Qn[:, :], ps[:, 128:256], [psr], [('invQ', ch, nxt)])
                c.mm(ps[:, 256:384], Pn[:, :], Tc[:, :], True, True, [('invP', ch, nxt), ('invT', ch, cur)], [psr])
                if lev < 5:
                    c.tt('dve', Tn[:, :], ps[:, 256:384], Tc[:, :], ALU.add, [psr, ('invT', ch, cur)], [('invT', ch, nxt)])
                else:
                    s, h = chains[ch]
                    c.tt('dve', outs[s][h][:, :], ps[:, 256:384], Tc[:, :], ALU.add, [psr, ('invT', ch, cur)],
                         [(outname, s, h)])
            cur = nxt


GN_EPS = 64e-5
DECAY_C = -float(np.exp(-0.5))


def _init_rwkv(self):
    c, NS, TT, T2 = self.c, self.NS, self.TT, self.T2
    cf = self.cf
    self.xr = c.dram("xr", [10 * 128, T2], BF16, "ExternalInput")
    self.rpar_d = c.dram("rpar", [128, 20], F32, "ExternalInput")
    self.rrep_d = c.dram("rrep", [128, 512], F32, "ExternalInput")
    self.rlw_d = c.dram("rlw", [128, 4, 256], F32, "ExternalInput")
    self.obT = c.dram("obT", [256, T2], BF16, "ExternalOutput")
    rp = self.rpar = c.sb("rpar_sb", [128, 20], F32)
    c.dma('sp', rp[:, :], self.rpar_d[:, :], [], ['rpar'], key='rpar')
    self.rrep = c.sb("rrep_sb", [128, 512], F32)
    c.dma('sp', self.rrep[:, :], self.rrep_d[:, :], [], ['rrep'], key='rrep')
    rlw_f = c.sb("rlw_f", [128, 4, 256], F32)
    c.dma('sp', rlw_f[:, :, :], self.rlw_d[:, :, :], [], ['rlw_f'], key='rlw_f')
    self.rlw = c.sb("rlw_b", [128, 4, 256], BF16)
    c.copy('dve', self.rlw[:, :, :], rlw_f[:, :, :], ['rlw_f'], ['rlw'])
    self.omka = c.sb("omka", [128, 2], F32)
    c.ts('dve', self.omka[:, :], rp[:, 16:18], -1.0, 1.0, ALU.mult, ALU.add, ['rpar'], ['omka'])
    self.bd_bf = c.sb("bd_bf", [128, 128], BF16)
    c.copy('dve', self.bd_bf[:, :], cf['bd'], ['cst'], ['cstb'])
    self.hsel = c.sb("hsel", [128, 2], BF16)
    c.copy('dve', self.hsel[:, 0:1], cf['c0'][:, 0:1], ['cst'], ['cstb'])
    c.copy('dve', self.hsel[:, 1:2], cf['c1'][:, 0:1], ['cst'], ['cstb'])
    self.gneps = c.sb("gneps", [128, 1], F32)
    c.memset('pool', self.gneps[:, :], GN_EPS, ['gneps'])
    self.rmask = c.sb("rmask", [128, TT], F32)
    c.memset('pool', self.rmask[:, :], 1.0, ['rmask'])
    for k in range(TT // 64):
        c.memset('pool', self.rmask[:, k * 64:k * 64 + 1], 0.0, ['rmask'])
    self.rraw = [c.sb("rraw%d" % j, [128, 1 + TT], BF16) for j in range(10)]
    self.rd = [c.sb("rd%d" % i, [128, TT], F32) for i in range(2)]
    self.xs = [c.sb("xs%d" % j, [128, TT], F32) for j in range(6)]
    self.lora_b = [c.sb("lorab%d" % j, [128, TT], BF16) for j in range(4)]
    self.lw = [c.sb("lw%d" % i, [128, TT], F32) for i in range(2)]
    self.av = [c.sb("av%d" % i, [128, TT], F32) for i in range(2)]
    self.g_tok = [c.sb("gtok%d" % s, [128, 256], F32) for s in range(NS)]
    tn = ['kkr', 'rkk', 'kk', 'tq', 'kp', 'ka', 'cs', 'Em', 'Ex', 'ktf', 'akf']
    self.rt = {n: c.sb("rt_" + n, [128, TT], F32) for n in tn}
    self.rsq = c.sb("rsq", [128, TT], BF16)
    self.Ep = [c.sb("Ep%d" % i, [128, TT], F32) for i in range(2)]
    self.br = [c.sb("br%d" % i, [128, NS, 2, 128], BF16) for i in range(2)]
    self.akT = [c.sb("akT%d" % i, [128, TT], BF16) for i in range(2)]
    self.ktT = [c.sb("ktT%d" % i, [128, TT], BF16) for i in range(2)]
    self.fm3 = [[c.sb("fm3_%d_%d" % (q, i), [128, TT], BF16) for i in range(2)] for q in range(3)]
    self.prod = c.sb("prod", [128, TT], BF16)
    self.tok3 = [[c.sb("tok3_%d_%d" % (q, s), [128, 256], BF16) for s in range(NS)] for q in range(3)]
    self.coef = [c.sb("coef%d" % s, [128, 4], F32) for s in range(NS)]
    self.AraT = [[c.sb("AraT%d_%d" % (s, h), [128, 128], BF16) for h in range(4)] for s in range(NS)]
    self.AbrkT = [[c.sb("AbrkT%d_%d" % (s, h), [128, 256], BF16) for h in range(4)] for s in range(NS)]
    self.TTr = [[c.sb("TTr%d_%d" % (s, h), [128, 128], BF16) for h in range(4)] for s in range(NS)]
    self.ZV = [c.sb("ZV%d" % s, [128, 256], F32) for s in range(NS)]
    self.YV = [c.sb("YV%d" % s, [128, 256], F32) for s in range(NS)]
    self.y_tok = [c.sb("ytok%d" % s, [128, 256], F32) for s in range(NS)]
    self.H_f = [c.sb("H_f%d" % i, [128, 128], F32) for i in range(2)]
    self.H_b = [c.sb("H_b%d" % i, [128, 128], BF16) for i in range(2)]
    for i in range(2):
        c.memset('pool', self.H_f[i][:, :], 0.0, [('H_f', i)])
        c.memset('pool', self.H_b[i][:, :], 0.0, [('H_b', i)])
    self.Z_bf = [c.sb("Z_bf%d" % i, [128, 128], BF16) for i in range(2)]
    self.U_bf = [c.sb("U_bf%d" % i, [128, 128], BF16) for i in range(2)]
    self.ysq = c.sb("ysq", [128, 256], F32)
    self.yn = c.sb("yn", [128, 256], F32)
    self.yfin = c.sb("yfin", [128, 256], BF16)
    self.gst = {n: c.sb("gst_" + n, [128, 4], F32) for n in ['s1', 's2', 'mean', 'msq', 'var']}
    self.obstage = [c.sb("obstage%d" % i, [128, TT], BF16) for i in range(2)]
    self.final_keys += ['obst']


def _rwkv_tile(self, tt):
    c, NS, TT = self.c, self.NS, self.TT
    cf = self.cf
    t0 = tt * TT
    rp = self.rpar
    RT = self.rt
    v3 = lambda ap: ap.rearrange("p (s t) -> p s t", t=128)
    for j in range(10):
        raw = self.rraw[j]
        rr = ('rraw', j)
        if tt == 0:
            c.memset('pool', raw[:, 0:1], 0.0, [rr])
            c.dma('sp', raw[:, 1:1 + TT], self.xr[j * 128:(j + 1) * 128, 0:TT], [], [rr], key='rraw%d' % j)
        else:
            c.dma('sp', raw[:, :], self.xr[j * 128:(j + 1) * 128, t0 - 1:t0 + TT], [], [rr], key='rraw%d' % j)
        d = self.rd[j % 2]
        dr = ('rd', j % 2)
        c.tt('pool', d[:, :], raw[:, 0:TT], raw[:, 1:1 + TT], ALU.subtract, [rr], [dr])
        if j < 6:
            c.stt(self.xs[j][:, :], d[:, :], rp[:, j:j + 1], raw[:, 1:1 + TT], ALU.mult, ALU.add, [dr, rr, 'rpar'], [('xs', j)])
        else:
            c.stt(d[:, :], d[:, :], rp[:, j:j + 1], raw[:, 1:1 + TT], ALU.mult, ALU.add, [dr, rr, 'rpar'], [dr])
            fn = AF.Tanh if j == 6 else (AF.Identity if j == 7 else AF.Sigmoid)
            c.act(self.lora_b[j - 6][:, :], d[:, :], fn, [dr], [('lorab', j - 6)])
    for cc in range(2):
        ps, psr = c.next_psum()
        c.mm(ps[:, :TT], self.rlw[:, 0, cc * 128:(cc + 1) * 128], self.lora_b[0][:, :], True, True, ['rlw', ('lorab', 0)], [psr])
        c.act(self.lw[cc][:, :], ps[:, :TT], AF.Sigmoid, [psr, 'rpar'], [('lw', cc)], bias=rp[:, 10 + cc:11 + cc])
        c.ts('dve', self.lw[cc][:, :], self.lw[cc][:, :], DECAY_C, None, ALU.mult, None, [('lw', cc)], [('lw', cc)])
        ps, psr = c.next_psum()
        c.mm(ps[:, :TT], self.rlw[:, 1, cc * 128:(cc + 1) * 128], self.lora_b[1][:, :], True, True, ['rlw', ('lorab', 1)], [psr])
        c.act(self.av[cc][:, :], ps[:, :TT], AF.Sigmoid, [psr, 'rpar'], [('av', cc)], bias=rp[:, 12 + cc:13 + cc])
    for s in range(NS):
        ps, psr = c.next_psum()
        for kc in range(2):
            c.mm(ps[:, 0:256], self.lora_b[2 + kc][:, s * 128:(s + 1) * 128], self.rlw[:, 2 + kc, :], kc == 0, kc == 1,
                 ['rlw', ('lorab', 2 + kc)], [psr])
        c.copy('act', self.g_tok[s][:, :], ps[:, 0:256], [psr], [('gtok', s)])
    for cc in range(2):
        xr_, xk_, xv_ = self.xs[cc], self.xs[2 + cc], self.xs[4 + cc]
        xrr, xkr, xvr = ('xs', cc), ('xs', 2 + cc), ('xs', 4 + cc)
        c.ts('dve', RT['kkr'][:, :], xk_[:, :], rp[:, 14 + cc:15 + cc], None, ALU.mult, None, [xkr, 'rpar'], ['kkr'])
        c.act(self.rsq[:, :], RT['kkr'][:, :], AF.Square, ['kkr'], ['rsq'])
        ps, psr = c.next_psum()
        c.mm(ps[:, :TT], self.bd_bf[:, :], self.rsq[:, :], True, True, ['cstb', 'rsq'], [psr])
        c.act(RT['rkk'][:, :], ps[:, :TT], AF.Ln, [psr, 'eps'], ['rkk'], bias=self.eps_sb[:, 0:1])
        c.act(RT['rkk'][:, :], RT['rkk'][:, :], AF.Exp, ['rkk'], ['rkk'], scale=-0.5)
        c.tt('dve', RT['kk'][:, :], RT['kkr'][:, :], RT['rkk'][:, :], ALU.mult, ['kkr', 'rkk'], ['kk'])
        c.ts('dve', RT['tq'][:, :], self.av[cc][:, :], rp[:, 16 + cc:17 + cc], self.omka[:, cc:cc + 1], ALU.mult, ALU.add,
             [('av', cc), 'rpar', 'omka'], ['tq'])
        c.tt('dve', RT['kp'][:, :], xk_[:, :], RT['tq'][:, :], ALU.mult, [xkr, 'tq'], ['kp'])
        c.tt('pool', RT['ka'][:, :], RT['kk'][:, :], self.av[cc][:, :], ALU.mult, ['kk', ('av', cc)], ['ka'])
        c.S.add('dve', lambda e, cc=cc: e.tensor_tensor_scan(out=RT['cs'][:, :], data0=self.rmask[:, :], data1=self.lw[cc][:, :],
                                                             initial=0.0, op0=ALU.mult, op1=ALU.add),
                ['rmask', ('lw', cc)], ['cs'])
        c.act(self.Ep[cc][:, :], RT['cs'][:, :], AF.Exp, ['cs'], [('Ep', cc)])
        c.act(RT['Em'][:, :], RT['cs'][:, :], AF.Exp, ['cs'], ['Em'], scale=-1.0)
        c.tt('pool', RT['Ex'][:, :], RT['cs'][:, :], self.lw[cc][:, :], ALU.subtract, ['cs', ('lw', cc)], ['Ex'])
        c.act(RT['Ex'][:, :], RT['Ex'][:, :], AF.Exp, ['Ex'], ['Ex'])
        brr = ('br', cc)
        c.tt('dve', self.br[cc][:, :, 0, :], v3(RT['kk'][:, :]), v3(RT['Ex'][:, :]), ALU.mult, ['kk', 'Ex'], [brr])
        c.tt('dve', self.br[cc][:, :, 1, :], v3(xr_[:, :]), v3(self.Ep[cc][:, :]), ALU.mult, [xrr, ('Ep', cc)], [brr])
        c.stt(RT['akf'][:, :], RT['ka'][:, :], -1.0, RT['Em'][:, :], ALU.mult, ALU.mult, ['ka', 'Em'], ['akf'])
        c.tt('dve', RT['ktf'][:, :], RT['kp'][:, :], RT['Em'][:, :], ALU.mult, ['kp', 'Em'], ['ktf'])
        c.copy('act', self.akT[cc][:, :], RT['akf'][:, :], ['akf'], [('akT', cc)])
        c.copy('act', self.ktT[cc][:, :], RT['ktf'][:, :], ['ktf'], [('ktT', cc)])
        for k8 in range(TT // 64):
            csl = slice(k8 * 64, (k8 + 1) * 64)
            epc = self.Ep[cc][:, k8 * 64 + 63:k8 * 64 + 64]
            c.ts('dve', self.fm3[0][cc][:, csl], RT['ktf'][:, csl], epc, None, ALU.mult, None, ['ktf', ('Ep', cc)], [('fm3', 0, cc)])
            c.ts('pool', self.fm3[1][cc][:, csl], RT['akf'][:, csl], epc, None, ALU.mult, None, ['akf', ('Ep', cc)], [('fm3', 1, cc)])
        c.copy('act', self.fm3[2][cc][:, :], xv_[:, :], [xvr], [('fm3', 2, cc)])
        c.stt(self.prod[:, :], xr_[:, :], rp[:, 18 + cc:19 + cc], RT['kp'][:, :], ALU.mult, ALU.mult, [xrr, 'rpar', 'kp'], ['prod'])
        ps, psr = c.next_psum()
        for s in range(NS):
            c.mm(ps[:, s * 2:s * 2 + 2], self.prod[:, s * 128:(s + 1) * 128], self.hsel[:, :], True, True, ['prod', 'cstb'], [psr])
        for s in range(NS):
            c.copy('dve', self.coef[s][:, cc * 2:cc * 2 + 2], ps[:, s * 2:s * 2 + 2], [psr], [('coef', s)])
        items = []
        for q in range(3):
            for s in range(NS):
                items.append((self.fm3[q][cc][:, s * 128:(s + 1) * 128], [('fm3', q, cc)],
                              lambda pt, res, q=q, s=s, cc=cc: c.copy('act' if (q + s) % 2 else 'dve',
                                                                       self.tok3[q][s][:, cc * 128:(cc + 1) * 128], pt, [res],
                                                                       [('tok3', q, s, cc)])))
        self.tr_batch(items)
    chains = []
    for s in range(NS):
        ssl = slice(s * 128, (s + 1) * 128)
        for hd in range(4):
            cc, hr = hd // 2, slice((hd % 2) * 64, (hd % 2) * 64 + 64)
            ch = len(chains)
            chains.append((s, hd))
            brf = self.br[cc][hr, s, :, :].rearrange("p a t -> p (a t)")
            psM, psMr = c.next_psum()
            c.mm(psM[:, 0:128], self.br[cc][hr, s, 0, :], self.akT[cc][hr, ssl], True, True, [('br', cc), ('akT', cc)], [psMr])
            c.mm(psM[:, 128:384], self.akT[cc][hr, ssl], brf, True, True, [('br', cc), ('akT', cc)], [psMr])
            psN, psNr = c.next_psum()
            c.mm(psN[:, 0:256], self.ktT[cc][hr, ssl], brf, True, True, [('br', cc), ('ktT', cc)], [psNr])
            P0, Q0, T0 = self.invP[ch][0], self.invQ[ch][0], self.invT[ch][0]
            c.tt('dve', P0[:, :], psM[:, 0:128], cf['m_s'], ALU.mult, [psMr, 'cst'], [('invP', ch, 0)])
            c.tt('dve', Q0[:, :], psM[:, 128:256], cf['mT_s'], ALU.mult, [psMr, 'cst'], [('invQ', ch, 0)])
            c.tt('dve', self.AraT[s][hd][:, :], psM[:, 256:384], cf['mT_i'], ALU.mult, [psMr, 'cst'], [('AraT', s, hd)])
            c.tt('dve', self.AbrkT[s][hd][:, :], psN[:, 0:256], self.cst[:, 10 * 128:12 * 128], ALU.mult, [psNr, 'cst'],
                 [('AbrkT', s, hd)])
            c.tt('pool', T0[:, :], Q0[:, :], cf['ident'], ALU.add, [('invQ', ch, 0), 'cst'], [('invT', ch, 0)])
    self.emit_inverse(len(chains), chains, self.TTr, 'TTr')
    for s in range(NS):
        ps, psr = c.next_psum()
        for hd in range(4):
            vc = slice(hd * 64, hd * 64 + 64)
            cc = hd // 2
            c.mm(ps[:, hd * 64:hd * 64 + 64], self.AbrkT[s][hd][:, 0:128], self.tok3[2][s][:, vc], True, True,
                 [('AbrkT', s, hd), ('tok3', 2, s, cc)], [psr])
            c.mm(ps[:, 256 + hd * 64:256 + hd * 64 + 64], self.AbrkT[s][hd][:, 128:256], self.tok3[2][s][:, vc], True, True,
                 [('AbrkT', s, hd), ('tok3', 2, s, cc)], [psr])
        c.copy('act', self.ZV[s][:, :], ps[:, 0:256], [psr], [('ZV', s)])
        c.copy('act', self.YV[s][:, :], ps[:, 256:512], [psr], [('YV', s)])
    for ck in range(2 * NS):
        s, half = ck // 2, ck % 2
        rows = slice(half * 64, half * 64 + 64)
        hb = slice(half * 64, half * 64 + 64)
        for cc in range(2):
            ccs = slice(cc * 128, (cc + 1) * 128)
            psZ, psZr = c.next_psum()
            c.mm(psZ[rows, 0:128], self.br[cc][:, s, 0, hb], self.H_b[cc][:, :], True, True, [('br', cc), ('H_b', cc)], [psZr])
            c.mm(psZ[rows, 128:256], self.br[cc][:, s, 1, hb], self.H_b[cc][:, :], True, True, [('br', cc), ('H_b', cc)], [psZr])
            c.tt('dve', self.Z_bf[cc][rows, :], psZ[rows, 0:128], self.ZV[s][rows, ccs], ALU.add, [psZr, ('ZV', s)], [('Z_bf', cc)])
            psU, psUr = c.next_psum()
            for hh in range(2):
                hd = cc * 2 + hh
                hc = slice(hh * 64, hh * 64 + 64)
                c.mm(psU[rows, hc], self.TTr[s][hd][rows, hb], self.Z_bf[cc][rows, hc], True, True,
                     [('TTr', s, hd), ('Z_bf', cc)], [psUr])
            c.copy('act', self.U_bf[cc][rows, :], psU[rows, 0:128], [psUr], [('U_bf', cc)])
            for hh in range(2):
                hd = cc * 2 + hh
                hc = slice(hh * 64, hh * 64 + 64)
                c.mm(psU[rows, 128 + hh * 64:128 + hh * 64 + 64], self.AraT[s][hd][rows, hb], self.U_bf[cc][rows, hc], True, True,
                     [('AraT', s, hd), ('U_bf', cc)], [psUr])
            c.tt('dve', self.y_tok[s][rows, ccs], psU[rows, 128:256], self.YV[s][rows, ccs], ALU.add, [psUr, ('YV', s)],
                 [('ytok', s, cc)])
            c.tt('dve', self.y_tok[s][rows, ccs], psZ[rows, 128:256], self.y_tok[s][rows, ccs], ALU.add, [psZr, ('ytok', s, cc)],
                 [('ytok', s, cc)])
            psH, psHr = c.next_psum()
            for hh in range(2):
                hd = cc * 2 + hh
                hc = slice(hh * 64, hh * 64 + 64)
                hg = slice(cc * 128 + hh * 64, cc * 128 + hh * 64 + 64)
                c.mm(psH[hc, hc], self.tok3[1][s][rows, hg], self.U_bf[cc][rows, hc], True, False,
                     [('tok3', 1, s, cc), ('U_bf', cc)], [psHr])
                c.mm(psH[hc, hc], self.tok3[0][s][rows, hg], self.tok3[2][s][rows, hg], False, True,
                     [('tok3', 0, s, cc), ('tok3', 2, s, cc)], [psHr])
            gcol = self.Ep[cc][:, ck * 64 + 63:ck * 64 + 64]
            for hh in range(2):
                hc = slice(hh * 64, hh * 64 + 64)
                c.stt(self.H_f[cc][hc, hc], self.H_f[cc][hc, hc], gcol[hc, :], psH[hc, hc], ALU.mult, ALU.add,
                      [('H_f', cc), ('Ep', cc), psHr], [('H_f', cc)])
            c.copy('act', self.H_b[cc][:, :], self.H_f[cc][:, :], [('H_f', cc)], [('H_b', cc)])
    gs = self.gst
    for s in range(NS):
        y = self.y_tok[s]
        yr = [('ytok', s, 0), ('ytok', s, 1)]
        y3 = y[:, :].rearrange("p (h n) -> p h n", n=64)
        c.S.add('dve', lambda e, y3=y3: e.tensor_reduce(out=gs['s1'][:, :], in_=y3, axis=AX.X, op=ALU.add), yr, ['gs1'])
        c.act(self.ysq[:, :], y[:, :], AF.Square, yr, ['ysq'])
        c.S.add('dve', lambda e: e.tensor_reduce(out=gs['s2'][:, :], in_=self.ysq[:, :].rearrange("p (h n) -> p h n", n=64),
                                                 axis=AX.X, op=ALU.add), ['ysq'], ['gs2'])
        c.ts('dve', gs['mean'][:, :], gs['s1'][:, :], 1.0 / 64, None, ALU.mult, None, ['gs1'], ['gmean'])
        c.tt('dve', gs['msq'][:, :], gs['mean'][:, :], gs['mean'][:, :], ALU.mult, ['gmean'], ['gmsq'])
        c.stt(gs['var'][:, :], gs['s2'][:, :], 1.0 / 64, gs['msq'][:, :], ALU.mult, ALU.subtract, ['gs2', 'gmsq'], ['gvar'])
        c.act(gs['var'][:, :], gs['var'][:, :], AF.Ln, ['gvar', 'gneps'], ['gvar'], bias=self.gneps[:, 0:1])
        c.act(gs['var'][:, :], gs['var'][:, :], AF.Exp, ['gvar'], ['gvar'], scale=-0.5)
        for hd in range(4):
            hg = slice(hd * 64, hd * 64 + 64)
            c.ts('dve', self.yn[:, hg], y[:, hg], gs['mean'][:, hd:hd + 1], gs['var'][:, hd:hd + 1], ALU.subtract, ALU.mult,
                 yr + ['gmean', 'gvar'], [('yn', hd)])
        ynr = [('yn', hd) for hd in range(4)]
        c.tt('dve', self.yn[:, :], self.yn[:, :], self.rrep[:, 0:256], ALU.mult, ynr + ['rrep'], ynr)
        c.tt('pool', self.yn[:, :], self.yn[:, :], self.rrep[:, 256:512], ALU.add, ynr + ['rrep'], ynr)
        for hd in range(4):
            hg = slice(hd * 64, hd * 64 + 64)
            c.stt(self.yn[:, hg], self.tok3[2][s][:, hg], self.coef[s][:, hd:hd + 1], self.yn[:, hg], ALU.mult, ALU.add,
                  [('tok3', 2, s, hd // 2), ('coef', s), ('yn', hd)], [('yn', hd)])
        c.tt('dve', self.yfin[:, :], self.yn[:, :], self.g_tok[s][:, :], ALU.mult, ynr + [('gtok', s)], ['yfin'])
        items = []
        for cc in range(2):
            items.append((self.yfin[:, cc * 128:(cc + 1) * 128], ['yfin'],
                          lambda pt, res, s=s, cc=cc: c.copy('act', self.obstage[cc][:, s * 128:(s + 1) * 128], pt, [res],
                                                             [('obstage', cc)])))
        self.tr_batch(items)
    for cc in range(2):
        c.dma('sp', self.obT[cc * 128:(cc + 1) * 128, t0:t0 + TT], self.obstage[cc][:, :], [('obstage', cc)], ['obT'], key='obst')


K2.init_rwkv = _init_rwkv
K2.rwkv_tile = _rwkv_tile


def build_k3(D, DFF, VW, T, NT=512, final=False):
    c = Ctx()
    nc = c.nc
    KC = D // 128
    NF = DFF // 128
    VC = VW // 128
    h1T = c.dram("h1T", [D, T], F32, "ExternalInput")
    oa = c.dram("oa", [VW, T], BF16, "ExternalInput")
    ob = c.dram("ob", [VW, T], BF16, "ExternalInput")
    ploc = c.dram("ploc", [VW + 2 * D, T], BF16, "ExternalInput")
    wa = c.dram("wa", [VW, D], F32, "ExternalInput")
    wb = c.dram("wb", [VW, D], F32, "ExternalInput")
    wo = c.dram("wo", [D, D], F32, "ExternalInput")
    g1 = c.dram("g1", [128, KC], F32, "ExternalInput")
    wg = c.dram("wg", [D, DFF], F32, "ExternalInput")
    wu = c.dram("wu", [D, DFF], F32, "ExternalInput")
    wd = c.dram("wd", [DFF, D], F32, "ExternalInput")
    h3T = c.dram("h3T", [D, T], F32, "ExternalOutput")
    if final:
        gf = c.dram("gf", [128, KC], F32, "ExternalInput")
        outT = c.dram("outT", [D, T], F32, "ExternalOutput")
    wa_b = c.dram("wa_b", [VW, D], BF16)
    wb_b = c.dram("wb_b", [VW, D], BF16)
    wo_b = c.dram("wo_b", [D, D], BF16)
    wg_b = c.dram("wg_b", [D, DFF], BF16)
    wu_b = c.dram("wu_b", [D, DFF], BF16)
    wd_b = c.dram("wd_b", [DFF, D], BF16)
    c.init_psum(8)
    h_sb = c.sb("h_sb", [128, KC, NT], F32)
    n_sb = c.sb("n_sb", [128, KC, NT], BF16)
    hid_sb = c.sb("hid_sb", [128, max(NF, 2 * VC + KC), NT], BF16)
    g1_sb = c.sb("g1_sb", [128, KC], F32)
    ones_bf = c.sb("ones_bf", [128, 128], BF16)
    eps_sb = c.sb("eps_sb", [128, 1], F32)
    rstd_sb = c.sb("rstd_sb", [128, NT], F32)
    sq_bufs = [(c.sb("sq%d" % i, [128, NT], BF16), "sq%d" % i) for i in range(2)]
    sg_bufs = [(c.sb("sg%d" % i, [128, NT], F32), "sg%d" % i) for i in range(2)]
    wslots = [((c.sb("wga%d" % i, [128, KC, 256], BF16), "wga%d" % i),
               (c.sb("wua%d" % i, [128, KC, 256], BF16), "wua%d" % i)) for i in range(2)]
    wdslots = [(c.sb("wds%d" % i, [128, 11, 512], BF16), "wds%d" % i) for i in range(2)]
    gt_bufs = [(c.sb("gt%d" % i, [128, 2, NT], BF16), "gt%d" % i) for i in range(2)]
    t_bufs = [(c.sb("tb%d" % i, [128, NT], F32), "tb%d" % i) for i in range(2)]
    c.eps_ap = eps_sb[:, 0:1]
    c.memset('pool', ones_bf[:, :], 1.0, ['ones'])
    c.memset('pool', eps_sb[:, :], NORM_EPS, ['eps'])
    c.dma('sp', g1_sb[:, :], g1[:, :], [], ['g1'], key='g1')
    if final:
        gf_sb = c.sb("gf_sb", [128, KC], F32)
        c.dma('sp', gf_sb[:, :], gf[:, :], [], ['gf'], key='gf')
        fo_bufs = [(c.sb("fo%d" % i, [128, NT], F32), "fo%d" % i) for i in range(2)]
    cast_dram(c, wa_b, wa, dres='wa_b')
    cast_dram(c, wb_b, wb, dres='wb_b')
    cast_dram(c, wo_b, wo, dres='wo_b')
    cast_dram(c, wg_b, wg, dres='wg_b')
    cast_dram(c, wu_b, wu, dres='wu_b')
    cast_dram(c, wd_b, wd, dres='wd_b')
    hv = h1T.rearrange("(kc p) t -> p kc t", p=128)
    h3v = h3T.rearrange("(kc p) t -> p kc t", p=128)
    oav = oa.rearrange("(kc p) t -> p kc t", p=128)
    obv = ob.rearrange("(kc p) t -> p kc t", p=128)
    plv = ploc.rearrange("(kc p) t -> p kc t", p=128)
    wav = wa_b.rearrange("(kc p) f -> p kc f", p=128)
    wbv = wb_b.rearrange("(kc p) f -> p kc f", p=128)
    wov = wo_b.rearrange("(kc p) f -> p kc f", p=128)
    CW = 256 if D % 256 == 0 else 128
    NJ = CW // 128
    YO = 2 * VC
    for tt in range(T // NT):
        tsl = slice(tt * NT, (tt + 1) * NT)
        c.dma('sp', h_sb[:, :, :], hv[:, :, tsl], [], [('h', kc) for kc in range(KC)], key='hload')
        c.dma('sp', n_sb[:, 0:VC, :], oav[:, :, tsl], [], [('n', k) for k in range(VC)], key='oaload')
        c.dma('sp', n_sb[:, VC:2 * VC, :], plv[:, 0:VC, tsl], [], [('n', VC + k) for k in range(VC)], key='zload')
        c.dma('sp', hid_sb[:, VC:2 * VC, :], obv[:, :, tsl], [], [('hid', VC + k) for k in range(VC)], key='obload')
        for k in range(VC):
            sg, sgr = sg_bufs[k % 2]
            c.act(sg[:, :], n_sb[:, VC + k, :], AF.Silu, [('n', VC + k)], [sgr])
            c.tt('dve', hid_sb[:, k, :], sg[:, :], n_sb[:, k, :], ALU.mult, [sgr, ('n', k)], [('hid', k)])
        si = 0
        for cs in range(D // CW):
            (wa_sb, war), (wb_sb, wbr) = wslots[si % 2]
            si += 1
            c.dma('sp', wa_sb[:, :VC, :CW], wav[:, :, cs * CW:(cs + 1) * CW], ['wa_b'], [war], key=war)
            c.dma('sp', wb_sb[:, :VC, :CW], wbv[:, :, cs * CW:(cs + 1) * CW], ['wb_b'], [wbr], key=wbr)
            for j in range(NJ):
                dc = cs * NJ + j
                gt, gtr = gt_bufs[dc % 2]
                c.dma('sp', gt[:, 0, :], plv[:, VC + dc, tsl], [], [gtr], key=gtr + 'a')
                c.dma('sp', gt[:, 1, :], plv[:, VC + KC + dc, tsl], [], [gtr], key=gtr + 'b')
                pa, par_ = c.next_psum()
                pb, pbr = c.next_psum()
                for k in range(VC):
                    c.mm(pa[:, :NT], wa_sb[:, k, j * 128:(j + 1) * 128], hid_sb[:, k, :], k == 0, k == VC - 1,
                         [war, ('hid', k)], [par_])
                for k in range(VC):
                    c.mm(pb[:, :NT], wb_sb[:, k, j * 128:(j + 1) * 128], hid_sb[:, VC + k, :], k == 0, k == VC - 1,
                         [wbr, ('hid', VC + k)], [pbr])
                sg, sgr = sg_bufs[0]
                sg2, sgr2 = sg_bufs[1]
                c.act(sg[:, :], gt[:, 0, :], AF.Sigmoid, [gtr], [sgr])
                c.act(sg2[:, :], gt[:, 1, :], AF.Sigmoid, [gtr], [sgr2])
                t1, t1r = t_bufs[0]
                t2, t2r = t_bufs[1]
                c.tt('dve', t1[:, :], sg[:, :], pa[:, :NT], ALU.mult, [sgr, par_], [t1r])
                c.tt('dve', t2[:, :], sg2[:, :], pb[:, :NT], ALU.mult, [sgr2, pbr], [t2r])
                c.tt('pool', hid_sb[:, YO + dc, :], t1[:, :], t2[:, :], ALU.add, [t1r, t2r], [('hid', YO + dc)])
        for cs in range(D // CW):
            (wo_sb, wor), _ = wslots[si % 2]
            si += 1
            c.dma('sp', wo_sb[:, :, :CW], wov[:, :, cs * CW:(cs + 1) * CW], ['wo_b'], [wor], key=wor)
            for j in range(NJ):
                dc = cs * NJ + j
                ps, psr = c.next_psum()
                for k in range(KC):
                    c.mm(ps[:, :NT], wo_sb[:, k, j * 128:(j + 1) * 128], hid_sb[:, YO + k, :], k == 0, k == KC - 1,
                         [wor, ('hid', YO + k)], [psr])
                c.tt('dve', h_sb[:, dc, :], ps[:, :NT], h_sb[:, dc, :], ALU.add, [psr, ('h', dc)], [('h', dc)])
        emit_rmsnorm(c, h_sb, 'h', g1_sb, 'g1', n_sb, 'n', KC, NT, D, ones_bf, sq_bufs, rstd_sb)
        emit_ffn(c, n_sb, 'n', h_sb, 'h', hid_sb, wg_b, wu_b, wd_b, ('wg_b', 'wu_b', 'wd_b'), KC, NT, DFF,
                 wslots, wdslots, sg_bufs)
        c.dma('sp', h3v[:, :, tsl], h_sb[:, :, :], [('h', kc) for kc in range(KC)], ['h3T'], key='hstore')
        if final:
            ps, psr = c.next_psum()
            for kc in range(KC):
                sq, sqr = sq_bufs[kc % 2]
                c.act(sq[:, :NT], h_sb[:, kc, :], AF.Square, [('h', kc)], [sqr])
                c.mm(ps[:, :NT], ones_bf[:, :], sq[:, :NT], kc == 0, kc == KC - 1, [sqr, 'ones'], [psr])
            c.act(rstd_sb[:, :NT], ps[:, :NT], AF.Ln, [psr], ['rstd'], bias=c.eps_ap, scale=1.0 / D)
            c.act(rstd_sb[:, :NT], rstd_sb[:, :NT], AF.Exp, ['rstd'], ['rstd'], scale=-0.5)
            for kc in range(KC):
                fo, fr = fo_bufs[kc % 2]
                c.stt(fo[:, :], h_sb[:, kc, :], gf_sb[:, kc:kc + 1], rstd_sb[:, :NT], ALU.mult, ALU.mult,
                      [('h', kc), 'rstd', 'gf'], [fr])
                c.dma('sp', outT[kc * 128:(kc + 1) * 128, tsl], fo[:, :], [fr], ['outT'], key='fo%d' % (kc % 2))
    fk = ['hstore'] + (['fo0', 'fo1'] if final else [])
    info = c.S.emit(final_wait_keys=fk)
    c.es.close()
    return nc, info


D_MODEL, D_FF, DEPTH, BATCH, SEQ = 2048, 5632, 4, 2, 8192
NCORES = 8
TPC = BATCH * SEQ // NCORES
NFM = 11776
_PROGS = {}


def _prog(name):
    if name not in _PROGS:
        if name == 'k1':
            _PROGS[name] = build_k1(D_MODEL, D_FF, NFM, 16, TPC)[0]
        elif name == 'k2':
            _PROGS[name] = K2(SEQ).c.nc
        elif name == 'k3':
            _PROGS[name] = build_k3(D_MODEL, D_FF, 1024, TPC, final=False)[0]
        elif name == 'k3f':
            _PROGS[name] = build_k3(D_MODEL, D_FF, 1024, TPC, final=True)[0]
    return _PROGS[name]


def _pk(g):
    g = np.asarray(g, np.float32)
    return np.ascontiguousarray(g.reshape(-1, 128).T)


def _pad128(x):
    o = np.zeros((128,) + x.shape[1:], x.dtype)
    o[:x.shape[0]] = x
    return o


def _run(nc, in_maps):
    res = run_bass_kernel_spmd(nc, in_maps, core_ids=list(range(NCORES)))
    return res.results


def kernel(**inp):
    f32 = lambda a: np.asarray(a, np.float32)
    x = f32(inp['x'])
    hT = []
    for cidx in range(NCORES):
        b, tq = cidx // 4, cidx % 4
        hT.append(np.ascontiguousarray(x[b, tq * TPC:(tq + 1) * TPC, :].T))
    cst = k2_consts()
    out = None
    for l in range(DEPTH):
        w_in = f32(inp['w_in'][l])
        wl_p = np.zeros((D_MODEL, 128), np.float32); wl_p[:, :96] = w_in[:, 4112 + 3072:4112 + 3168]
        al_p = np.zeros((D_MODEL, 128), np.float32); al_p[:, :96] = w_in[:, 4112 + 3168:4112 + 3264]
        wfm = np.ascontiguousarray(np.concatenate([
            w_in[:, 0:3072], w_in[:, 4112:4112 + 3072], wl_p, al_p, w_in[:, 4112 + 3264:4112 + 3520],
            w_in[:, 3072:4096], w_in[:, 7632:9680], w_in[:, 9680:11728]], axis=1))
        wab = np.ascontiguousarray(w_in[:, 4096:4112])
        common1 = dict(g1=_pk(inp['ffn1_norm'][l]), g2=_pk(inp['mix_norm'][l]), wg=f32(inp['ffn1_w_gate'][l]),
                       wu=f32(inp['ffn1_w_up'][l]), wd=f32(inp['ffn1_w_down'][l]), wfm=wfm, wab=wab)
        r1 = _run(_prog('k1'), [dict(hT=hT[cidx], **common1) for cidx in range(NCORES)])
        del wfm, common1
        conv = f32(inp['gdn_conv'][l]); a_log = f32(inp['gdn_a_log'][l]); dtb = f32(inp['gdn_dt_bias'][l])
        ogain = f32(inp['gdn_out_norm'][l]); mu = f32(inp['rw_mu'][l])
        w0 = f32(inp['rw_w0'][l]); a0 = f32(inp['rw_a0'][l]); kk_ = f32(inp['rw_k_k'][l]); ka_ = f32(inp['rw_k_a'][l])
        rk_ = f32(inp['rw_r_k'][l]).reshape(-1); lnw = f32(inp['rw_ln_w'][l]); lnb = f32(inp['rw_ln_b'][l])
        w_up = f32(inp['rw_w_up'][l]); a_up = f32(inp['rw_a_up'][l]); g_up = f32(inp['rw_g_up'][l])
        in2 = []
        for m in range(NCORES):
            b, hg = m // 4, m % 4
            PTb = [r1[b * 4 + tq]['PT'] for tq in range(4)]
            rows = lambda r0, n: np.concatenate([p[r0:r0 + n] for p in PTb], axis=1)
            xg = np.concatenate([rows(0 + hg * 256, 256), rows(1024 + hg * 256, 256), rows(2048 + hg * 256, 256)], axis=0)
            xr = np.concatenate([rows(3072 + hg * 256, 256), rows(4096 + hg * 256, 256), rows(5120 + hg * 256, 256),
                                 rows(6144, 512)], axis=0)
            abf = np.concatenate([r1[b * 4 + tq]['ab'] for tq in range(4)], axis=0)
            abd = np.ascontiguousarray(np.concatenate([abf[:, hg * 2:hg * 2 + 2], abf[:, 8 + hg * 2:8 + hg * 2 + 2]], axis=1))
            gpar = np.zeros((128, 156), np.float32)
            for ci in range(6):
                kind, h = ci // 2, ci % 2
                ch0 = kind * 1024 + (hg * 2 + h) * 128
                gpar[:, ci * 4:(ci + 1) * 4] = conv[:, ch0:ch0 + 128].T
            gpar[:, 24:26] = a_log[None, hg * 2:hg * 2 + 2]
            gpar[:, 26:28] = dtb[None, hg * 2:hg * 2 + 2]
            gpar[:, 28:156] = ogain[None, :]
            c0 = hg * 256
            muT = np.concatenate([mu[c0:c0 + 256], mu[1024 + c0:1024 + c0 + 256], mu[2048 + c0:2048 + c0 + 256],
                                  _pad128(mu[3072:3168]), _pad128(mu[3168:3264]), mu[3264:3520]])
            rpar = np.zeros((128, 20), np.float32)
            rpar[:, 0:10] = muT.reshape(10, 128).T
            for cc in range(2):
                sl_ = slice(c0 + cc * 128, c0 + (cc + 1) * 128)
                rpar[:, 10 + cc] = w0[sl_]; rpar[:, 12 + cc] = a0[sl_]; rpar[:, 14 + cc] = kk_[sl_]
                rpar[:, 16 + cc] = ka_[sl_]; rpar[:, 18 + cc] = rk_[sl_]
            rrep = np.ascontiguousarray(np.concatenate([np.broadcast_to(lnw[c0:c0 + 256], (128, 256)),
                                                        np.broadcast_to(lnb[c0:c0 + 256], (128, 256))], axis=1))
            rlw = np.ascontiguousarray(np.stack([_pad128(w_up[:, c0:c0 + 256]), _pad128(a_up[:, c0:c0 + 256]),
                                                 g_up[0:128, c0:c0 + 256], g_up[128:256, c0:c0 + 256]], axis=1))
            in2.append(dict(cst=cst, xg=np.ascontiguousarray(xg), abd=abd, gpar=gpar, xr=np.ascontiguousarray(xr),
                            rpar=rpar, rrep=rrep, rlw=rlw))
        r2 = _run(_prog('k2'), in2)
        del in2
        last = (l == DEPTH - 1)
        common3 = dict(wa=f32(inp['w_branch_a'][l]), wb=f32(inp['w_branch_b'][l]), wo=f32(inp['w_out'][l]),
                       g1=_pk(inp['ffn2_norm'][l]), wg=f32(inp['ffn2_w_gate'][l]), wu=f32(inp['ffn2_w_up'][l]),
                       wd=f32(inp['ffn2_w_down'][l]))
        if last:
            common3['gf'] = _pk(inp['final_norm'])
        in3 = []
        for cidx in range(NCORES):
            b, tq = cidx // 4, cidx % 4
            tsl = slice(tq * TPC, (tq + 1) * TPC)
            oa = np.ascontiguousarray(np.concatenate([r2[b * 4 + hg]['oaT'][:, tsl] for hg in range(4)], axis=0))
            ob = np.ascontiguousarray(np.concatenate([r2[b * 4 + hg]['obT'][:, tsl] for hg in range(4)], axis=0))
            ploc = np.ascontiguousarray(r1[cidx]['PT'][6656:11776])
            in3.append(dict(h1T=r1[cidx]['h1T'], oa=oa, ob=ob, ploc=ploc, **common3))
        del r2
        r3 = _run(_prog('k3f' if last else 'k3'), in3)
        del in3, r1
        hT = [r3[cidx]['h3T'] for cidx in range(NCORES)]
        if last:
            out = np.zeros((BATCH, SEQ, D_MODEL), np.float32)
            for cidx in range(NCORES):
                b, tq = cidx // 4, cidx % 4
                out[b, tq * TPC:(tq + 1) * TPC, :] = r3[cidx]['outT'].T
    return out
```

```python
import numpy as np
import ml_dtypes
from contextlib import ExitStack
import concourse.bass as bass
import concourse.mybir as mybir
from concourse.bass_utils import run_bass_kernel_spmd

F32 = mybir.dt.float32
BF16 = mybir.dt.bfloat16
AF = mybir.ActivationFunctionType
ALU = mybir.AluOpType
AX = mybir.AxisListType

SEM_LIM = 30000


class Sched:
    def __init__(self, nc, same_engine_sync=True):
        self.nc = nc
        self.ops = []
        self.lastw = {}
        self.readers = {}
        self.last_dma = {}
        self.exclusive = set()
        self.same_engine_sync = same_engine_sync

    def add(self, eng, fn, reads=(), writes=(), dma_key=None):
        idx = len(self.ops)
        deps = set()
        for r in reads:
            w = self.lastw.get(r)
            if w is not None:
                deps.add(w)
            if r in self.exclusive:
                for k_, x in self.readers.get(r, {}).items():
                    if k_[0] != eng:
                        deps.add(x)
        for w_ in writes:
            w = self.lastw.get(w_)
            if w is not None:
                deps.add(w)
            for x in self.readers.get(w_, {}).values():
                deps.add(x)
        if dma_key is not None:
            p = self.last_dma.get(dma_key)
            if p is not None:
                deps.add(p)
            self.last_dma[dma_key] = idx
        rk = (eng, idx) if dma_key is not None else (eng, -1)
        for r in reads:
            self.readers.setdefault(r, {})[rk] = idx
        for w_ in writes:
            self.lastw[w_] = idx
            self.readers[w_] = {}
        deps.discard(idx)
        self.ops.append(dict(eng=eng, fn=fn, deps=deps, dma_key=dma_key))
        return idx

    def emit(self, final_wait_keys=()):
        nc = self.nc
        ops = self.ops
        needed = set()
        for o in ops:
            eff = set()
            for j in o['deps']:
                d = ops[j]
                if d['dma_key'] is None and d['eng'] == o['eng'] and (o['eng'] == 'pe' or not self.same_engine_sync):
                    continue
                eff.add(j)
            o['deps'] = eff
            needed |= eff
        final_ops = [self.last_dma[k] for k in final_wait_keys if k in self.last_dma]
        needed |= set(final_ops)
        eng_count = {}
        dma_count = {}
        sem_names = set()
        for i, o in enumerate(ops):
            o['sig'] = None
            if i not in needed:
                continue
            if o['dma_key'] is not None:
                k = o['dma_key']
                dma_count[k] = dma_count.get(k, 0) + 1
                o['sig'] = ('d_%s' % (k,), 16 * dma_count[k], 16)
            else:
                e = o['eng']
                c = eng_count.get(e, 0)
                eng_count[e] = c + 1
                o['sig'] = ('e_%s_%d' % (e, c // SEM_LIM), (c % SEM_LIM) + 1, 1)
            sem_names.add(o['sig'][0])
        engs = ['sp', 'act', 'dve', 'pe', 'pool']
        per_eng = {e: [] for e in engs}
        for i, o in enumerate(ops):
            per_eng[o['eng']].append(i)
        with ExitStack() as es:
            sems = {}
            for n in sorted(sem_names):
                sems[n] = es.enter_context(nc.semaphore(n))
            block = es.enter_context(nc.Block())

            def run_engine(ename, eobj):
                waited = {}
                for i in per_eng[ename]:
                    o = ops[i]
                    for j in sorted(o['deps']):
                        d = ops[j]
                        if d['dma_key'] is None and d['eng'] == ename:
                            if ename == 'pe' or not self.same_engine_sync:
                                continue
                        sname, val, _ = d['sig']
                        if waited.get(sname, 0) >= val:
                            continue
                        eobj.wait_ge(sems[sname], val)
                        waited[sname] = val
                    ins = o['fn'](eobj)
                    if o['sig'] is not None:
                        ins.then_inc(sems[o['sig'][0]], o['sig'][2])
                if ename == 'sp':
                    for j in final_ops:
                        sname, val, _ = ops[j]['sig']
                        if waited.get(sname, 0) >= val:
                            continue
                        eobj.wait_ge(sems[sname], val)
                        waited[sname] = val

            @block.sync
            def _(e):
                run_engine('sp', e)

            @block.scalar
            def _(e):
                run_engine('act', e)

            @block.vector
            def _(e):
                run_engine('dve', e)

            @block.tensor
            def _(e):
                run_engine('pe', e)

            @block.gpsimd
            def _(e):
                run_engine('pool', e)
        return dict(n_ops=len(ops), eng_count=eng_count, n_sems=len(sem_names))


class Ctx:
    def __init__(self, name="k"):
        self.nc = bass.Bass("TRN2", target_bir_lowering=False)
        self.S = Sched(self.nc)
        self.es = ExitStack()
        self.psum = []
        self.psum_i = 0
        self.uid = 0
        self.wres = {}

    def dram(self, name, shape, dtype, kind="Internal"):
        return self.nc.dram_tensor(name, list(shape), dtype, kind=kind).ap()

    def sb(self, name, shape, dtype):
        return self.es.enter_context(self.nc.sbuf_tensor(name, list(shape), dtype))

    def init_psum(self, n=8):
        for i in range(n):
            t = self.es.enter_context(self.nc.psum_tensor("ps%d" % i, [128, 512], F32))
            self.psum.append((t, 'ps%d' % i))
            self.S.exclusive.add('ps%d' % i)

    def next_psum(self):
        t = self.psum[self.psum_i % len(self.psum)]
        self.psum_i += 1
        return t

    def dma(self, eng, out, in_, reads, writes, key):
        return self.S.add(eng, lambda e: e.dma_start(out=out, in_=in_), reads, writes, dma_key=key)

    def mm(self, out, lhsT, rhs, start, stop, reads, writes):
        return self.S.add('pe', lambda e: e.matmul(out, lhsT, rhs, start=start, stop=stop), reads, writes)

    def act(self, out, in_, func, reads, writes, bias=None, scale=None):
        kw = {}
        if bias is not None:
            kw['bias'] = bias
        if scale is not None:
            kw['scale'] = scale
        return self.S.add('act', lambda e: e.activation(out=out, in_=in_, func=func, **kw), reads, writes)

    def tt(self, eng, out, in0, in1, op, reads, writes):
        return self.S.add(eng, lambda e: e.tensor_tensor(out=out, in0=in0, in1=in1, op=op), reads, writes)

    def stt(self, out, in0, scalar, in1, op0, op1, reads, writes, eng='dve'):
        return self.S.add(eng, lambda e: e.scalar_tensor_tensor(out=out, in0=in0, scalar=scalar, in1=in1,
                                                                op0=op0, op1=op1), reads, writes)

    def ts(self, eng, out, in0, s1, s2, op0, op1, reads, writes):
        if s2 is None:
            return self.S.add(eng, lambda e: e.tensor_scalar(out=out, in0=in0, scalar1=s1, scalar2=None, op0=op0),
                              reads, writes)
        return self.S.add(eng, lambda e: e.tensor_scalar(out=out, in0=in0, scalar1=s1, scalar2=s2, op0=op0, op1=op1),
                          reads, writes)

    def copy(self, eng, out, in_, reads, writes):
        if eng == 'act':
            return self.S.add('act', lambda e: e.copy(out=out, in_=in_), reads, writes)
        return self.S.add(eng, lambda e: e.tensor_copy(out=out, in_=in_), reads, writes)

    def memset(self, eng, ap, val, writes):
        return self.S.add(eng, lambda e: e.memset(ap, val), (), writes)


NORM_EPS = 1e-6


def emit_rmsnorm(c, h_sb, hres, g_sb, gres, out_sb, outres, KC, NT, D, ones_bf, sq_bufs, rstd_sb):
    ps, psr = c.next_psum()
    for kc in range(KC):
        sq, sqr = sq_bufs[kc % len(sq_bufs)]
        c.act(sq[:, :NT], h_sb[:, kc, :], AF.Square, [(hres, kc)], [sqr])
        c.mm(ps[:, :NT], ones_bf[:, :], sq[:, :NT], kc == 0, kc == KC - 1, [sqr, 'ones'], [psr])
    c.act(rstd_sb[:, :NT], ps[:, :NT], AF.Ln, [psr], ['rstd'], bias=c.eps_ap, scale=1.0 / D)
    c.act(rstd_sb[:, :NT], rstd_sb[:, :NT], AF.Exp, ['rstd'], ['rstd'], scale=-0.5)
    for kc in range(KC):
        c.stt(out_sb[:, kc, :], h_sb[:, kc, :], g_sb[:, kc:kc + 1], rstd_sb[:, :NT], ALU.mult, ALU.mult,
              [(hres, kc), 'rstd', gres], [(outres, kc)])


def emit_ffn(c, n_sb, nres, h_sb, hres, hid_sb, wg_d, wu_d, wd_d, wres, KC, NT, DFF, wslots, wdslots, sg_bufs):
    NF = DFF // 128
    FS = 512 if DFF % 512 == 0 else 128
    NJ = FS // 128
    wgv = wg_d.rearrange("(kc p) f -> p kc f", p=128)
    wuv = wu_d.rearrange("(kc p) f -> p kc f", p=128)
    for fs in range(DFF // FS):
        (wg_sb, wgr), (wu_sb, wur) = wslots[fs % len(wslots)]
        c.dma('sp', wg_sb[:, :, :FS], wgv[:, :, fs * FS:(fs + 1) * FS], c.wres[wres[0]], [wgr], key=wgr)
        c.dma('sp', wu_sb[:, :, :FS], wuv[:, :, fs * FS:(fs + 1) * FS], c.wres[wres[1]], [wur], key=wur)
        for j in range(NJ):
            f = fs * NJ + j
            pg, pgr = c.next_psum()
            pu, pur = c.next_psum()
            for kc in range(KC):
                c.mm(pg[:, :NT], wg_sb[:, kc, j * 128:(j + 1) * 128], n_sb[:, kc, :], kc == 0, kc == KC - 1,
                     [wgr, (nres, kc)], [pgr])
            for kc in range(KC):
                c.mm(pu[:, :NT], wu_sb[:, kc, j * 128:(j + 1) * 128], n_sb[:, kc, :], kc == 0, kc == KC - 1,
                     [wur, (nres, kc)], [pur])
            sg, sgr = sg_bufs[f % len(sg_bufs)]
            c.act(sg[:, :NT], pg[:, :NT], AF.Silu, [pgr], [sgr])
            c.tt('dve', hid_sb[:, f, :], sg[:, :NT], pu[:, :NT], ALU.mult, [sgr, pur], [('hid', f)])
    NDC = min(4, KC)
    FG = 11 if NF % 11 == 0 else (4 if NF % 4 == 0 else 1)
    wdv = wd_d.rearrange("(f p) d -> p f d", p=128)
    li = 0
    for p0 in range(0, KC, NDC):
        accs = [c.next_psum() for _ in range(NDC)]
        for fg in range(NF // FG):
            wd_sb, wdr = wdslots[li % len(wdslots)]
            li += 1
            c.dma('sp', wd_sb[:, :FG, :NDC * 128], wdv[:, fg * FG:(fg + 1) * FG, p0 * 128:(p0 + NDC) * 128],
                  c.wres[wres[2]], [wdr], key=wdr)
            for fi in range(FG):
                f = fg * FG + fi
                for di in range(NDC):
                    ps, psr = accs[di]
                    c.mm(ps[:, :NT], wd_sb[:, fi, di * 128:(di + 1) * 128], hid_sb[:, f, :], f == 0, f == NF - 1,
                         [wdr, ('hid', f)], [psr])
        for di in range(NDC):
            ps, psr = accs[di]
            dc = p0 + di
            c.stt(h_sb[:, dc, :], ps[:, :NT], 0.5, h_sb[:, dc, :], ALU.mult, ALU.add, [psr, (hres, dc)], [(hres, dc)])


def cast_dram(c, dst, src, rows_per=256, sres=None, dres=None):
    R = src.shape[0]
    res = []
    for bi, r0 in enumerate(range(0, R, rows_per)):
        r1 = min(R, r0 + rows_per)
        c.dma('pool', dst[r0:r1, :], src[r0:r1, :], [], [(dres, bi)], key=dres + "_%d" % (bi % 4))
        res.append((dres, bi))
    c.wres[dres] = res
    return res


def build_k1(D, DFF, NFM, NAB, T, NT=512):
    c = Ctx()
    nc = c.nc
    KC = D // 128
    NF = DFF // 128
    hT = c.dram("hT", [D, T], F32, "ExternalInput")
    g1 = c.dram("g1", [128, KC], F32, "ExternalInput")
    g2 = c.dram("g2", [128, KC], F32, "ExternalInput")
    wg_b = c.dram("wg", [D, DFF], BF16, "ExternalInput")
    wu_b = c.dram("wu", [D, DFF], BF16, "ExternalInput")
    wd_b = c.dram("wd", [DFF, D], BF16, "ExternalInput")
    wfm_b = c.dram("wfm", [D, NFM], BF16, "ExternalInput")
    wab = c.dram("wab", [D, NAB], F32, "ExternalInput")
    h1T = c.dram("h1T", [D, T], F32, "ExternalOutput")
    PT = c.dram("PT", [NFM, T], BF16, "ExternalOutput")
    ab = c.dram("ab", [T, NAB], F32, "ExternalOutput")
    for n_ in ('wg_b', 'wu_b', 'wd_b', 'wfm_b'):
        c.wres[n_] = []
    c.init_psum(8)
    h_sb = c.sb("h_sb", [128, KC, NT], F32)
    n_sb = c.sb("n_sb", [128, KC, NT], BF16)
    hid_sb = c.sb("hid_sb", [128, NF, NT], BF16)
    g1_sb = c.sb("g1_sb", [128, KC], F32)
    g2_sb = c.sb("g2_sb", [128, KC], F32)
    ones_bf = c.sb("ones_bf", [128, 128], BF16)
    eps_sb = c.sb("eps_sb", [128, 1], F32)
    rstd_sb = c.sb("rstd_sb", [128, NT], F32)
    sq_bufs = [(c.sb("sq%d" % i, [128, NT], BF16), "sq%d" % i) for i in range(2)]
    sg_bufs = [(c.sb("sg%d" % i, [128, NT], F32), "sg%d" % i) for i in range(2)]
    wslots = [((c.sb("wga%d" % i, [128, KC, 512], BF16), "wga%d" % i),
               (c.sb("wua%d" % i, [128, KC, 512], BF16), "wua%d" % i)) for i in range(2)]
    wdslots = [(c.sb("wds%d" % i, [128, 11, 512], BF16), "wds%d" % i) for i in range(2)]
    wab_sb = c.sb("wab_sb", [128, KC, NAB], F32)
    wab_bf = c.sb("wab_bf", [128, KC, NAB], BF16)
    pt_bufs = [(c.sb("pt%d" % i, [128, 4, NT], BF16), "pt%d" % i) for i in range(2)]
    ab_bufs = [(c.sb("abs%d" % i, [128, NAB], F32), "abs%d" % i) for i in range(2)]
    c.eps_ap = eps_sb[:, 0:1]

    c.memset('pool', ones_bf[:, :], 1.0, ['ones'])
    c.memset('pool', eps_sb[:, :], NORM_EPS, ['eps'])
    c.dma('sp', g1_sb[:, :], g1[:, :], [], ['g1'], key='g1')
    c.dma('sp', g2_sb[:, :], g2[:, :], [], ['g2'], key='g2')
    c.dma('sp', wab_sb[:, :, :], wab.rearrange("(kc p) n -> p kc n", p=128), [], ['wab'], key='wab')
    c.copy('dve', wab_bf[:, :, :], wab_sb[:, :, :], ['wab'], ['wabb'])

    hv = hT.rearrange("(kc p) t -> p kc t", p=128)
    h1v = h1T.rearrange("(kc p) t -> p kc t", p=128)
    PTv = PT.rearrange("(cc p) t -> p cc t", p=128)
    wfv = wfm_b.rearrange("(kc p) f -> p kc f", p=128)
    NCH = NFM // 128
    CS = 4
    for tt in range(T // NT):
        tsl = slice(tt * NT, (tt + 1) * NT)
        c.dma('sp', h_sb[:, :, :], hv[:, :, tsl], [], [('h', kc) for kc in range(KC)], key='hload')
        emit_rmsnorm(c, h_sb, 'h', g1_sb, 'g1', n_sb, 'n', KC, NT, D, ones_bf, sq_bufs, rstd_sb)
        emit_ffn(c, n_sb, 'n', h_sb, 'h', hid_sb, wg_b, wu_b, wd_b, ('wg_b', 'wu_b', 'wd_b'), KC, NT, DFF,
                 wslots, wdslots, sg_bufs)
        c.dma('sp', h1v[:, :, tsl], h_sb[:, :, :], [('h', kc) for kc in range(KC)], ['h1T'], key='hstore')
        emit_rmsnorm(c, h_sb, 'h', g2_sb, 'g2', n_sb, 'n', KC, NT, D, ones_bf, sq_bufs, rstd_sb)
        si = 0
        for cs in range(0, NCH, CS):
            ncs = min(CS, NCH - cs)
            (w_sb, wr), _ = wslots[si % len(wslots)]
            pt_sb, ptr = pt_bufs[si % len(pt_bufs)]
            si += 1
            c.dma('sp', w_sb[:, :, :ncs * 128], wfv[:, :, cs * 128:(cs + ncs) * 128], c.wres['wfm_b'], [wr], key=wr)
            for j in range(ncs):
                ps, psr = c.next_psum()
                for kc in range(KC):
                    c.mm(ps[:, :NT], w_sb[:, kc, j * 128:(j + 1) * 128], n_sb[:, kc, :], kc == 0, kc == KC - 1,
                         [wr, ('n', kc)], [psr])
                if j % 2 == 0:
                    c.copy('act', pt_sb[:, j, :], ps[:, :NT], [psr], [(ptr, j)])
                else:
                    c.copy('dve', pt_sb[:, j, :], ps[:, :NT], [psr], [(ptr, j)])
            c.dma('act', PTv[:, cs:cs + ncs, tsl], pt_sb[:, :ncs, :], [(ptr, j) for j in range(ncs)], ['PT'],
                  key='ptstore%d' % (si % 2))
        for tk in range(NT // 128):
            ps, psr = c.next_psum()
            for kc in range(KC):
                c.mm(ps[:, :NAB], n_sb[:, kc, tk * 128:(tk + 1) * 128], wab_bf[:, kc, :], kc == 0, kc == KC - 1,
                     ['wabb', ('n', kc)], [psr])
            ab_sb, abr = ab_bufs[tk % 2]
            c.copy('dve', ab_sb[:, :], ps[:, :NAB], [psr], [abr])
            c.dma('act', ab[tt * NT + tk * 128: tt * NT + (tk + 1) * 128, :], ab_sb[:, :], [abr], ['ab'],
                  key='abstore%d' % (tk % 2))
    info = c.S.emit(final_wait_keys=['hstore', 'ptstore0', 'ptstore1', 'abstore0', 'abstore1'])
    c.es.close()
    return nc, info


NEG = -30000.0
import os as _os
DBG_STOP = int(_os.environ.get('K2_STOP', '0'))
INV_F32R = int(_os.environ.get('K2_F32R', '1'))


def r32(a):
    return a.bitcast(mybir.dt.float32r) if INV_F32R else a
CST_NAMES = ['ident', 'ones', 'ltri', 'bd', 'c0', 'c1', 'ms', 'msT', 'miT', 'm_s', 'mT_s', 'mT_i']


def k2_consts():
    i = np.arange(128)
    same = (i[:, None] // 64) == (i[None, :] // 64)
    P, Fr = i[:, None], i[None, :]
    t = {}
    t['ident'] = np.eye(128)
    t['ones'] = np.ones((128, 128))
    t['ltri'] = same & (P <= Fr)
    t['bd'] = same
    t['c0'] = np.broadcast_to(P < 64, (128, 128))
    t['c1'] = np.broadcast_to(P >= 64, (128, 128))
    t['ms'] = np.where(same & (P > Fr), 0.0, NEG)
    t['msT'] = np.where(same & (Fr > P), 0.0, NEG)
    t['miT'] = np.where(same & (Fr >= P), 0.0, NEG)
    t['m_s'] = same & (Fr < P)
    t['mT_s'] = same & (P < Fr)
    t['mT_i'] = same & (P <= Fr)
    return np.ascontiguousarray(np.concatenate([np.asarray(t[n], np.float32) for n in CST_NAMES], axis=1))


class K2:
    def __init__(self, T2, TT=256, do_gdn=True, do_rwkv=True):
        self.T2, self.TT = T2, TT
        self.NS = TT // 128
        c = self.c = Ctx()
        self.do_gdn, self.do_rwkv = do_gdn, do_rwkv
        NS = self.NS
        self.cst_d = c.dram("cst", [128, 128 * len(CST_NAMES)], F32, "ExternalInput")
        c.init_psum(8)
        cst = self.cst = c.sb("cst_sb", [128, 128 * len(CST_NAMES)], F32)
        c.dma('sp', cst[:, :], self.cst_d[:, :], [], ['cst'], key='cst')
        self.cf = {n: cst[:, i * 128:(i + 1) * 128] for i, n in enumerate(CST_NAMES)}
        self.ident_bf = c.sb("ident_bf", [128, 128], BF16)
        self.ones_bf = c.sb("ones_bf", [128, 128], BF16)
        c.copy('dve', self.ident_bf[:, :], self.cf['ident'], ['cst'], ['cstb'])
        c.copy('dve', self.ones_bf[:, :], self.cf['ones'], ['cst'], ['cstb'])
        self.eps_sb = c.sb("eps_sb", [128, 1], F32)
        self.one_sb = c.sb("one_sb", [128, 1], F32)
        c.memset('pool', self.eps_sb[:, :], 1e-6, ['eps'])
        c.memset('pool', self.one_sb[:, :], 1.0, ['one'])
        self.final_keys = []
        NCH = NS * 6
        self.invP = [[c.sb("invP%d_%d" % (ch, p), [128, 128], F32) for p in range(2)] for ch in range(NCH)]
        self.invQ = [[c.sb("invQ%d_%d" % (ch, p), [128, 128], F32) for p in range(2)] for ch in range(NCH)]
        self.invT = [[c.sb("invT%d_%d" % (ch, p), [128, 128], F32) for p in range(2)] for ch in range(NCH)]
        if do_gdn:
            self.init_gdn()
        if do_rwkv:
            self.init_rwkv()
        for tt in range(T2 // TT):
            gens = []
            if do_gdn:
                gens.append(self.gdn_tile(tt))
            if do_rwkv:
                gens.append(self.rwkv_tile(tt))
            for g in gens:
                assert next(g) == 'prep'
            reqs = []
            for g in gens:
                r = next(g)
                assert r[0] == 'inv'
                reqs += r[1]
            self.emit_inverse(reqs)
            alive = list(gens)
            while alive:
                steps = []
                for g in list(alive):
                    try:
                        r = next(g)
                        if isinstance(r, tuple) and r[0] == 'seq':
                            steps += r[1]
                    except StopIteration:
                        alive.remove(g)
                while steps:
                    for st in list(steps):
                        try:
                            next(st)
                        except StopIteration:
                            steps.remove(st)
        self.info = c.S.emit(final_wait_keys=self.final_keys)
        c.es.close()

    def tr_batch(self, items):
        c = self.c
        for i0 in range(0, len(items), 4):
            ps, res = c.next_psum()
            grp = items[i0:i0 + 4]
            for i, (in_ap, rd, ev) in enumerate(grp):
                c.mm(ps[:, i * 128:(i + 1) * 128], in_ap, self.ident_bf[:, :], True, True, list(rd) + ['cstb'], [res])
            for i, (in_ap, rd, ev) in enumerate(grp):
                ev(ps[:, i * 128:(i + 1) * 128], res)

    def init_gdn(self):
        c, NS, TT, T2 = self.c, self.NS, self.TT, self.T2
        HA = self.HA = 2
        self.xg = c.dram("xg", [6 * 128, T2], BF16, "ExternalInput")
        self.abd = c.dram("abd", [T2, 4], F32, "ExternalInput")
        self.gpar_d = c.dram("gpar", [128, 24 + 4 + 128], F32, "ExternalInput")
        self.oaT = c.dram("oaT", [HA * 128, T2], BF16, "ExternalOutput")
        gp = self.gpar = c.sb("gpar_sb", [128, 156], F32)
        c.dma('sp', gp[:, :], self.gpar_d[:, :], [], ['gpar'], key='gpar')
        self.cw = lambda ci, j: gp[:, ci * 4 + j: ci * 4 + j + 1]
        self.gain_rep = gp[:, 28:156]
        self.dtb4 = c.sb("dtb4", [128, NS, 2], F32)
        self.nea4 = c.sb("nea4", [128, NS, 2], F32)
        nea = c.sb("nea", [128, 2], F32)
        c.act(nea[:, :], gp[:, 24:26], AF.Exp, ['gpar'], ['nea'])
        for s in range(NS):
            c.copy('dve', self.dtb4[:, s, :], gp[:, 26:28], ['gpar'], [('dtb4', s)])
            c.ts('dve', self.nea4[:, s, :], nea[:, :], -1.0, None, ALU.mult, None, ['nea'], [('nea4', s)])
        self.graw = [[c.sb("graw%d_%d" % (p, ci), [128, 3 + TT], BF16) for ci in range(6)] for p in range(2)]
        self.gacc = [c.sb("gacc%d" % i, [128, TT], F32) for i in range(2)]
        self.gxc = [c.sb("gxc%d" % ci, [128, TT], BF16) for ci in range(6)]
        self.gsq = [c.sb("gsq%d" % i, [128, TT], BF16) for i in range(2)]
        self.abt = c.sb("abt", [128, NS, 4], F32)
        self.lnr = c.sb("lnr", [128, NS, 4], F32)
        names = ['gx', 'ge', 'gg', 'glnb', 'gc', 'glrow', 'ga1', 'gb1', 'ga2', 'gns1', 'gs2', 'gs3', 'gbeta',
                 'gl0', 'gl1']
        self.gcol = {n: c.sb(n, [128, NS, 2], F32) for n in names}
        self.k_tok = [[c.sb("ktok%d_%d" % (s, h), [128, 128], BF16) for h in range(2)] for s in range(NS)]
        self.vb_tok = [[c.sb("vbtok%d_%d" % (s, h), [128, 128], F32) for h in range(2)] for s in range(NS)]
        self.AqkT = [[c.sb("aqkT%d_%d" % (s, h), [128, 128], BF16) for h in range(2)] for s in range(NS)]
        self.TTb = [[c.sb("TTb%d_%d" % (s, h), [128, 128], BF16) for h in range(2)] for s in range(NS)]
        self.o_tok = [[c.sb("otok%d_%d" % (s, h), [128, 128], F32) for h in range(2)] for s in range(NS)]
        self.Emat = [c.sb("emat%d" % i, [128, 128], F32) for i in range(3)]
        self.S_f = [c.sb("S_f%d" % h, [128, 128], F32) for h in range(2)]
        self.S_b = [c.sb("S_b%d" % h, [128, 128], BF16) for h in range(2)]
        for h in range(2):
            c.memset('pool', self.S_f[h][:, :], 0.0, [('S_f', h)])
            c.memset('pool', self.S_b[h][:, :], 0.0, [('S_b', h)])
        self.R_bf = [c.sb("R_bf%d" % h, [128, 128], BF16) for h in range(2)]
        self.x2s = [c.sb("x2s%d" % h, [128, 128], F32) for h in range(2)]
        self.vn = [c.sb("vn%d" % h, [128, 128], BF16) for h in range(2)]
        self.vs = [c.sb("vs%d" % h, [128, 128], BF16) for h in range(2)]
        self.on_bf = [c.sb("on_bf%d" % i, [128, 128], BF16) for i in range(NS)]
        self.ojunk = c.sb("ojunk", [128, 128], F32)
        self.oss = c.sb("oss", [128, 2], F32)
        self.ostage = [[c.sb("ostage%d_%d" % (p, h), [128, TT], BF16) for h in range(2)] for p in range(2)]
        self.final_keys += ['oast0', 'oast1']

    def gdn_tile(self, tt):
        c, NS, TT = self.c, self.NS, self.TT
        cf = self.cf
        par = tt % 2
        t0 = tt * TT
        G = self.gcol
        for ci in range(6):
            raw = self.graw[par][ci]
            rr = ('graw', par, ci)
            if tt == 0:
                c.memset('pool', raw[:, 0:3], 0.0, [rr])
                c.dma('sp', raw[:, 3:3 + TT], self.xg[ci * 128:(ci + 1) * 128, 0:TT], [], [rr], key='graw%d_%d' % (par, ci))
            else:
                c.dma('sp', raw[:, :], self.xg[ci * 128:(ci + 1) * 128, t0 - 3:t0 + TT], [], [rr],
                      key='graw%d_%d' % (par, ci))
            acc = self.gacc[ci % 2]
            ar = ('gacc', ci % 2)
            c.ts('dve', acc[:, :], raw[:, 3:3 + TT], self.cw(ci, 3), None, ALU.mult, None, [rr, 'gpar'], [ar])
            for j in (2, 1, 0):
                c.stt(acc[:, :], raw[:, j:j + TT], self.cw(ci, j), acc[:, :], ALU.mult, ALU.add, [rr, 'gpar', ar], [ar])
            c.act(self.gxc[ci][:, :], acc[:, :], AF.Silu, [ar], [('gxc', ci)])
        if DBG_STOP == 1:
            return
        ps_ss, ps_ssr = c.next_psum()
        for ci in range(4):
            sq = self.gsq[ci % 2]
            sr = ('gsq', ci % 2)
            c.act(sq[:, :], self.gxc[ci][:, :], AF.Square, [('gxc', ci)], [sr])
            for s in range(NS):
                c.mm(ps_ss[:, s * 4 + ci: s * 4 + ci + 1], sq[:, s * 128:(s + 1) * 128], self.ones_bf[:, 0:1], True, True,
                     [sr, 'cstb'], [ps_ssr])
        lnr = self.lnr
        c.act(lnr[:, :, :].rearrange("p s c -> p (s c)"), ps_ss[:, 0:NS * 4], AF.Ln, [ps_ssr, 'eps'], ['lnr'],
              bias=self.eps_sb[:, 0:1])
        c.ts('dve', lnr[:, :, :], lnr[:, :, :], -0.5, None, ALU.mult, None, ['lnr'], ['lnr'])
        if DBG_STOP == 2:
            return
        abt = self.abt
        c.dma('sp', abt[:, :, :], self.abd[t0:t0 + TT, :].rearrange("(s p) c -> p s c", p=128), [], ['abt'], key='abt')
        c.tt('dve', G['gx'][:, :, :], abt[:, :, 0:2], self.dtb4[:, :, :], ALU.add, ['abt'] + [('dtb4', s) for s in range(NS)], ['gx'])
        c.act(G['ge'][:, :, :], G['gx'][:, :, :], AF.Exp, ['gx'], ['ge'])
        c.act(G['ge'][:, :, :], G['ge'][:, :, :], AF.Ln, ['ge', 'one'], ['ge'], bias=self.one_sb[:, 0:1])
        c.tt('dve', G['gg'][:, :, :], G['ge'][:, :, :], self.nea4[:, :, :], ALU.mult, ['ge'] + [('nea4', s) for s in range(NS)], ['gg'])
        c.act(G['glnb'][:, :, :], abt[:, :, 2:4], AF.Exp, ['abt'], ['glnb'], scale=-1.0)
        c.act(G['glnb'][:, :, :], G['glnb'][:, :, :], AF.Ln, ['glnb', 'one'], ['glnb'], bias=self.one_sb[:, 0:1])
        ggf = G['gg'][:, :, :].rearrange("p s c -> p (s c)")
        NC2 = NS * 2
        ps_g, ps_gr = c.next_psum()
        c.mm(ps_g[:, 0:NC2], cf['ltri'], ggf, True, True, ['cst', 'gg'], [ps_gr])
        c.mm(ps_g[:, 16:16 + NC2], cf['bd'], ggf, True, True, ['cst', 'gg'], [ps_gr])
        c.mm(ps_g[:, 32:32 + NC2], cf['c0'], ggf, True, True, ['cst', 'gg'], [ps_gr])
        c.mm(ps_g[:, 48:48 + NC2], cf['c1'], ggf, True, True, ['cst', 'gg'], [ps_gr])
        fl = lambda n: G[n][:, :, :].rearrange("p s c -> p (s c)")
        c.copy('dve', fl('gc'), ps_g[:, 0:NC2], [ps_gr], ['gc'])
        c.copy('dve', fl('glrow'), ps_g[:, 16:16 + NC2], [ps_gr], ['glrow'])
        c.act(fl('gl0'), ps_g[:, 32:32 + NC2], AF.Exp, [ps_gr], ['gl0'])
        c.act(fl('gl1'), ps_g[:, 48:48 + NC2], AF.Exp, [ps_gr], ['gl1'])
        lnrq, lnrk = lnr[:, :, 0:2], lnr[:, :, 2:4]
        A3 = lambda n: G[n][:, :, :]
        c.tt('dve', A3('ga1'), A3('gc'), A3('glnb'), ALU.subtract, ['gc', 'glnb'], ['ga1'])
        c.tt('dve', A3('ga1'), A3('ga1'), lnrk, ALU.add, ['ga1', 'lnr'], ['ga1'])
        c.tt('dve', A3('gb1'), lnrk, A3('gc'), ALU.subtract, ['gc', 'lnr'], ['gb1'])
        c.stt(A3('ga2'), A3('gc'), float(np.log(128.0 ** -0.5)), lnrq, ALU.add, ALU.add, ['gc', 'lnr'], ['ga2'])
        c.act(A3('gns1'), A3('ga1'), AF.Exp, ['ga1'], ['gns1'])
        c.ts('dve', A3('gns1'), A3('gns1'), -1.0, None, ALU.mult, None, ['gns1'], ['gns1'])
        c.act(A3('gs2'), A3('ga2'), AF.Exp, ['ga2'], ['gs2'])
        c.tt('dve', A3('gs3'), A3('gb1'), A3('glrow'), ALU.add, ['gb1', 'glrow'], ['gs3'])
        c.act(A3('gs3'), A3('gs3'), AF.Exp, ['gs3'], ['gs3'])
        c.act(A3('gbeta'), A3('glnb'), AF.Exp, ['glnb'], ['gbeta'], scale=-1.0)
        if DBG_STOP == 3:
            return
        items = []
        for s in range(NS):
            for h in range(2):
                items.append((self.gxc[2 + h][:, s * 128:(s + 1) * 128], [('gxc', 2 + h)],
                              lambda pt, res, s=s, h=h: c.copy('act', self.k_tok[s][h][:, :], pt, [res], [('ktok', s, h)])))
                items.append((self.gxc[4 + h][:, s * 128:(s + 1) * 128], [('gxc', 4 + h)],
                              lambda pt, res, s=s, h=h: c.ts('dve', self.vb_tok[s][h][:, :], pt, G['gbeta'][:, s, h:h + 1], None,
                                                              ALU.mult, None, [res, 'gbeta'], [('vbtok', s, h)])))
        self.tr_batch(items)
        if DBG_STOP == 4:
            return
        yield 'prep'
        for s in range(NS):
            for h in range(2):
                ch = s * 2 + h
                ssl = slice(s * 128, (s + 1) * 128)
                kc_, qc_ = self.gxc[2 + h], self.gxc[h]
                psK, psKr = c.next_psum()
                c.mm(psK[:, 0:128], kc_[:, ssl], kc_[:, ssl], True, True, [('gxc', 2 + h)], [psKr])
                c.mm(psK[:, 128:256], kc_[:, ssl], qc_[:, ssl], True, True, [('gxc', 2 + h), ('gxc', h)], [psKr])
                cols = [G['gb1'][:, s, h:h + 1], G['ga1'][:, s, h:h + 1], G['ga2'][:, s, h:h + 1]]
                coln = ['gb1', 'ga1', 'ga2']
                psE, psEr = c.next_psum()
                masks = ['ms', 'msT', 'miT']
                for i3 in range(3):
                    o_ = psE[:, i3 * 128:(i3 + 1) * 128]
                    c.mm(o_, cols[i3].to_broadcast([128, 128]), cf['ident'], True, False, ['cst', coln[i3]], [psEr])
                    c.mm(o_, cf['ident'], cf[masks[i3]], False, True, ['cst'], [psEr])
                biases = [cols[1], cols[0], cols[0]]
                biasn = ['ga1', 'gb1', 'gb1']
                for i3 in range(3):
                    c.act(self.Emat[i3][:, :], psE[:, i3 * 128:(i3 + 1) * 128], AF.Exp, [psEr, biasn[i3]], [('emat', i3)],
                          bias=biases[i3])
                P0, Q0, T0 = self.invP[ch][0], self.invQ[ch][0], self.invT[ch][0]
                c.tt('dve', r32(P0[:, :]), self.Emat[0][:, :], psK[:, 0:128], ALU.mult, [('emat', 0), psKr], [('invP', ch, 0)])
                c.tt('dve', r32(Q0[:, :]), self.Emat[1][:, :], psK[:, 0:128], ALU.mult, [('emat', 1), psKr], [('invQ', ch, 0)])
                c.tt('dve', self.AqkT[s][h][:, :], self.Emat[2][:, :], psK[:, 128:256], ALU.mult, [('emat', 2), psKr],
                     [('aqkT', s, h)])
                c.stt(r32(T0[:, :]), Q0[:, :], -1.0, cf['ident'], ALU.mult, ALU.add, [('invQ', ch, 0), 'cst'], [('invT', ch, 0)],
                      eng='dve')
        if DBG_STOP == 5:
            return
        yield ('inv', [(s * 2 + h, s, h, self.TTb, 'TTb') for s in range(NS) for h in range(2)])
        if DBG_STOP == 6:
            return
        def gstep(ck, h):
            s, half = ck // 2, ck % 2
            rows = slice(half * 64, half * 64 + 64)
            tcols = slice(s * 128 + half * 64, s * 128 + half * 64 + 64)
            psX, psXr = c.next_psum()
            c.mm(psX[rows, 0:128], self.gxc[2 + h][:, tcols], self.S_b[h][:, :], True, True, [('gxc', 2 + h), ('S_b', h)], [psXr])
            c.mm(psX[rows, 128:256], self.gxc[h][:, tcols], self.S_b[h][:, :], True, True, [('gxc', h), ('S_b', h)], [psXr])
            yield
            c.stt(self.R_bf[h][rows, :], psX[rows, 0:128], G['gns1'][rows, s, h:h + 1], self.vb_tok[s][h][rows, :],
                  ALU.mult, ALU.add, [psXr, 'gns1', ('vbtok', s, h)], [('R_bf', h)])
            c.S.add('dve', lambda e: e.tensor_scalar(out=self.x2s[h][rows, :], in0=psX[rows, 128:256],
                                                     scalar1=G['gs2'][rows, s, h:h + 1], scalar2=None, op0=ALU.mult),
                    [psXr, 'gs2'], [('x2s', h)])
            yield
            psV, psVr = c.next_psum()
            c.mm(psV[rows, 0:128], self.TTb[s][h][rows, half * 64:half * 64 + 64], self.R_bf[h][rows, :], True, True,
                 [('TTb', s, h), ('R_bf', h)], [psVr])
            yield
            c.copy('act', self.vn[h][rows, :], psV[rows, 0:128], [psVr], [('vn', h)])
            c.act(self.vs[h][rows, :], psV[rows, 0:128], AF.Identity, [psVr, 'gs3'], [('vs', h)],
                  scale=G['gs3'][rows, s, h:h + 1])
            yield
            psS, psSr = c.next_psum()
            c.mm(psS[:, 0:128], self.k_tok[s][h][rows, :], self.vs[h][rows, :], True, True,
                 [('ktok', s, h), ('vs', h)], [psSr])
            c.mm(psV[rows, 128:256], self.AqkT[s][h][rows, half * 64:half * 64 + 64], self.vn[h][rows, :], True, True,
                 [('aqkT', s, h), ('vn', h)], [psVr])
            yield
            gl = G['gl0'] if half == 0 else G['gl1']
            c.stt(self.S_f[h][:, :], self.S_f[h][:, :], gl[:, s, h:h + 1], psS[:, 0:128], ALU.mult, ALU.add,
                  [('S_f', h), 'gl0', 'gl1', psSr], [('S_f', h)])
            c.copy('act', self.S_b[h][:, :], self.S_f[h][:, :], [('S_f', h)], [('S_b', h)])
            c.tt('dve', self.o_tok[s][h][rows, :], psV[rows, 128:256], self.x2s[h][rows, :], ALU.add,
                 [psVr, ('x2s', h)], [('otok', s, h)])

        for ck in range(2 * NS):
            yield ('seq', [gstep(ck, 0), gstep(ck, 1)])
        if DBG_STOP == 7:
            return
        for h in range(2):
            st = self.ostage[par][h]
            items = []
            for s in range(NS):
                o = self.o_tok[s][h]
                c.S.add('act', lambda e, o=o, h=h: e.activation(out=self.ojunk[:, :], in_=o[:, :], func=AF.Square,
                                                                 accum_out=self.oss[:, h:h + 1]),
                        [('otok', s, h)], ['ojunk', ('oss', h)])
                c.act(self.oss[:, h:h + 1], self.oss[:, h:h + 1], AF.Ln, [('oss', h), 'eps'], [('oss', h)],
                      bias=self.eps_sb[:, 0:1], scale=1.0 / 128)
                c.act(self.oss[:, h:h + 1], self.oss[:, h:h + 1], AF.Exp, [('oss', h)], [('oss', h)], scale=-0.5)
                onb = self.on_bf[s]
                onr = ('on_bf', s)
                c.stt(onb[:, :], o[:, :], self.oss[:, h:h + 1], self.gain_rep, ALU.mult, ALU.mult,
                      [('otok', s, h), ('oss', h), 'gpar'], [onr])
                items.append((onb[:, :], [onr],
                              lambda pt, res, s=s, st=st, h=h: c.copy('act', st[:, s * 128:(s + 1) * 128], pt, [res],
                                                                      [('ostage', par, h)])))
            self.tr_batch(items)
            c.dma('sp', self.oaT[h * 128:(h + 1) * 128, tt * TT:(tt + 1) * TT], st[:, :], [('ostage', par, h)], ['oaT'],
                  key='oast%d' % par)

    def emit_inverse(self, reqs):
        c = self.c
        R_ = r32
        GRP = 4
        for g0 in range(0, len(reqs), GRP):
            grp = reqs[g0:g0 + GRP]
            cur = 0
            for lev in range(1, 6):
                nxt = 1 - cur
                banks = [c.next_psum() for _ in grp]
                for (ch, s, h, outs, outname), (ps, psr) in zip(grp, banks):
                    Pc, Qc = self.invP[ch][cur], self.invQ[ch][cur]
                    c.mm(ps[:, 0:128], R_(Qc[:, :]), R_(Pc[:, :]), True, True, [('invQ', ch, cur), ('invP', ch, cur)], [psr])
                    if lev < 5:
                        c.mm(ps[:, 128:256], R_(Pc[:, :]), R_(Qc[:, :]), True, True, [('invQ', ch, cur), ('invP', ch, cur)], [psr])
                for i, ((ch, s, h, outs, outname), (ps, psr)) in enumerate(zip(grp, banks)):
                    Pn, Qn = self.invP[ch][nxt], self.invQ[ch][nxt]
                    e1 = 'act' if i % 2 == 0 else 'dve'
                    if lev < 5:
                        c.copy(e1, r32(Pn[:, :]), ps[:, 0:128], [psr], [('invP', ch, nxt)])
                        c.copy(e1, r32(Qn[:, :]), ps[:, 128:256], [psr], [('invQ', ch, nxt)])
                    else:
                        c.copy(e1, r32(Pn[:, :]), ps[:, 0:128], [psr], [('invP', ch, nxt)])
                for (ch, s, h, outs, outname), (ps, psr) in zip(grp, banks):
                    Pn, Tc = self.invP[ch][nxt], self.invT[ch][cur]
                    c.mm(ps[:, 256:384], R_(Pn[:, :]), R_(Tc[:, :]), True, True, [('invP', ch, nxt), ('invT', ch, cur)], [psr])
                for (ch, s, h, outs, outname), (ps, psr) in zip(grp, banks):
                    Tc, Tn = self.invT[ch][cur], self.invT[ch][nxt]
                    if lev < 5:
                        c.tt('dve', r32(Tn[:, :]), ps[:, 256:384], Tc[:, :], ALU.add, [psr, ('invT', ch, cur)], [('invT', ch, nxt)])
                    else:
                        c.tt('dve', outs[s][h][:, :], ps[:, 256:384], Tc[:, :], ALU.add, [psr, ('invT', ch, cur)],
                             [(outname, s, h)])
                cur = nxt


GN_EPS = 64e-5
DECAY_C = -float(np.exp(-0.5))


def _init_rwkv(self):
    c, NS, TT, T2 = self.c, self.NS, self.TT, self.T2
    cf = self.cf
    self.xr = c.dram("xr", [10 * 128, T2], BF16, "ExternalInput")
    self.rpar_d = c.dram("rpar", [128, 20], F32, "ExternalInput")
    self.rrep_d = c.dram("rrep", [128, 512], F32, "ExternalInput")
    self.rlw_d = c.dram("rlw", [128, 4, 256], F32, "ExternalInput")
    self.obT = c.dram("obT", [256, T2], BF16, "ExternalOutput")
    rp = self.rpar = c.sb("rpar_sb", [128, 20], F32)
    c.dma('sp', rp[:, :], self.rpar_d[:, :], [], ['rpar'], key='rpar')
    self.rrep = c.sb("rrep_sb", [128, 512], F32)
    c.dma('sp', self.rrep[:, :], self.rrep_d[:, :], [], ['rrep'], key='rrep')
    rlw_f = c.sb("rlw_f", [128, 4, 256], F32)
    c.dma('sp', rlw_f[:, :, :], self.rlw_d[:, :, :], [], ['rlw_f'], key='rlw_f')
    self.rlw = c.sb("rlw_b", [128, 4, 256], BF16)
    c.copy('dve', self.rlw[:, :, :], rlw_f[:, :, :], ['rlw_f'], ['rlw'])
    self.omka = c.sb("omka", [128, 2], F32)
    c.ts('dve', self.omka[:, :], rp[:, 16:18], -1.0, 1.0, ALU.mult, ALU.add, ['rpar'], ['omka'])
    self.bd_bf = c.sb("bd_bf", [128, 128], BF16)
    c.copy('dve', self.bd_bf[:, :], cf['bd'], ['cst'], ['cstb'])
    self.hsel = c.sb("hsel", [128, 2], BF16)
    c.copy('dve', self.hsel[:, 0:1], cf['c0'][:, 0:1], ['cst'], ['cstb'])
    c.copy('dve', self.hsel[:, 1:2], cf['c1'][:, 0:1], ['cst'], ['cstb'])
    self.gneps = c.sb("gneps", [128, 1], F32)
    c.memset('pool', self.gneps[:, :], GN_EPS, ['gneps'])
    self.rmask = c.sb("rmask", [128, TT], F32)
    c.memset('pool', self.rmask[:, :], 1.0, ['rmask'])
    for k in range(TT // 64):
        c.memset('pool', self.rmask[:, k * 64:k * 64 + 1], 0.0, ['rmask'])
    self.rraw = [c.sb("rraw%d" % j, [128, 1 + TT], BF16) for j in range(10)]
    self.rd = [c.sb("rd%d" % i, [128, TT], F32) for i in range(2)]
    self.xs = [c.sb("xs%d" % j, [128, TT], F32) for j in range(6)]
    self.lora_b = [c.sb("lorab%d" % j, [128, TT], BF16) for j in range(4)]
    self.lw = [c.sb("lw%d" % i, [128, TT], F32) for i in range(2)]
    self.av = [c.sb("av%d" % i, [128, TT], F32) for i in range(2)]
    self.g_tok = [c.sb("gtok%d" % s, [128, 256], F32) for s in range(NS)]
    tn = ['kkr', 'rkk', 'kk', 'tq', 'kp', 'ka', 'cs', 'Em', 'Ex', 'ktf', 'akf']
    self.rt = {n: c.sb("rt_" + n, [128, TT], F32) for n in tn}
    self.rsq = c.sb("rsq", [128, TT], BF16)
    self.Ep = [c.sb("Ep%d" % i, [128, TT], F32) for i in range(2)]
    self.br = [c.sb("br%d" % i, [128, NS, 2, 128], BF16) for i in range(2)]
    self.akT = [c.sb("akT%d" % i, [128, TT], BF16) for i in range(2)]
    self.ktT = [c.sb("ktT%d" % i, [128, TT], BF16) for i in range(2)]
    self.fm3 = [[c.sb("fm3_%d_%d" % (q, i), [128, TT], BF16) for i in range(2)] for q in range(3)]
    self.prod = c.sb("prod", [128, TT], BF16)
    self.tok3 = [[c.sb("tok3_%d_%d" % (q, s), [128, 256], BF16) for s in range(NS)] for q in range(3)]
    self.coef = [c.sb("coef%d" % s, [128, 4], F32) for s in range(NS)]
    self.AraT = [[c.sb("AraT%d_%d" % (s, h), [128, 128], BF16) for h in range(4)] for s in range(NS)]
    self.AbrkT = [[c.sb("AbrkT%d_%d" % (s, h), [128, 256], BF16) for h in range(4)] for s in range(NS)]
    self.TTr = [[c.sb("TTr%d_%d" % (s, h), [128, 128], BF16) for h in range(4)] for s in range(NS)]
    self.ZV = [c.sb("ZV%d" % s, [128, 256], F32) for s in range(NS)]
    self.YV = [c.sb("YV%d" % s, [128, 256], F32) for s in range(NS)]
    self.y_tok = [c.sb("ytok%d" % s, [128, 256], F32) for s in range(NS)]
    self.H_f = [c.sb("H_f%d" % i, [128, 128], F32) for i in range(2)]
    self.H_b = [c.sb("H_b%d" % i, [128, 128], BF16) for i in range(2)]
    for i in range(2):
        c.memset('pool', self.H_f[i][:, :], 0.0, [('H_f', i)])
        c.memset('pool', self.H_b[i][:, :], 0.0, [('H_b', i)])
    self.Z_bf = [c.sb("Z_bf%d" % i, [128, 128], BF16) for i in range(2)]
    self.U_bf = [c.sb("U_bf%d" % i, [128, 128], BF16) for i in range(2)]
    self.ysq = c.sb("ysq", [128, 256], F32)
    self.yn = c.sb("yn", [128, 256], F32)
    self.yfin = c.sb("yfin", [128, 256], BF16)
    self.gst = {n: c.sb("gst_" + n, [128, 4], F32) for n in ['s1', 's2', 'mean', 'msq', 'var']}
    self.obstage = [c.sb("obstage%d" % i, [128, TT], BF16) for i in range(2)]
    self.final_keys += ['obst']


def _rwkv_tile(self, tt):
    c, NS, TT = self.c, self.NS, self.TT
    cf = self.cf
    t0 = tt * TT
    rp = self.rpar
    RT = self.rt
    v3 = lambda ap: ap.rearrange("p (s t) -> p s t", t=128)
    for j in range(10):
        raw = self.rraw[j]
        rr = ('rraw', j)
        if tt == 0:
            c.memset('pool', raw[:, 0:1], 0.0, [rr])
            c.dma('sp', raw[:, 1:1 + TT], self.xr[j * 128:(j + 1) * 128, 0:TT], [], [rr], key='rraw%d' % j)
        else:
            c.dma('sp', raw[:, :], self.xr[j * 128:(j + 1) * 128, t0 - 1:t0 + TT], [], [rr], key='rraw%d' % j)
        d = self.rd[j % 2]
        dr = ('rd', j % 2)
        c.tt('pool', d[:, :], raw[:, 0:TT], raw[:, 1:1 + TT], ALU.subtract, [rr], [dr])
        if j < 6:
            c.stt(self.xs[j][:, :], d[:, :], rp[:, j:j + 1], raw[:, 1:1 + TT], ALU.mult, ALU.add, [dr, rr, 'rpar'], [('xs', j)])
        else:
            c.stt(d[:, :], d[:, :], rp[:, j:j + 1], raw[:, 1:1 + TT], ALU.mult, ALU.add, [dr, rr, 'rpar'], [dr])
            fn = AF.Tanh if j == 6 else (AF.Identity if j == 7 else AF.Sigmoid)
            c.act(self.lora_b[j - 6][:, :], d[:, :], fn, [dr], [('lorab', j - 6)])
    for cc in range(2):
        ps, psr = c.next_psum()
        c.mm(ps[:, :TT], self.rlw[:, 0, cc * 128:(cc + 1) * 128], self.lora_b[0][:, :], True, True, ['rlw', ('lorab', 0)], [psr])
        c.act(self.lw[cc][:, :], ps[:, :TT], AF.Sigmoid, [psr, 'rpar'], [('lw', cc)], bias=rp[:, 10 + cc:11 + cc])
        c.ts('dve', self.lw[cc][:, :], self.lw[cc][:, :], DECAY_C, None, ALU.mult, None, [('lw', cc)], [('lw', cc)])
        ps, psr = c.next_psum()
        c.mm(ps[:, :TT], self.rlw[:, 1, cc * 128:(cc + 1) * 128], self.lora_b[1][:, :], True, True, ['rlw', ('lorab', 1)], [psr])
        c.act(self.av[cc][:, :], ps[:, :TT], AF.Sigmoid, [psr, 'rpar'], [('av', cc)], bias=rp[:, 12 + cc:13 + cc])
    for s in range(NS):
        ps, psr = c.next_psum()
        for kc in range(2):
            c.mm(ps[:, 0:256], self.lora_b[2 + kc][:, s * 128:(s + 1) * 128], self.rlw[:, 2 + kc, :], kc == 0, kc == 1,
                 ['rlw', ('lorab', 2 + kc)], [psr])
        c.copy('act', self.g_tok[s][:, :], ps[:, 0:256], [psr], [('gtok', s)])
    for cc in range(2):
        xr_, xk_, xv_ = self.xs[cc], self.xs[2 + cc], self.xs[4 + cc]
        xrr, xkr, xvr = ('xs', cc), ('xs', 2 + cc), ('xs', 4 + cc)
        c.ts('dve', RT['kkr'][:, :], xk_[:, :], rp[:, 14 + cc:15 + cc], None, ALU.mult, None, [xkr, 'rpar'], ['kkr'])
        c.act(self.rsq[:, :], RT['kkr'][:, :], AF.Square, ['kkr'], ['rsq'])
        ps, psr = c.next_psum()
        c.mm(ps[:, :TT], self.bd_bf[:, :], self.rsq[:, :], True, True, ['cstb', 'rsq'], [psr])
        c.act(RT['rkk'][:, :], ps[:, :TT], AF.Ln, [psr, 'eps'], ['rkk'], bias=self.eps_sb[:, 0:1])
        c.act(RT['rkk'][:, :], RT['rkk'][:, :], AF.Exp, ['rkk'], ['rkk'], scale=-0.5)
        c.tt('dve', RT['kk'][:, :], RT['kkr'][:, :], RT['rkk'][:, :], ALU.mult, ['kkr', 'rkk'], ['kk'])
        c.ts('dve', RT['tq'][:, :], self.av[cc][:, :], rp[:, 16 + cc:17 + cc], self.omka[:, cc:cc + 1], ALU.mult, ALU.add,
             [('av', cc), 'rpar', 'omka'], ['tq'])
        c.tt('dve', RT['kp'][:, :], xk_[:, :], RT['tq'][:, :], ALU.mult, [xkr, 'tq'], ['kp'])
        c.tt('pool', RT['ka'][:, :], RT['kk'][:, :], self.av[cc][:, :], ALU.mult, ['kk', ('av', cc)], ['ka'])
        c.S.add('dve', lambda e, cc=cc: e.tensor_tensor_scan(out=RT['cs'][:, :], data0=self.rmask[:, :], data1=self.lw[cc][:, :],
                                                             initial=0.0, op0=ALU.mult, op1=ALU.add),
                ['rmask', ('lw', cc)], ['cs'])
        c.act(self.Ep[cc][:, :], RT['cs'][:, :], AF.Exp, ['cs'], [('Ep', cc)])
        c.act(RT['Em'][:, :], RT['cs'][:, :], AF.Exp, ['cs'], ['Em'], scale=-1.0)
        c.tt('pool', RT['Ex'][:, :], RT['cs'][:, :], self.lw[cc][:, :], ALU.subtract, ['cs', ('lw', cc)], ['Ex'])
        c.act(RT['Ex'][:, :], RT['Ex'][:, :], AF.Exp, ['Ex'], ['Ex'])
        brr = ('br', cc)
        c.tt('dve', self.br[cc][:, :, 0, :], v3(RT['kk'][:, :]), v3(RT['Ex'][:, :]), ALU.mult, ['kk', 'Ex'], [brr])
        c.tt('dve', self.br[cc][:, :, 1, :], v3(xr_[:, :]), v3(self.Ep[cc][:, :]), ALU.mult, [xrr, ('Ep', cc)], [brr])
        c.stt(RT['akf'][:, :], RT['ka'][:, :], -1.0, RT['Em'][:, :], ALU.mult, ALU.mult, ['ka', 'Em'], ['akf'])
        c.tt('dve', RT['ktf'][:, :], RT['kp'][:, :], RT['Em'][:, :], ALU.mult, ['kp', 'Em'], ['ktf'])
        c.copy('act', self.akT[cc][:, :], RT['akf'][:, :], ['akf'], [('akT', cc)])
        c.copy('act', self.ktT[cc][:, :], RT['ktf'][:, :], ['ktf'], [('ktT', cc)])
        for k8 in range(TT // 64):
            csl = slice(k8 * 64, (k8 + 1) * 64)
            epc = self.Ep[cc][:, k8 * 64 + 63:k8 * 64 + 64]
            c.ts('dve', self.fm3[0][cc][:, csl], RT['ktf'][:, csl], epc, None, ALU.mult, None, ['ktf', ('Ep', cc)], [('fm3', 0, cc)])
            c.ts('pool', self.fm3[1][cc][:, csl], RT['akf'][:, csl], epc, None, ALU.mult, None, ['akf', ('Ep', cc)], [('fm3', 1, cc)])
        c.copy('act', self.fm3[2][cc][:, :], xv_[:, :], [xvr], [('fm3', 2, cc)])
        c.stt(self.prod[:, :], xr_[:, :], rp[:, 18 + cc:19 + cc], RT['kp'][:, :], ALU.mult, ALU.mult, [xrr, 'rpar', 'kp'], ['prod'])
        ps, psr = c.next_psum()
        for s in range(NS):
            c.mm(ps[:, s * 2:s * 2 + 2], self.prod[:, s * 128:(s + 1) * 128], self.hsel[:, :], True, True, ['prod', 'cstb'], [psr])
        for s in range(NS):
            c.copy('dve', self.coef[s][:, cc * 2:cc * 2 + 2], ps[:, s * 2:s * 2 + 2], [psr], [('coef', s)])
        items = []
        for q in range(3):
            for s in range(NS):
                items.append((self.fm3[q][cc][:, s * 128:(s + 1) * 128], [('fm3', q, cc)],
                              lambda pt, res, q=q, s=s, cc=cc: c.copy('act' if (q + s) % 2 else 'dve',
                                                                       self.tok3[q][s][:, cc * 128:(cc + 1) * 128], pt, [res],
                                                                       [('tok3', q, s, cc)])))
        self.tr_batch(items)
    yield 'prep'
    chains = []
    for s in range(NS):
        ssl = slice(s * 128, (s + 1) * 128)
        for hd in range(4):
            cc, hr = hd // 2, slice((hd % 2) * 64, (hd % 2) * 64 + 64)
            ch = NS * 2 + len(chains)
            chains.append((ch, s, hd, self.TTr, 'TTr'))
            brf = self.br[cc][hr, s, :, :].rearrange("p a t -> p (a t)")
            psM, psMr = c.next_psum()
            c.mm(psM[:, 0:128], self.br[cc][hr, s, 0, :], self.akT[cc][hr, ssl], True, True, [('br', cc), ('akT', cc)], [psMr])
            c.mm(psM[:, 128:384], self.akT[cc][hr, ssl], brf, True, True, [('br', cc), ('akT', cc)], [psMr])
            psN, psNr = c.next_psum()
            c.mm(psN[:, 0:256], self.ktT[cc][hr, ssl], brf, True, True, [('br', cc), ('ktT', cc)], [psNr])
            P0, Q0, T0 = self.invP[ch][0], self.invQ[ch][0], self.invT[ch][0]
            c.tt('dve', r32(P0[:, :]), psM[:, 0:128], cf['m_s'], ALU.mult, [psMr, 'cst'], [('invP', ch, 0)])
            c.tt('dve', r32(Q0[:, :]), psM[:, 128:256], cf['mT_s'], ALU.mult, [psMr, 'cst'], [('invQ', ch, 0)])
            c.tt('dve', self.AraT[s][hd][:, :], psM[:, 256:384], cf['mT_i'], ALU.mult, [psMr, 'cst'], [('AraT', s, hd)])
            c.tt('dve', self.AbrkT[s][hd][:, :], psN[:, 0:256], self.cst[:, 10 * 128:12 * 128], ALU.mult, [psNr, 'cst'],
                 [('AbrkT', s, hd)])
            c.tt('dve', r32(T0[:, :]), Q0[:, :], cf['ident'], ALU.add, [('invQ', ch, 0), 'cst'], [('invT', ch, 0)])
    yield ('inv', chains)
    for s in range(NS):
        ps, psr = c.next_psum()
        for hd in range(4):
            vc = slice(hd * 64, hd * 64 + 64)
            cc = hd // 2
            c.mm(ps[:, hd * 64:hd * 64 + 64], self.AbrkT[s][hd][:, 0:128], self.tok3[2][s][:, vc], True, True,
                 [('AbrkT', s, hd), ('tok3', 2, s, cc)], [psr])
            c.mm(ps[:, 256 + hd * 64:256 + hd * 64 + 64], self.AbrkT[s][hd][:, 128:256], self.tok3[2][s][:, vc], True, True,
                 [('AbrkT', s, hd), ('tok3', 2, s, cc)], [psr])
        c.copy('act', self.ZV[s][:, :], ps[:, 0:256], [psr], [('ZV', s)])
        c.copy('act', self.YV[s][:, :], ps[:, 256:512], [psr], [('YV', s)])
    def rstep(ck, cc):
        s, half = ck // 2, ck % 2
        rows = slice(half * 64, half * 64 + 64)
        hb = slice(half * 64, half * 64 + 64)
        ccs = slice(cc * 128, (cc + 1) * 128)
        psZ, psZr = c.next_psum()
        c.mm(psZ[rows, 0:128], self.br[cc][:, s, 0, hb], self.H_b[cc][:, :], True, True, [('br', cc), ('H_b', cc)], [psZr])
        c.mm(psZ[rows, 128:256], self.br[cc][:, s, 1, hb], self.H_b[cc][:, :], True, True, [('br', cc), ('H_b', cc)], [psZr])
        yield
        c.tt('dve', self.Z_bf[cc][rows, :], psZ[rows, 0:128], self.ZV[s][rows, ccs], ALU.add, [psZr, ('ZV', s)], [('Z_bf', cc)])
        c.tt('dve', self.y_tok[s][rows, ccs], psZ[rows, 128:256], self.YV[s][rows, ccs], ALU.add, [psZr, ('YV', s)],
             [('ytok', s, cc)])
        yield
        psU, psUr = c.next_psum()
        for hh in range(2):
            hd = cc * 2 + hh
            hc = slice(hh * 64, hh * 64 + 64)
            c.mm(psU[rows, hc], self.TTr[s][hd][rows, hb], self.Z_bf[cc][rows, hc], True, True,
                 [('TTr', s, hd), ('Z_bf', cc)], [psUr])
        yield
        c.copy('act', self.U_bf[cc][rows, :], psU[rows, 0:128], [psUr], [('U_bf', cc)])
        yield
        psH, psHr = c.next_psum()
        for hh in range(2):
            hd = cc * 2 + hh
            hc = slice(hh * 64, hh * 64 + 64)
            hg = slice(cc * 128 + hh * 64, cc * 128 + hh * 64 + 64)
            c.mm(psH[hc, hc], self.tok3[1][s][rows, hg], self.U_bf[cc][rows, hc], True, False,
                 [('tok3', 1, s, cc), ('U_bf', cc)], [psHr])
            c.mm(psH[hc, hc], self.tok3[0][s][rows, hg], self.tok3[2][s][rows, hg], False, True,
                 [('tok3', 0, s, cc), ('tok3', 2, s, cc)], [psHr])
        for hh in range(2):
            hd = cc * 2 + hh
            hc = slice(hh * 64, hh * 64 + 64)
            c.mm(psU[rows, 128 + hh * 64:128 + hh * 64 + 64], self.AraT[s][hd][rows, hb], self.U_bf[cc][rows, hc], True, True,
                 [('AraT', s, hd), ('U_bf', cc)], [psUr])
        yield
        gcol = self.Ep[cc][:, ck * 64 + 63:ck * 64 + 64]
        for hh in range(2):
            hc = slice(hh * 64, hh * 64 + 64)
            c.stt(self.H_f[cc][hc, hc], self.H_f[cc][hc, hc], gcol[hc, :], psH[hc, hc], ALU.mult, ALU.add,
                  [('H_f', cc), ('Ep', cc), psHr], [('H_f', cc)])
        c.copy('act', self.H_b[cc][:, :], self.H_f[cc][:, :], [('H_f', cc)], [('H_b', cc)])
        c.tt('dve', self.y_tok[s][rows, ccs], psU[rows, 128:256], self.y_tok[s][rows, ccs], ALU.add, [psUr, ('ytok', s, cc)],
             [('ytok', s, cc)])

    for ck in range(2 * NS):
        yield ('seq', [rstep(ck, 0), rstep(ck, 1)])
    gs = self.gst
    for s in range(NS):
        y = self.y_tok[s]
        yr = [('ytok', s, 0), ('ytok', s, 1)]
        y3 = y[:, :].rearrange("p (h n) -> p h n", n=64)
        c.S.add('dve', lambda e, y3=y3: e.tensor_reduce(out=gs['s1'][:, :], in_=y3, axis=AX.X, op=ALU.add), yr, ['gs1'])
        c.act(self.ysq[:, :], y[:, :], AF.Square, yr, ['ysq'])
        c.S.add('dve', lambda e: e.tensor_reduce(out=gs['s2'][:, :], in_=self.ysq[:, :].rearrange("p (h n) -> p h n", n=64),
                                                 axis=AX.X, op=ALU.add), ['ysq'], ['gs2'])
        c.ts('dve', gs['mean'][:, :], gs['s1'][:, :], 1.0 / 64, None, ALU.mult, None, ['gs1'], ['gmean'])
        c.tt('dve', gs['msq'][:, :], gs['mean'][:, :], gs['mean'][:, :], ALU.mult, ['gmean'], ['gmsq'])
        c.stt(gs['var'][:, :], gs['s2'][:, :], 1.0 / 64, gs['msq'][:, :], ALU.mult, ALU.subtract, ['gs2', 'gmsq'], ['gvar'])
        c.act(gs['var'][:, :], gs['var'][:, :], AF.Ln, ['gvar', 'gneps'], ['gvar'], bias=self.gneps[:, 0:1])
        c.act(gs['var'][:, :], gs['var'][:, :], AF.Exp, ['gvar'], ['gvar'], scale=-0.5)
        for hd in range(4):
            hg = slice(hd * 64, hd * 64 + 64)
            c.ts('dve', self.yn[:, hg], y[:, hg], gs['mean'][:, hd:hd + 1], gs['var'][:, hd:hd + 1], ALU.subtract, ALU.mult,
                 yr + ['gmean', 'gvar'], [('yn', hd)])
        ynr = [('yn', hd) for hd in range(4)]
        c.tt('dve', self.yn[:, :], self.yn[:, :], self.rrep[:, 0:256], ALU.mult, ynr + ['rrep'], ynr)
        c.tt('pool', self.yn[:, :], self.yn[:, :], self.rrep[:, 256:512], ALU.add, ynr + ['rrep'], ynr)
        for hd in range(4):
            hg = slice(hd * 64, hd * 64 + 64)
            c.stt(self.yn[:, hg], self.tok3[2][s][:, hg], self.coef[s][:, hd:hd + 1], self.yn[:, hg], ALU.mult, ALU.add,
                  [('tok3', 2, s, hd // 2), ('coef', s), ('yn', hd)], [('yn', hd)])
        c.tt('dve', self.yfin[:, :], self.yn[:, :], self.g_tok[s][:, :], ALU.mult, ynr + [('gtok', s)], ['yfin'])
        items = []
        for cc in range(2):
            items.append((self.yfin[:, cc * 128:(cc + 1) * 128], ['yfin'],
                          lambda pt, res, s=s, cc=cc: c.copy('act', self.obstage[cc][:, s * 128:(s + 1) * 128], pt, [res],
                                                             [('obstage', cc)])))
        self.tr_batch(items)
    for cc in range(2):
        c.dma('sp', self.obT[cc * 128:(cc + 1) * 128, t0:t0 + TT], self.obstage[cc][:, :], [('obstage', cc)], ['obT'], key='obst')


K2.init_rwkv = _init_rwkv
K2.rwkv_tile = _rwkv_tile


def build_k3(D, DFF, VW, T, NT=512, final=False):
    c = Ctx()
    nc = c.nc
    KC = D // 128
    NF = DFF // 128
    VC = VW // 128
    h1T = c.dram("h1T", [D, T], F32, "ExternalInput")
    oa = c.dram("oa", [VW, T], BF16, "ExternalInput")
    ob = c.dram("ob", [VW, T], BF16, "ExternalInput")
    ploc = c.dram("ploc", [VW + 2 * D, T], BF16, "ExternalInput")
    wa_b = c.dram("wa", [VW, D], BF16, "ExternalInput")
    wb_b = c.dram("wb", [VW, D], BF16, "ExternalInput")
    wo_b = c.dram("wo", [D, D], BF16, "ExternalInput")
    g1 = c.dram("g1", [128, KC], F32, "ExternalInput")
    wg_b = c.dram("wg", [D, DFF], BF16, "ExternalInput")
    wu_b = c.dram("wu", [D, DFF], BF16, "ExternalInput")
    wd_b = c.dram("wd", [DFF, D], BF16, "ExternalInput")
    for n_ in ('wa_b', 'wb_b', 'wo_b', 'wg_b', 'wu_b', 'wd_b'):
        c.wres[n_] = []
    h3T = c.dram("h3T", [D, T], F32, "ExternalOutput")
    if final:
        gf = c.dram("gf", [128, KC], F32, "ExternalInput")
        outT = c.dram("outT", [D, T], F32, "ExternalOutput")
    c.init_psum(8)
    h_sb = c.sb("h_sb", [128, KC, NT], F32)
    n_sb = c.sb("n_sb", [128, KC, NT], BF16)
    hid_sb = c.sb("hid_sb", [128, max(NF, 2 * VC + KC), NT], BF16)
    g1_sb = c.sb("g1_sb", [128, KC], F32)
    ones_bf = c.sb("ones_bf", [128, 128], BF16)
    eps_sb = c.sb("eps_sb", [128, 1], F32)
    rstd_sb = c.sb("rstd_sb", [128, NT], F32)
    sq_bufs = [(c.sb("sq%d" % i, [128, NT], BF16), "sq%d" % i) for i in range(2)]
    sg_bufs = [(c.sb("sg%d" % i, [128, NT], F32), "sg%d" % i) for i in range(2)]
    wslots = [((c.sb("wga%d" % i, [128, KC, 512], BF16), "wga%d" % i),
               (c.sb("wua%d" % i, [128, KC, 512], BF16), "wua%d" % i)) for i in range(2)]
    wdslots = [(c.sb("wds%d" % i, [128, 11, 512], BF16), "wds%d" % i) for i in range(2)]
    gt_bufs = [(c.sb("gt%d" % i, [128, 2, NT], BF16), "gt%d" % i) for i in range(2)]
    t_bufs = [(c.sb("tb%d" % i, [128, NT], F32), "tb%d" % i) for i in range(2)]
    c.eps_ap = eps_sb[:, 0:1]
    c.memset('pool', ones_bf[:, :], 1.0, ['ones'])
    c.memset('pool', eps_sb[:, :], NORM_EPS, ['eps'])
    c.dma('sp', g1_sb[:, :], g1[:, :], [], ['g1'], key='g1')
    if final:
        gf_sb = c.sb("gf_sb", [128, KC], F32)
        c.dma('sp', gf_sb[:, :], gf[:, :], [], ['gf'], key='gf')
        fo_bufs = [(c.sb("fo%d" % i, [128, NT], F32), "fo%d" % i) for i in range(2)]
    hv = h1T.rearrange("(kc p) t -> p kc t", p=128)
    h3v = h3T.rearrange("(kc p) t -> p kc t", p=128)
    oav = oa.rearrange("(kc p) t -> p kc t", p=128)
    obv = ob.rearrange("(kc p) t -> p kc t", p=128)
    plv = ploc.rearrange("(kc p) t -> p kc t", p=128)
    wav = wa_b.rearrange("(kc p) f -> p kc f", p=128)
    wbv = wb_b.rearrange("(kc p) f -> p kc f", p=128)
    wov = wo_b.rearrange("(kc p) f -> p kc f", p=128)
    CW = 512 if D % 512 == 0 else 128
    NJ = CW // 128
    YO = 2 * VC
    for tt in range(T // NT):
        tsl = slice(tt * NT, (tt + 1) * NT)
        c.dma('sp', h_sb[:, :, :], hv[:, :, tsl], [], [('h', kc) for kc in range(KC)], key='hload')
        c.dma('sp', n_sb[:, 0:VC, :], oav[:, :, tsl], [], [('n', k) for k in range(VC)], key='oaload')
        c.dma('sp', n_sb[:, VC:2 * VC, :], plv[:, 0:VC, tsl], [], [('n', VC + k) for k in range(VC)], key='zload')
        c.dma('sp', hid_sb[:, VC:2 * VC, :], obv[:, :, tsl], [], [('hid', VC + k) for k in range(VC)], key='obload')
        for k in range(VC):
            sg, sgr = sg_bufs[k % 2]
            c.act(sg[:, :], n_sb[:, VC + k, :], AF.Silu, [('n', VC + k)], [sgr])
            c.tt('dve', hid_sb[:, k, :], sg[:, :], n_sb[:, k, :], ALU.mult, [sgr, ('n', k)], [('hid', k)])
        si = 0
        for cs in range(D // CW):
            (wa_sb, war), (wb_sb, wbr) = wslots[si % 2]
            si += 1
            c.dma('sp', wa_sb[:, :VC, :CW], wav[:, :, cs * CW:(cs + 1) * CW], c.wres['wa_b'], [war], key=war)
            c.dma('sp', wb_sb[:, :VC, :CW], wbv[:, :, cs * CW:(cs + 1) * CW], c.wres['wb_b'], [wbr], key=wbr)
            for j in range(NJ):
                dc = cs * NJ + j
                gt, gtr = gt_bufs[dc % 2]
                c.dma('act', gt[:, 0, :], plv[:, VC + dc, tsl], [], [gtr], key=gtr + 'a')
                c.dma('act', gt[:, 1, :], plv[:, VC + KC + dc, tsl], [], [gtr], key=gtr + 'b')
                pa, par_ = c.next_psum()
                pb, pbr = c.next_psum()
                for k in range(VC):
                    c.mm(pa[:, :NT], wa_sb[:, k, j * 128:(j + 1) * 128], hid_sb[:, k, :], k == 0, k == VC - 1,
                         [war, ('hid', k)], [par_])
                for k in range(VC):
                    c.mm(pb[:, :NT], wb_sb[:, k, j * 128:(j + 1) * 128], hid_sb[:, VC + k, :], k == 0, k == VC - 1,
                         [wbr, ('hid', VC + k)], [pbr])
                sg, sgr = sg_bufs[0]
                sg2, sgr2 = sg_bufs[1]
                c.act(sg[:, :], gt[:, 0, :], AF.Sigmoid, [gtr], [sgr])
                c.act(sg2[:, :], gt[:, 1, :], AF.Sigmoid, [gtr], [sgr2])
                t1, t1r = t_bufs[0]
                t2, t2r = t_bufs[1]
                c.tt('dve', t1[:, :], sg[:, :], pa[:, :NT], ALU.mult, [sgr, par_], [t1r])
                c.tt('dve', t2[:, :], sg2[:, :], pb[:, :NT], ALU.mult, [sgr2, pbr], [t2r])
                c.tt('pool', hid_sb[:, YO + dc, :], t1[:, :], t2[:, :], ALU.add, [t1r, t2r], [('hid', YO + dc)])
        for cs in range(D // CW):
            (wo_sb, wor), _ = wslots[si % 2]
            si += 1
            c.dma('sp', wo_sb[:, :, :CW], wov[:, :, cs * CW:(cs + 1) * CW], c.wres['wo_b'], [wor], key=wor)
            for j in range(NJ):
                dc = cs * NJ + j
                ps, psr = c.next_psum()
                for k in range(KC):
                    c.mm(ps[:, :NT], wo_sb[:, k, j * 128:(j + 1) * 128], hid_sb[:, YO + k, :], k == 0, k == KC - 1,
                         [wor, ('hid', YO + k)], [psr])
                c.tt('dve', h_sb[:, dc, :], ps[:, :NT], h_sb[:, dc, :], ALU.add, [psr, ('h', dc)], [('h', dc)])
        emit_rmsnorm(c, h_sb, 'h', g1_sb, 'g1', n_sb, 'n', KC, NT, D, ones_bf, sq_bufs, rstd_sb)
        emit_ffn(c, n_sb, 'n', h_sb, 'h', hid_sb, wg_b, wu_b, wd_b, ('wg_b', 'wu_b', 'wd_b'), KC, NT, DFF,
                 wslots, wdslots, sg_bufs)
        c.dma('sp', h3v[:, :, tsl], h_sb[:, :, :], [('h', kc) for kc in range(KC)], ['h3T'], key='hstore')
        if final:
            ps, psr = c.next_psum()
            for kc in range(KC):
                sq, sqr = sq_bufs[kc % 2]
                c.act(sq[:, :NT], h_sb[:, kc, :], AF.Square, [('h', kc)], [sqr])
                c.mm(ps[:, :NT], ones_bf[:, :], sq[:, :NT], kc == 0, kc == KC - 1, [sqr, 'ones'], [psr])
            c.act(rstd_sb[:, :NT], ps[:, :NT], AF.Ln, [psr], ['rstd'], bias=c.eps_ap, scale=1.0 / D)
            c.act(rstd_sb[:, :NT], rstd_sb[:, :NT], AF.Exp, ['rstd'], ['rstd'], scale=-0.5)
            for kc in range(KC):
                fo, fr = fo_bufs[kc % 2]
                c.stt(fo[:, :], h_sb[:, kc, :], gf_sb[:, kc:kc + 1], rstd_sb[:, :NT], ALU.mult, ALU.mult,
                      [('h', kc), 'rstd', 'gf'], [fr])
                c.dma('act', outT[kc * 128:(kc + 1) * 128, tsl], fo[:, :], [fr], ['outT'], key='fo%d' % (kc % 2))
    fk = ['hstore'] + (['fo0', 'fo1'] if final else [])
    info = c.S.emit(final_wait_keys=fk)
    c.es.close()
    return nc, info


K0_SPECS = [('wg1', 2048, 5632), ('wu1', 2048, 5632), ('wd1', 5632, 2048), ('wfm', 2048, 11776), ('wa', 1024, 2048),
            ('wb', 1024, 2048), ('wo', 2048, 2048), ('wg2', 2048, 5632), ('wu2', 2048, 5632), ('wd2', 5632, 2048)]


def build_k0(depth, ncores=8):
    c = Ctx()
    keys = []
    i = 0
    for l in range(depth):
        for (n, r, cl) in K0_SPECS:
            rs = r // ncores
            src = c.dram("%s_%d" % (n, l), [rs, cl], F32, "ExternalInput")
            dst = c.dram("%s_%db" % (n, l), [rs, cl], BF16, "ExternalOutput")
            k = 'cast%d' % (i % 8)
            i += 1
            c.dma('pool', dst[:, :], src[:, :], [], [("o", n, l)], key=k)
            if k not in keys:
                keys.append(k)
    c.S.emit(final_wait_keys=keys)
    c.es.close()
    return c.nc


D_MODEL, D_FF, DEPTH, BATCH, SEQ = 2048, 5632, 4, 2, 8192
NCORES = 8
TPC = BATCH * SEQ // NCORES
NFM = 11776
_PROGS = {}


def _prog(name):
    if name not in _PROGS:
        if name == 'k0':
            _PROGS[name] = build_k0(DEPTH)
        elif name == 'k1':
            _PROGS[name] = build_k1(D_MODEL, D_FF, NFM, 16, TPC)[0]
        elif name == 'k2':
            _PROGS[name] = K2(SEQ).c.nc
        elif name == 'k3':
            _PROGS[name] = build_k3(D_MODEL, D_FF, 1024, TPC, final=False)[0]
        elif name == 'k3f':
            _PROGS[name] = build_k3(D_MODEL, D_FF, 1024, TPC, final=True)[0]
    return _PROGS[name]


def _pk(g):
    g = np.asarray(g, np.float32)
    return np.ascontiguousarray(g.reshape(-1, 128).T)


def _pad128(x):
    o = np.zeros((128,) + x.shape[1:], x.dtype)
    o[:x.shape[0]] = x
    return o


def _run(nc, in_maps):
    res = run_bass_kernel_spmd(nc, in_maps, core_ids=list(range(NCORES)))
    return res.results


def kernel(**inp):
    f32 = lambda a: np.asarray(a, np.float32)
    x = f32(inp['x'])
    hT = []
    for cidx in range(NCORES):
        b, tq = cidx // 4, cidx % 4
        hT.append(np.ascontiguousarray(x[b, tq * TPC:(tq + 1) * TPC, :].T))
    cst = k2_consts()
    out = None
    wsrc = {}
    wabs = []
    for l in range(DEPTH):
        w_in = f32(inp['w_in'][l])
        wl_p = np.zeros((D_MODEL, 128), np.float32); wl_p[:, :96] = w_in[:, 4112 + 3072:4112 + 3168]
        al_p = np.zeros((D_MODEL, 128), np.float32); al_p[:, :96] = w_in[:, 4112 + 3168:4112 + 3264]
        wfm = np.concatenate([
            w_in[:, 0:3072], w_in[:, 4112:4112 + 3072], wl_p, al_p, w_in[:, 4112 + 3264:4112 + 3520],
            w_in[:, 3072:4096], w_in[:, 7632:9680], w_in[:, 9680:11728]], axis=1)
        wabs.append(np.ascontiguousarray(w_in[:, 4096:4112]))
        srcs = dict(wg1=inp['ffn1_w_gate'][l], wu1=inp['ffn1_w_up'][l], wd1=inp['ffn1_w_down'][l], wfm=wfm,
                    wa=inp['w_branch_a'][l], wb=inp['w_branch_b'][l], wo=inp['w_out'][l],
                    wg2=inp['ffn2_w_gate'][l], wu2=inp['ffn2_w_up'][l], wd2=inp['ffn2_w_down'][l])
        for n, r, cl in K0_SPECS:
            wsrc[(n, l)] = f32(srcs[n])
        del wfm, w_in
    in0 = []
    for cidx in range(NCORES):
        d = {}
        for (n, l), a in wsrc.items():
            rs = a.shape[0] // NCORES
            d["%s_%d" % (n, l)] = np.ascontiguousarray(a[cidx * rs:(cidx + 1) * rs])
        in0.append(d)
    r0 = _run(_prog('k0'), in0)
    del in0
    wbf = {}
    for (n, l) in list(wsrc.keys()):
        wbf[(n, l)] = np.ascontiguousarray(np.concatenate([r0[cidx]["%s_%db" % (n, l)] for cidx in range(NCORES)], axis=0))
    del r0, wsrc
    for l in range(DEPTH):
        common1 = dict(g1=_pk(inp['ffn1_norm'][l]), g2=_pk(inp['mix_norm'][l]), wg=wbf[('wg1', l)],
                       wu=wbf[('wu1', l)], wd=wbf[('wd1', l)], wfm=wbf[('wfm', l)], wab=wabs[l])
        r1 = _run(_prog('k1'), [dict(hT=hT[cidx], **common1) for cidx in range(NCORES)])
        del common1
        conv = f32(inp['gdn_conv'][l]); a_log = f32(inp['gdn_a_log'][l]); dtb = f32(inp['gdn_dt_bias'][l])
        ogain = f32(inp['gdn_out_norm'][l]); mu = f32(inp['rw_mu'][l])
        w0 = f32(inp['rw_w0'][l]); a0 = f32(inp['rw_a0'][l]); kk_ = f32(inp['rw_k_k'][l]); ka_ = f32(inp['rw_k_a'][l])
        rk_ = f32(inp['rw_r_k'][l]).reshape(-1); lnw = f32(inp['rw_ln_w'][l]); lnb = f32(inp['rw_ln_b'][l])
        w_up = f32(inp['rw_w_up'][l]); a_up = f32(inp['rw_a_up'][l]); g_up = f32(inp['rw_g_up'][l])
        in2 = []
        for m in range(NCORES):
            b, hg = m // 4, m % 4
            PTb = [r1[b * 4 + tq]['PT'] for tq in range(4)]
            rows = lambda r0, n: np.concatenate([p[r0:r0 + n] for p in PTb], axis=1)
            xg = np.concatenate([rows(0 + hg * 256, 256), rows(1024 + hg * 256, 256), rows(2048 + hg * 256, 256)], axis=0)
            xr = np.concatenate([rows(3072 + hg * 256, 256), rows(4096 + hg * 256, 256), rows(5120 + hg * 256, 256),
                                 rows(6144, 512)], axis=0)
            abf = np.concatenate([r1[b * 4 + tq]['ab'] for tq in range(4)], axis=0)
            abd = np.ascontiguousarray(np.concatenate([abf[:, hg * 2:hg * 2 + 2], abf[:, 8 + hg * 2:8 + hg * 2 + 2]], axis=1))
            gpar = np.zeros((128, 156), np.float32)
            for ci in range(6):
                kind, h = ci // 2, ci % 2
                ch0 = kind * 1024 + (hg * 2 + h) * 128
                gpar[:, ci * 4:(ci + 1) * 4] = conv[:, ch0:ch0 + 128].T
            gpar[:, 24:26] = a_log[None, hg * 2:hg * 2 + 2]
            gpar[:, 26:28] = dtb[None, hg * 2:hg * 2 + 2]
            gpar[:, 28:156] = ogain[None, :]
            c0 = hg * 256
            muT = np.concatenate([mu[c0:c0 + 256], mu[1024 + c0:1024 + c0 + 256], mu[2048 + c0:2048 + c0 + 256],
                                  _pad128(mu[3072:3168]), _pad128(mu[3168:3264]), mu[3264:3520]])
            rpar = np.zeros((128, 20), np.float32)
            rpar[:, 0:10] = muT.reshape(10, 128).T
            for cc in range(2):
                sl_ = slice(c0 + cc * 128, c0 + (cc + 1) * 128)
                rpar[:, 10 + cc] = w0[sl_]; rpar[:, 12 + cc] = a0[sl_]; rpar[:, 14 + cc] = kk_[sl_]
                rpar[:, 16 + cc] = ka_[sl_]; rpar[:, 18 + cc] = rk_[sl_]
            rrep = np.ascontiguousarray(np.concatenate([np.broadcast_to(lnw[c0:c0 + 256], (128, 256)),
                                                        np.broadcast_to(lnb[c0:c0 + 256], (128, 256))], axis=1))
            rlw = np.ascontiguousarray(np.stack([_pad128(w_up[:, c0:c0 + 256]), _pad128(a_up[:, c0:c0 + 256]),
                                                 g_up[0:128, c0:c0 + 256], g_up[128:256, c0:c0 + 256]], axis=1))
            in2.append(dict(cst=cst, xg=np.ascontiguousarray(xg), abd=abd, gpar=gpar, xr=np.ascontiguousarray(xr),
                            rpar=rpar, rrep=rrep, rlw=rlw))
        r2 = _run(_prog('k2'), in2)
        del in2
        last = (l == DEPTH - 1)
        common3 = dict(wa=wbf[('wa', l)], wb=wbf[('wb', l)], wo=wbf[('wo', l)],
                       g1=_pk(inp['ffn2_norm'][l]), wg=wbf[('wg2', l)], wu=wbf[('wu2', l)],
                       wd=wbf[('wd2', l)])
        if last:
            common3['gf'] = _pk(inp['final_norm'])
        in3 = []
        for cidx in range(NCORES):
            b, tq = cidx // 4, cidx % 4
            tsl = slice(tq * TPC, (tq + 1) * TPC)
            oa = np.ascontiguousarray(np.concatenate([r2[b * 4 + hg]['oaT'][:, tsl] for hg in range(4)], axis=0))
            ob = np.ascontiguousarray(np.concatenate([r2[b * 4 + hg]['obT'][:, tsl] for hg in range(4)], axis=0))
            ploc = np.ascontiguousarray(r1[cidx]['PT'][6656:11776])
            in3.append(dict(h1T=r1[cidx]['h1T'], oa=oa, ob=ob, ploc=ploc, **common3))
        del r2
        r3 = _run(_prog('k3f' if last else 'k3'), in3)
        del in3, r1
        hT = [r3[cidx]['h3T'] for cidx in range(NCORES)]
        if last:
            out = np.zeros((BATCH, SEQ, D_MODEL), np.float32)
            for cidx in range(NCORES):
                b, tq = cidx // 4, cidx % 4
                out[b, tq * TPC:(tq + 1) * TPC, :] = r3[cidx]['outT'].T
    return out
```

```python
import numpy as np
import ml_dtypes
from contextlib import ExitStack
import concourse.bass as bass
import concourse.mybir as mybir
from concourse.bass_utils import run_bass_kernel_spmd

F32 = mybir.dt.float32
BF16 = mybir.dt.bfloat16
AF = mybir.ActivationFunctionType
ALU = mybir.AluOpType
AX = mybir.AxisListType

SEM_LIM = 30000


class Sched:
    def __init__(self, nc, same_engine_sync=True):
        self.nc = nc
        self.ops = []
        self.lastw = {}
        self.readers = {}
        self.last_dma = {}
        self.exclusive = set()
        self.same_engine_sync = same_engine_sync

    def add(self, eng, fn, reads=(), writes=(), dma_key=None):
        idx = len(self.ops)
        deps = set()
        for r in reads:
            w = self.lastw.get(r)
            if w is not None:
                deps.add(w)
            if r in self.exclusive:
                for k_, x in self.readers.get(r, {}).items():
                    if k_[0] != eng:
                        deps.add(x)
        for w_ in writes:
            w = self.lastw.get(w_)
            if w is not None:
                deps.add(w)
            for x in self.readers.get(w_, {}).values():
                deps.add(x)
        if dma_key is not None:
            p = self.last_dma.get(dma_key)
            if p is not None:
                deps.add(p)
            self.last_dma[dma_key] = idx
        rk = (eng, idx) if dma_key is not None else (eng, -1)
        for r in reads:
            self.readers.setdefault(r, {})[rk] = idx
        for w_ in writes:
            self.lastw[w_] = idx
            self.readers[w_] = {}
        deps.discard(idx)
        self.ops.append(dict(eng=eng, fn=fn, deps=deps, dma_key=dma_key))
        return idx

    def emit(self, final_wait_keys=()):
        nc = self.nc
        ops = self.ops
        needed = set()
        for o in ops:
            eff = set()
            for j in o['deps']:
                d = ops[j]
                if d['dma_key'] is None and d['eng'] == o['eng'] and (o['eng'] == 'pe' or not self.same_engine_sync):
                    continue
                eff.add(j)
            o['deps'] = eff
            needed |= eff
        final_ops = [self.last_dma[k] for k in final_wait_keys if k in self.last_dma]
        needed |= set(final_ops)
        eng_count = {}
        dma_count = {}
        sem_names = set()
        for i, o in enumerate(ops):
            o['sig'] = None
            if i not in needed:
                continue
            if o['dma_key'] is not None:
                k = o['dma_key']
                dma_count[k] = dma_count.get(k, 0) + 1
                o['sig'] = ('d_%s' % (k,), 16 * dma_count[k], 16)
            else:
                e = o['eng']
                c = eng_count.get(e, 0)
                eng_count[e] = c + 1
                o['sig'] = ('e_%s_%d' % (e, c // SEM_LIM), (c % SEM_LIM) + 1, 1)
            sem_names.add(o['sig'][0])
        engs = ['sp', 'act', 'dve', 'pe', 'pool']
        per_eng = {e: [] for e in engs}
        for i, o in enumerate(ops):
            per_eng[o['eng']].append(i)
        with ExitStack() as es:
            sems = {}
            for n in sorted(sem_names):
                sems[n] = es.enter_context(nc.semaphore(n))
            block = es.enter_context(nc.Block())

            def run_engine(ename, eobj):
                waited = {}
                for i in per_eng[ename]:
                    o = ops[i]
                    for j in sorted(o['deps']):
                        d = ops[j]
                        if d['dma_key'] is None and d['eng'] == ename:
                            if ename == 'pe' or not self.same_engine_sync:
                                continue
                        sname, val, _ = d['sig']
                        if waited.get(sname, 0) >= val:
                            continue
                        eobj.wait_ge(sems[sname], val)
                        waited[sname] = val
                    ins = o['fn'](eobj)
                    if o['sig'] is not None:
                        ins.then_inc(sems[o['sig'][0]], o['sig'][2])
                if ename == 'sp':
                    for j in final_ops:
                        sname, val, _ = ops[j]['sig']
                        if waited.get(sname, 0) >= val:
                            continue
                        eobj.wait_ge(sems[sname], val)
                        waited[sname] = val

            @block.sync
            def _(e):
                run_engine('sp', e)

            @block.scalar
            def _(e):
                run_engine('act', e)

            @block.vector
            def _(e):
                run_engine('dve', e)

            @block.tensor
            def _(e):
                run_engine('pe', e)

            @block.gpsimd
            def _(e):
                run_engine('pool', e)
        return dict(n_ops=len(ops), eng_count=eng_count, n_sems=len(sem_names))


class Ctx:
    def __init__(self, name="k"):
        self.nc = bass.Bass("TRN2", target_bir_lowering=False)
        import os as _os2
        self.S = Sched(self.nc, same_engine_sync=bool(int(_os2.environ.get('MK_SES', '1'))))
        self.es = ExitStack()
        self.psum = []
        self.psum_i = 0
        self.uid = 0
        self.wres = {}

    def dram(self, name, shape, dtype, kind="Internal"):
        return self.nc.dram_tensor(name, list(shape), dtype, kind=kind).ap()

    def sb(self, name, shape, dtype):
        return self.es.enter_context(self.nc.sbuf_tensor(name, list(shape), dtype))

    def init_psum(self, n=8):
        for i in range(n):
            t = self.es.enter_context(self.nc.psum_tensor("ps%d" % i, [128, 512], F32))
            self.psum.append((t, 'ps%d' % i))
            self.S.exclusive.add('ps%d' % i)

    def next_psum(self):
        t = self.psum[self.psum_i % len(self.psum)]
        self.psum_i += 1
        return t

    def dma(self, eng, out, in_, reads, writes, key):
        return self.S.add(eng, lambda e: e.dma_start(out=out, in_=in_), reads, writes, dma_key=key)

    def mm(self, out, lhsT, rhs, start, stop, reads, writes):
        return self.S.add('pe', lambda e: e.matmul(out, lhsT, rhs, start=start, stop=stop), reads, writes)

    def act(self, out, in_, func, reads, writes, bias=None, scale=None):
        kw = {}
        if bias is not None:
            kw['bias'] = bias
        if scale is not None:
            kw['scale'] = scale
        return self.S.add('act', lambda e: e.activation(out=out, in_=in_, func=func, **kw), reads, writes)

    def tt(self, eng, out, in0, in1, op, reads, writes):
        return self.S.add(eng, lambda e: e.tensor_tensor(out=out, in0=in0, in1=in1, op=op), reads, writes)

    def stt(self, out, in0, scalar, in1, op0, op1, reads, writes, eng='dve'):
        return self.S.add(eng, lambda e: e.scalar_tensor_tensor(out=out, in0=in0, scalar=scalar, in1=in1,
                                                                op0=op0, op1=op1), reads, writes)

    def ts(self, eng, out, in0, s1, s2, op0, op1, reads, writes):
        if s2 is None:
            return self.S.add(eng, lambda e: e.tensor_scalar(out=out, in0=in0, scalar1=s1, scalar2=None, op0=op0),
                              reads, writes)
        return self.S.add(eng, lambda e: e.tensor_scalar(out=out, in0=in0, scalar1=s1, scalar2=s2, op0=op0, op1=op1),
                          reads, writes)

    def copy(self, eng, out, in_, reads, writes):
        if eng == 'act':
            return self.S.add('act', lambda e: e.copy(out=out, in_=in_), reads, writes)
        return self.S.add(eng, lambda e: e.tensor_copy(out=out, in_=in_), reads, writes)

    def memset(self, eng, ap, val, writes):
        return self.S.add(eng, lambda e: e.memset(ap, val), (), writes)


NORM_EPS = 1e-6


def emit_rmsnorm(c, h_sb, hres, g_sb, gres, out_sb, outres, KC, NT, D, ones_bf, sq_bufs, rstd_sb):
    ps, psr = c.next_psum()
    for kc in range(KC):
        sq, sqr = sq_bufs[kc % len(sq_bufs)]
        c.act(sq[:, :NT], h_sb[:, kc, :], AF.Square, [(hres, kc)], [sqr])
        c.mm(ps[:, :NT], ones_bf[:, :], sq[:, :NT], kc == 0, kc == KC - 1, [sqr, 'ones'], [psr])
    c.act(rstd_sb[:, :NT], ps[:, :NT], AF.Ln, [psr], ['rstd'], bias=c.eps_ap, scale=1.0 / D)
    c.act(rstd_sb[:, :NT], rstd_sb[:, :NT], AF.Exp, ['rstd'], ['rstd'], scale=-0.5)
    for kc in range(KC):
        c.stt(out_sb[:, kc, :], h_sb[:, kc, :], g_sb[:, kc:kc + 1], rstd_sb[:, :NT], ALU.mult, ALU.mult,
              [(hres, kc), 'rstd', gres], [(outres, kc)])


def emit_ffn(c, n_sb, nres, h_sb, hres, hid_sb, wg_d, wu_d, wd_d, wres, KC, NT, DFF, wslots, wdslots, sg_bufs):
    NF = DFF // 128
    FS = 512 if DFF % 512 == 0 else 128
    NJ = FS // 128
    wgv = wg_d.rearrange("(kc p) f -> p kc f", p=128)
    wuv = wu_d.rearrange("(kc p) f -> p kc f", p=128)
    for fs in range(DFF // FS):
        (wg_sb, wgr), (wu_sb, wur) = wslots[fs % len(wslots)]
        c.dma('sp', wg_sb[:, :, :FS], wgv[:, :, fs * FS:(fs + 1) * FS], c.wres[wres[0]], [wgr], key=wgr)
        c.dma('sp', wu_sb[:, :, :FS], wuv[:, :, fs * FS:(fs + 1) * FS], c.wres[wres[1]], [wur], key=wur)
        for j in range(NJ):
            f = fs * NJ + j
            pg, pgr = c.next_psum()
            pu, pur = c.next_psum()
            for kc in range(KC):
                c.mm(pg[:, :NT], wg_sb[:, kc, j * 128:(j + 1) * 128], n_sb[:, kc, :], kc == 0, kc == KC - 1,
                     [wgr, (nres, kc)], [pgr])
            for kc in range(KC):
                c.mm(pu[:, :NT], wu_sb[:, kc, j * 128:(j + 1) * 128], n_sb[:, kc, :], kc == 0, kc == KC - 1,
                     [wur, (nres, kc)], [pur])
            sg, sgr = sg_bufs[f % len(sg_bufs)]
            c.act(sg[:, :NT], pg[:, :NT], AF.Silu, [pgr], [sgr])
            c.tt('dve', hid_sb[:, f, :], sg[:, :NT], pu[:, :NT], ALU.mult, [sgr, pur], [('hid', f)])
    NDC = min(4, KC)
    FG = 11 if NF % 11 == 0 else (4 if NF % 4 == 0 else 1)
    wdv = wd_d.rearrange("(f p) d -> p f d", p=128)
    li = 0
    for p0 in range(0, KC, NDC):
        accs = [c.next_psum() for _ in range(NDC)]
        for fg in range(NF // FG):
            wd_sb, wdr = wdslots[li % len(wdslots)]
            li += 1
            c.dma('sp', wd_sb[:, :FG, :NDC * 128], wdv[:, fg * FG:(fg + 1) * FG, p0 * 128:(p0 + NDC) * 128],
                  c.wres[wres[2]], [wdr], key=wdr)
            for fi in range(FG):
                f = fg * FG + fi
                for di in range(NDC):
                    ps, psr = accs[di]
                    c.mm(ps[:, :NT], wd_sb[:, fi, di * 128:(di + 1) * 128], hid_sb[:, f, :], f == 0, f == NF - 1,
                         [wdr, ('hid', f)], [psr])
        for di in range(NDC):
            ps, psr = accs[di]
            dc = p0 + di
            c.stt(h_sb[:, dc, :], ps[:, :NT], 0.5, h_sb[:, dc, :], ALU.mult, ALU.add, [psr, (hres, dc)], [(hres, dc)])


def cast_dram(c, dst, src, rows_per=256, sres=None, dres=None):
    R = src.shape[0]
    res = []
    for bi, r0 in enumerate(range(0, R, rows_per)):
        r1 = min(R, r0 + rows_per)
        c.dma('pool', dst[r0:r1, :], src[r0:r1, :], [], [(dres, bi)], key=dres + "_%d" % (bi % 4))
        res.append((dres, bi))
    c.wres[dres] = res
    return res


def build_k1(D, DFF, NFM, NAB, T, NT=512):
    c = Ctx()
    nc = c.nc
    KC = D // 128
    NF = DFF // 128
    hT = c.dram("hT", [D, T], F32, "ExternalInput")
    g1 = c.dram("g1", [128, KC], F32, "ExternalInput")
    g2 = c.dram("g2", [128, KC], F32, "ExternalInput")
    wg_b = c.dram("wg", [D, DFF], BF16, "ExternalInput")
    wu_b = c.dram("wu", [D, DFF], BF16, "ExternalInput")
    wd_b = c.dram("wd", [DFF, D], BF16, "ExternalInput")
    wfm_b = c.dram("wfm", [D, NFM], BF16, "ExternalInput")
    wab = c.dram("wab", [D, NAB], F32, "ExternalInput")
    h1T = c.dram("h1T", [D, T], F32, "ExternalOutput")
    PT = c.dram("PT", [NFM, T], BF16, "ExternalOutput")
    ab = c.dram("ab", [T, NAB], F32, "ExternalOutput")
    for n_ in ('wg_b', 'wu_b', 'wd_b', 'wfm_b'):
        c.wres[n_] = []
    c.init_psum(8)
    h_sb = c.sb("h_sb", [128, KC, NT], F32)
    n_sb = c.sb("n_sb", [128, KC, NT], BF16)
    hid_sb = c.sb("hid_sb", [128, NF, NT], BF16)
    g1_sb = c.sb("g1_sb", [128, KC], F32)
    g2_sb = c.sb("g2_sb", [128, KC], F32)
    ones_bf = c.sb("ones_bf", [128, 128], BF16)
    eps_sb = c.sb("eps_sb", [128, 1], F32)
    rstd_sb = c.sb("rstd_sb", [128, NT], F32)
    sq_bufs = [(c.sb("sq%d" % i, [128, NT], BF16), "sq%d" % i) for i in range(2)]
    sg_bufs = [(c.sb("sg%d" % i, [128, NT], F32), "sg%d" % i) for i in range(2)]
    wslots = [((c.sb("wga%d" % i, [128, KC, 512], BF16), "wga%d" % i),
               (c.sb("wua%d" % i, [128, KC, 512], BF16), "wua%d" % i)) for i in range(2)]
    wdslots = [(c.sb("wds%d" % i, [128, 11, 512], BF16), "wds%d" % i) for i in range(2)]
    wab_sb = c.sb("wab_sb", [128, KC, NAB], F32)
    wab_bf = c.sb("wab_bf", [128, KC, NAB], BF16)
    pt_bufs = [(c.sb("pt%d" % i, [128, 4, NT], BF16), "pt%d" % i) for i in range(2)]
    ab_bufs = [(c.sb("abs%d" % i, [128, NAB], F32), "abs%d" % i) for i in range(2)]
    c.eps_ap = eps_sb[:, 0:1]

    c.memset('pool', ones_bf[:, :], 1.0, ['ones'])
    c.memset('pool', eps_sb[:, :], NORM_EPS, ['eps'])
    c.dma('sp', g1_sb[:, :], g1[:, :], [], ['g1'], key='g1')
    c.dma('sp', g2_sb[:, :], g2[:, :], [], ['g2'], key='g2')
    c.dma('sp', wab_sb[:, :, :], wab.rearrange("(kc p) n -> p kc n", p=128), [], ['wab'], key='wab')
    c.copy('dve', wab_bf[:, :, :], wab_sb[:, :, :], ['wab'], ['wabb'])

    hv = hT.rearrange("(kc p) t -> p kc t", p=128)
    h1v = h1T.rearrange("(kc p) t -> p kc t", p=128)
    PTv = PT.rearrange("(cc p) t -> p cc t", p=128)
    wfv = wfm_b.rearrange("(kc p) f -> p kc f", p=128)
    NCH = NFM // 128
    CS = 4
    for tt in range(T // NT):
        tsl = slice(tt * NT, (tt + 1) * NT)
        c.dma('sp', h_sb[:, :, :], hv[:, :, tsl], [], [('h', kc) for kc in range(KC)], key='hload')
        emit_rmsnorm(c, h_sb, 'h', g1_sb, 'g1', n_sb, 'n', KC, NT, D, ones_bf, sq_bufs, rstd_sb)
        emit_ffn(c, n_sb, 'n', h_sb, 'h', hid_sb, wg_b, wu_b, wd_b, ('wg_b', 'wu_b', 'wd_b'), KC, NT, DFF,
                 wslots, wdslots, sg_bufs)
        c.dma('sp', h1v[:, :, tsl], h_sb[:, :, :], [('h', kc) for kc in range(KC)], ['h1T'], key='hstore')
        emit_rmsnorm(c, h_sb, 'h', g2_sb, 'g2', n_sb, 'n', KC, NT, D, ones_bf, sq_bufs, rstd_sb)
        si = 0
        for cs in range(0, NCH, CS):
            ncs = min(CS, NCH - cs)
            (w_sb, wr), _ = wslots[si % len(wslots)]
            pt_sb, ptr = pt_bufs[si % len(pt_bufs)]
            si += 1
            c.dma('sp', w_sb[:, :, :ncs * 128], wfv[:, :, cs * 128:(cs + ncs) * 128], c.wres['wfm_b'], [wr], key=wr)
            for j in range(ncs):
                ps, psr = c.next_psum()
                for kc in range(KC):
                    c.mm(ps[:, :NT], w_sb[:, kc, j * 128:(j + 1) * 128], n_sb[:, kc, :], kc == 0, kc == KC - 1,
                         [wr, ('n', kc)], [psr])
                if j % 2 == 0:
                    c.copy('act', pt_sb[:, j, :], ps[:, :NT], [psr], [(ptr, j)])
                else:
                    c.copy('dve', pt_sb[:, j, :], ps[:, :NT], [psr], [(ptr, j)])
            c.dma('act', PTv[:, cs:cs + ncs, tsl], pt_sb[:, :ncs, :], [(ptr, j) for j in range(ncs)], ['PT'],
                  key='ptstore%d' % (si % 2))
        for tk in range(NT // 128):
            ps, psr = c.next_psum()
            for kc in range(KC):
                c.mm(ps[:, :NAB], n_sb[:, kc, tk * 128:(tk + 1) * 128], wab_bf[:, kc, :], kc == 0, kc == KC - 1,
                     ['wabb', ('n', kc)], [psr])
            ab_sb, abr = ab_bufs[tk % 2]
            c.copy('dve', ab_sb[:, :], ps[:, :NAB], [psr], [abr])
            c.dma('act', ab[tt * NT + tk * 128: tt * NT + (tk + 1) * 128, :], ab_sb[:, :], [abr], ['ab'],
                  key='abstore%d' % (tk % 2))
    info = c.S.emit(final_wait_keys=['hstore', 'ptstore0', 'ptstore1', 'abstore0', 'abstore1'])
    c.es.close()
    return nc, info


NEG = -30000.0
import os as _os
DBG_STOP = int(_os.environ.get('K2_STOP', '0'))
INV_F32R = int(_os.environ.get('K2_F32R', '1'))


def r32(a):
    return a.bitcast(mybir.dt.float32r) if INV_F32R else a
CST_NAMES = ['ident', 'ones', 'ltri', 'bd', 'c0', 'c1', 'ms', 'msT', 'miT', 'm_s', 'mT_s', 'mT_i']


def k2_consts():
    i = np.arange(128)
    same = (i[:, None] // 64) == (i[None, :] // 64)
    P, Fr = i[:, None], i[None, :]
    t = {}
    t['ident'] = np.eye(128)
    t['ones'] = np.ones((128, 128))
    t['ltri'] = same & (P <= Fr)
    t['bd'] = same
    t['c0'] = np.broadcast_to(P < 64, (128, 128))
    t['c1'] = np.broadcast_to(P >= 64, (128, 128))
    t['ms'] = np.where(same & (P > Fr), 0.0, NEG)
    t['msT'] = np.where(same & (Fr > P), 0.0, NEG)
    t['miT'] = np.where(same & (Fr >= P), 0.0, NEG)
    t['m_s'] = same & (Fr < P)
    t['mT_s'] = same & (P < Fr)
    t['mT_i'] = same & (P <= Fr)
    return np.ascontiguousarray(np.concatenate([np.asarray(t[n], np.float32) for n in CST_NAMES], axis=1))


class K2:
    def __init__(self, T2, TT=256, do_gdn=True, do_rwkv=True):
        self.T2, self.TT = T2, TT
        self.NS = TT // 128
        c = self.c = Ctx()
        self.do_gdn, self.do_rwkv = do_gdn, do_rwkv
        NS = self.NS
        self.cst_d = c.dram("cst", [128, 128 * len(CST_NAMES)], F32, "ExternalInput")
        c.init_psum(8)
        cst = self.cst = c.sb("cst_sb", [128, 128 * len(CST_NAMES)], F32)
        c.dma('sp', cst[:, :], self.cst_d[:, :], [], ['cst'], key='cst')
        self.cf = {n: cst[:, i * 128:(i + 1) * 128] for i, n in enumerate(CST_NAMES)}
        self.ident_bf = c.sb("ident_bf", [128, 128], BF16)
        self.ones_bf = c.sb("ones_bf", [128, 128], BF16)
        c.copy('dve', self.ident_bf[:, :], self.cf['ident'], ['cst'], ['cstb'])
        c.copy('dve', self.ones_bf[:, :], self.cf['ones'], ['cst'], ['cstb'])
        self.eps_sb = c.sb("eps_sb", [128, 1], F32)
        self.one_sb = c.sb("one_sb", [128, 1], F32)
        c.memset('pool', self.eps_sb[:, :], 1e-6, ['eps'])
        c.memset('pool', self.one_sb[:, :], 1.0, ['one'])
        self.final_keys = []
        NCH = NS * 6
        self.invP = [[c.sb("invP%d_%d" % (ch, p), [128, 128], F32) for p in range(2)] for ch in range(NCH)]
        self.invQ = [[c.sb("invQ%d_%d" % (ch, p), [128, 128], F32) for p in range(2)] for ch in range(NCH)]
        self.invT = [[c.sb("invT%d_%d" % (ch, p), [128, 128], F32) for p in range(2)] for ch in range(NCH)]
        if do_gdn:
            self.init_gdn()
        if do_rwkv:
            self.init_rwkv()
        for tt in range(T2 // TT):
            gens = []
            if do_gdn:
                gens.append(self.gdn_tile(tt))
            if do_rwkv:
                gens.append(self.rwkv_tile(tt))
            for g in gens:
                assert next(g) == 'prep'
            reqs = []
            for g in gens:
                r = next(g)
                assert r[0] == 'inv'
                reqs += r[1]
            self.emit_inverse(reqs)
            alive = list(gens)
            while alive:
                steps = []
                for g in list(alive):
                    try:
                        r = next(g)
                        if isinstance(r, tuple) and r[0] == 'seq':
                            steps += r[1]
                    except StopIteration:
                        alive.remove(g)
                while steps:
                    for st in list(steps):
                        try:
                            next(st)
                        except StopIteration:
                            steps.remove(st)
        self.info = c.S.emit(final_wait_keys=self.final_keys)
        c.es.close()

    def tr_batch(self, items):
        c = self.c
        for i0 in range(0, len(items), 4):
            ps, res = c.next_psum()
            grp = items[i0:i0 + 4]
            for i, (in_ap, rd, ev) in enumerate(grp):
                c.mm(ps[:, i * 128:(i + 1) * 128], in_ap, self.ident_bf[:, :], True, True, list(rd) + ['cstb'], [res])
            for i, (in_ap, rd, ev) in enumerate(grp):
                ev(ps[:, i * 128:(i + 1) * 128], res)

    def init_gdn(self):
        c, NS, TT, T2 = self.c, self.NS, self.TT, self.T2
        HA = self.HA = 2
        self.xg = c.dram("xg", [6 * 128, T2], BF16, "ExternalInput")
        self.abd = c.dram("abd", [T2, 4], F32, "ExternalInput")
        self.gpar_d = c.dram("gpar", [128, 24 + 4 + 128], F32, "ExternalInput")
        self.oaT = c.dram("oaT", [HA * 128, T2], BF16, "ExternalOutput")
        gp = self.gpar = c.sb("gpar_sb", [128, 156], F32)
        c.dma('sp', gp[:, :], self.gpar_d[:, :], [], ['gpar'], key='gpar')
        self.cw = lambda ci, j: gp[:, ci * 4 + j: ci * 4 + j + 1]
        self.gain_rep = gp[:, 28:156]
        self.dtb4 = c.sb("dtb4", [128, NS, 2], F32)
        self.nea4 = c.sb("nea4", [128, NS, 2], F32)
        nea = c.sb("nea", [128, 2], F32)
        c.act(nea[:, :], gp[:, 24:26], AF.Exp, ['gpar'], ['nea'])
        for s in range(NS):
            c.copy('dve', self.dtb4[:, s, :], gp[:, 26:28], ['gpar'], [('dtb4', s)])
            c.ts('dve', self.nea4[:, s, :], nea[:, :], -1.0, None, ALU.mult, None, ['nea'], [('nea4', s)])
        self.graw = [[c.sb("graw%d_%d" % (p, ci), [128, 3 + TT], BF16) for ci in range(6)] for p in range(2)]
        self.gacc = [c.sb("gacc%d" % i, [128, TT], F32) for i in range(2)]
        self.gxc = [c.sb("gxc%d" % ci, [128, TT], BF16) for ci in range(6)]
        self.gsq = [c.sb("gsq%d" % i, [128, TT], BF16) for i in range(2)]
        self.abt = c.sb("abt", [128, NS, 4], F32)
        self.lnr = c.sb("lnr", [128, NS, 4], F32)
        names = ['gx', 'ge', 'gg', 'glnb', 'gc', 'glrow', 'ga1', 'gb1', 'ga2', 'gns1', 'gs2', 'gs3', 'gbeta',
                 'gl0', 'gl1']
        self.gcol = {n: c.sb(n, [128, NS, 2], F32) for n in names}
        self.k_tok = [[c.sb("ktok%d_%d" % (s, h), [128, 128], BF16) for h in range(2)] for s in range(NS)]
        self.vb_tok = [[c.sb("vbtok%d_%d" % (s, h), [128, 128], F32) for h in range(2)] for s in range(NS)]
        self.AqkT = [[c.sb("aqkT%d_%d" % (s, h), [128, 128], BF16) for h in range(2)] for s in range(NS)]
        self.TTb = [[c.sb("TTb%d_%d" % (s, h), [128, 128], BF16) for h in range(2)] for s in range(NS)]
        self.o_tok = [[c.sb("otok%d_%d" % (s, h), [128, 128], F32) for h in range(2)] for s in range(NS)]
        self.Emat = [c.sb("emat%d" % i, [128, 128], F32) for i in range(3)]
        self.S_f = [c.sb("S_f%d" % h, [128, 128], F32) for h in range(2)]
        self.S_b = [c.sb("S_b%d" % h, [128, 128], BF16) for h in range(2)]
        for h in range(2):
            c.memset('pool', self.S_f[h][:, :], 0.0, [('S_f', h)])
            c.memset('pool', self.S_b[h][:, :], 0.0, [('S_b', h)])
        self.R_bf = [c.sb("R_bf%d" % h, [128, 128], BF16) for h in range(2)]
        self.x2s = [c.sb("x2s%d" % h, [128, 128], F32) for h in range(2)]
        self.vn = [c.sb("vn%d" % h, [128, 128], BF16) for h in range(2)]
        self.vs = [c.sb("vs%d" % h, [128, 128], BF16) for h in range(2)]
        self.on_bf = [c.sb("on_bf%d" % i, [128, 128], BF16) for i in range(NS)]
        self.ojunk = c.sb("ojunk", [128, 128], F32)
        self.oss = c.sb("oss", [128, 2], F32)
        self.ostage = [[c.sb("ostage%d_%d" % (p, h), [128, TT], BF16) for h in range(2)] for p in range(2)]
        self.final_keys += ['oast0', 'oast1']

    def gdn_tile(self, tt):
        c, NS, TT = self.c, self.NS, self.TT
        cf = self.cf
        par = tt % 2
        t0 = tt * TT
        G = self.gcol
        for ci in range(6):
            raw = self.graw[par][ci]
            rr = ('graw', par, ci)
            if tt == 0:
                c.memset('pool', raw[:, 0:3], 0.0, [rr])
                c.dma('sp', raw[:, 3:3 + TT], self.xg[ci * 128:(ci + 1) * 128, 0:TT], [], [rr], key='graw%d_%d' % (par, ci))
            else:
                c.dma('sp', raw[:, :], self.xg[ci * 128:(ci + 1) * 128, t0 - 3:t0 + TT], [], [rr],
                      key='graw%d_%d' % (par, ci))
            acc = self.gacc[ci % 2]
            ar = ('gacc', ci % 2)
            c.ts('dve', acc[:, :], raw[:, 3:3 + TT], self.cw(ci, 3), None, ALU.mult, None, [rr, 'gpar'], [ar])
            for j in (2, 1, 0):
                c.stt(acc[:, :], raw[:, j:j + TT], self.cw(ci, j), acc[:, :], ALU.mult, ALU.add, [rr, 'gpar', ar], [ar])
            c.act(self.gxc[ci][:, :], acc[:, :], AF.Silu, [ar], [('gxc', ci)])
        if DBG_STOP == 1:
            return
        ps_ss, ps_ssr = c.next_psum()
        for ci in range(4):
            sq = self.gsq[ci % 2]
            sr = ('gsq', ci % 2)
            c.act(sq[:, :], self.gxc[ci][:, :], AF.Square, [('gxc', ci)], [sr])
            for s in range(NS):
                c.mm(ps_ss[:, s * 4 + ci: s * 4 + ci + 1], sq[:, s * 128:(s + 1) * 128], self.ones_bf[:, 0:1], True, True,
                     [sr, 'cstb'], [ps_ssr])
        lnr = self.lnr
        c.act(lnr[:, :, :].rearrange("p s c -> p (s c)"), ps_ss[:, 0:NS * 4], AF.Ln, [ps_ssr, 'eps'], ['lnr'],
              bias=self.eps_sb[:, 0:1])
        c.ts('dve', lnr[:, :, :], lnr[:, :, :], -0.5, None, ALU.mult, None, ['lnr'], ['lnr'])
        if DBG_STOP == 2:
            return
        abt = self.abt
        c.dma('sp', abt[:, :, :], self.abd[t0:t0 + TT, :].rearrange("(s p) c -> p s c", p=128), [], ['abt'], key='abt')
        c.tt('dve', G['gx'][:, :, :], abt[:, :, 0:2], self.dtb4[:, :, :], ALU.add, ['abt'] + [('dtb4', s) for s in range(NS)], ['gx'])
        c.act(G['ge'][:, :, :], G['gx'][:, :, :], AF.Exp, ['gx'], ['ge'])
        c.act(G['ge'][:, :, :], G['ge'][:, :, :], AF.Ln, ['ge', 'one'], ['ge'], bias=self.one_sb[:, 0:1])
        c.tt('dve', G['gg'][:, :, :], G['ge'][:, :, :], self.nea4[:, :, :], ALU.mult, ['ge'] + [('nea4', s) for s in range(NS)], ['gg'])
        c.act(G['glnb'][:, :, :], abt[:, :, 2:4], AF.Exp, ['abt'], ['glnb'], scale=-1.0)
        c.act(G['glnb'][:, :, :], G['glnb'][:, :, :], AF.Ln, ['glnb', 'one'], ['glnb'], bias=self.one_sb[:, 0:1])
        ggf = G['gg'][:, :, :].rearrange("p s c -> p (s c)")
        NC2 = NS * 2
        ps_g, ps_gr = c.next_psum()
        c.mm(ps_g[:, 0:NC2], cf['ltri'], ggf, True, True, ['cst', 'gg'], [ps_gr])
        c.mm(ps_g[:, 16:16 + NC2], cf['bd'], ggf, True, True, ['cst', 'gg'], [ps_gr])
        c.mm(ps_g[:, 32:32 + NC2], cf['c0'], ggf, True, True, ['cst', 'gg'], [ps_gr])
        c.mm(ps_g[:, 48:48 + NC2], cf['c1'], ggf, True, True, ['cst', 'gg'], [ps_gr])
        fl = lambda n: G[n][:, :, :].rearrange("p s c -> p (s c)")
        c.copy('dve', fl('gc'), ps_g[:, 0:NC2], [ps_gr], ['gc'])
        c.copy('dve', fl('glrow'), ps_g[:, 16:16 + NC2], [ps_gr], ['glrow'])
        c.act(fl('gl0'), ps_g[:, 32:32 + NC2], AF.Exp, [ps_gr], ['gl0'])
        c.act(fl('gl1'), ps_g[:, 48:48 + NC2], AF.Exp, [ps_gr], ['gl1'])
        lnrq, lnrk = lnr[:, :, 0:2], lnr[:, :, 2:4]
        A3 = lambda n: G[n][:, :, :]
        c.tt('dve', A3('ga1'), A3('gc'), A3('glnb'), ALU.subtract, ['gc', 'glnb'], ['ga1'])
        c.tt('dve', A3('ga1'), A3('ga1'), lnrk, ALU.add, ['ga1', 'lnr'], ['ga1'])
        c.tt('dve', A3('gb1'), lnrk, A3('gc'), ALU.subtract, ['gc', 'lnr'], ['gb1'])
        c.stt(A3('ga2'), A3('gc'), float(np.log(128.0 ** -0.5)), lnrq, ALU.add, ALU.add, ['gc', 'lnr'], ['ga2'])
        c.act(A3('gns1'), A3('ga1'), AF.Exp, ['ga1'], ['gns1'])
        c.ts('dve', A3('gns1'), A3('gns1'), -1.0, None, ALU.mult, None, ['gns1'], ['gns1'])
        c.act(A3('gs2'), A3('ga2'), AF.Exp, ['ga2'], ['gs2'])
        c.tt('dve', A3('gs3'), A3('gb1'), A3('glrow'), ALU.add, ['gb1', 'glrow'], ['gs3'])
        c.act(A3('gs3'), A3('gs3'), AF.Exp, ['gs3'], ['gs3'])
        c.act(A3('gbeta'), A3('glnb'), AF.Exp, ['glnb'], ['gbeta'], scale=-1.0)
        if DBG_STOP == 3:
            return
        items = []
        for s in range(NS):
            for h in range(2):
                items.append((self.gxc[2 + h][:, s * 128:(s + 1) * 128], [('gxc', 2 + h)],
                              lambda pt, res, s=s, h=h: c.copy('act', self.k_tok[s][h][:, :], pt, [res], [('ktok', s, h)])))
                items.append((self.gxc[4 + h][:, s * 128:(s + 1) * 128], [('gxc', 4 + h)],
                              lambda pt, res, s=s, h=h: c.ts('dve', self.vb_tok[s][h][:, :], pt, G['gbeta'][:, s, h:h + 1], None,
                                                              ALU.mult, None, [res, 'gbeta'], [('vbtok', s, h)])))
        self.tr_batch(items)
        if DBG_STOP == 4:
            return
        yield 'prep'
        for s in range(NS):
            for h in range(2):
                ch = s * 2 + h
                ssl = slice(s * 128, (s + 1) * 128)
                kc_, qc_ = self.gxc[2 + h], self.gxc[h]
                psK, psKr = c.next_psum()
                c.mm(psK[:, 0:128], kc_[:, ssl], kc_[:, ssl], True, True, [('gxc', 2 + h)], [psKr])
                c.mm(psK[:, 128:256], kc_[:, ssl], qc_[:, ssl], True, True, [('gxc', 2 + h), ('gxc', h)], [psKr])
                cols = [G['gb1'][:, s, h:h + 1], G['ga1'][:, s, h:h + 1], G['ga2'][:, s, h:h + 1]]
                coln = ['gb1', 'ga1', 'ga2']
                psE, psEr = c.next_psum()
                masks = ['ms', 'msT', 'miT']
                for i3 in range(3):
                    o_ = psE[:, i3 * 128:(i3 + 1) * 128]
                    c.mm(o_, cols[i3].to_broadcast([128, 128]), cf['ident'], True, False, ['cst', coln[i3]], [psEr])
                    c.mm(o_, cf['ident'], cf[masks[i3]], False, True, ['cst'], [psEr])
                biases = [cols[1], cols[0], cols[0]]
                biasn = ['ga1', 'gb1', 'gb1']
                for i3 in range(3):
                    c.act(self.Emat[i3][:, :], psE[:, i3 * 128:(i3 + 1) * 128], AF.Exp, [psEr, biasn[i3]], [('emat', i3)],
                          bias=biases[i3])
                P0, Q0, T0 = self.invP[ch][0], self.invQ[ch][0], self.invT[ch][0]
                c.tt('dve', r32(P0[:, :]), self.Emat[0][:, :], psK[:, 0:128], ALU.mult, [('emat', 0), psKr], [('invP', ch, 0)])
                c.tt('dve', r32(Q0[:, :]), self.Emat[1][:, :], psK[:, 0:128], ALU.mult, [('emat', 1), psKr], [('invQ', ch, 0)])
                c.tt('dve', self.AqkT[s][h][:, :], self.Emat[2][:, :], psK[:, 128:256], ALU.mult, [('emat', 2), psKr],
                     [('aqkT', s, h)])
                c.stt(r32(T0[:, :]), Q0[:, :], -1.0, cf['ident'], ALU.mult, ALU.add, [('invQ', ch, 0), 'cst'], [('invT', ch, 0)],
                      eng='dve')
        if DBG_STOP == 5:
            return
        yield ('inv', [(s * 2 + h, s, h, self.TTb, 'TTb') for s in range(NS) for h in range(2)])
        if DBG_STOP == 6:
            return
        def gstep(ck, h):
            s, half = ck // 2, ck % 2
            rows = slice(half * 64, half * 64 + 64)
            tcols = slice(s * 128 + half * 64, s * 128 + half * 64 + 64)
            psX, psXr = c.next_psum()
            c.mm(psX[rows, 0:128], self.gxc[2 + h][:, tcols], self.S_b[h][:, :], True, True, [('gxc', 2 + h), ('S_b', h)], [psXr])
            c.mm(psX[rows, 128:256], self.gxc[h][:, tcols], self.S_b[h][:, :], True, True, [('gxc', h), ('S_b', h)], [psXr])
            yield
            c.stt(self.R_bf[h][rows, :], psX[rows, 0:128], G['gns1'][rows, s, h:h + 1], self.vb_tok[s][h][rows, :],
                  ALU.mult, ALU.add, [psXr, 'gns1', ('vbtok', s, h)], [('R_bf', h)])
            c.S.add('dve', lambda e: e.tensor_scalar(out=self.x2s[h][rows, :], in0=psX[rows, 128:256],
                                                     scalar1=G['gs2'][rows, s, h:h + 1], scalar2=None, op0=ALU.mult),
                    [psXr, 'gs2'], [('x2s', h)])
            yield
            psV, psVr = c.next_psum()
            c.mm(psV[rows, 0:128], self.TTb[s][h][rows, half * 64:half * 64 + 64], self.R_bf[h][rows, :], True, True,
                 [('TTb', s, h), ('R_bf', h)], [psVr])
            yield
            c.copy('act', self.vn[h][rows, :], psV[rows, 0:128], [psVr], [('vn', h)])
            c.act(self.vs[h][rows, :], psV[rows, 0:128], AF.Identity, [psVr, 'gs3'], [('vs', h)],
                  scale=G['gs3'][rows, s, h:h + 1])
            yield
            psS, psSr = c.next_psum()
            c.mm(psS[:, 0:128], self.k_tok[s][h][rows, :], self.vs[h][rows, :], True, True,
                 [('ktok', s, h), ('vs', h)], [psSr])
            c.mm(psV[rows, 128:256], self.AqkT[s][h][rows, half * 64:half * 64 + 64], self.vn[h][rows, :], True, True,
                 [('aqkT', s, h), ('vn', h)], [psVr])
            yield
            gl = G['gl0'] if half == 0 else G['gl1']
            c.stt(self.S_f[h][:, :], self.S_f[h][:, :], gl[:, s, h:h + 1], psS[:, 0:128], ALU.mult, ALU.add,
                  [('S_f', h), 'gl0', 'gl1', psSr], [('S_f', h)])
            c.copy('act', self.S_b[h][:, :], self.S_f[h][:, :], [('S_f', h)], [('S_b', h)])
            c.tt('dve', self.o_tok[s][h][rows, :], psV[rows, 128:256], self.x2s[h][rows, :], ALU.add,
                 [psVr, ('x2s', h)], [('otok', s, h)])

        for ck in range(2 * NS):
            yield ('seq', [gstep(ck, 0), gstep(ck, 1)])
        if DBG_STOP == 7:
            return
        for h in range(2):
            st = self.ostage[par][h]
            items = []
            for s in range(NS):
                o = self.o_tok[s][h]
                c.S.add('act', lambda e, o=o, h=h: e.activation(out=self.ojunk[:, :], in_=o[:, :], func=AF.Square,
                                                                 accum_out=self.oss[:, h:h + 1]),
                        [('otok', s, h)], ['ojunk', ('oss', h)])
                c.act(self.oss[:, h:h + 1], self.oss[:, h:h + 1], AF.Ln, [('oss', h), 'eps'], [('oss', h)],
                      bias=self.eps_sb[:, 0:1], scale=1.0 / 128)
                c.act(self.oss[:, h:h + 1], self.oss[:, h:h + 1], AF.Exp, [('oss', h)], [('oss', h)], scale=-0.5)
                onb = self.on_bf[s]
                onr = ('on_bf', s)
                c.stt(onb[:, :], o[:, :], self.oss[:, h:h + 1], self.gain_rep, ALU.mult, ALU.mult,
                      [('otok', s, h), ('oss', h), 'gpar'], [onr])
                items.append((onb[:, :], [onr],
                              lambda pt, res, s=s, st=st, h=h: c.copy('act', st[:, s * 128:(s + 1) * 128], pt, [res],
                                                                      [('ostage', par, h)])))
            self.tr_batch(items)
            c.dma('sp', self.oaT[h * 128:(h + 1) * 128, tt * TT:(tt + 1) * TT], st[:, :], [('ostage', par, h)], ['oaT'],
                  key='oast%d' % par)

    def emit_inverse(self, reqs):
        c = self.c
        R_ = r32
        GRP = 6
        for g0 in range(0, len(reqs), GRP):
            grp = reqs[g0:g0 + GRP]
            cur = 0
            for lev in range(1, 6):
                nxt = 1 - cur
                banks = [c.next_psum() for _ in grp]
                for (ch, s, h, outs, outname), (ps, psr) in zip(grp, banks):
                    Pc, Qc = self.invP[ch][cur], self.invQ[ch][cur]
                    c.mm(ps[:, 0:128], R_(Qc[:, :]), R_(Pc[:, :]), True, True, [('invQ', ch, cur), ('invP', ch, cur)], [psr])
                    if lev < 5:
                        c.mm(ps[:, 128:256], R_(Pc[:, :]), R_(Qc[:, :]), True, True, [('invQ', ch, cur), ('invP', ch, cur)], [psr])
                for i, ((ch, s, h, outs, outname), (ps, psr)) in enumerate(zip(grp, banks)):
                    Pn, Qn = self.invP[ch][nxt], self.invQ[ch][nxt]
                    e1 = 'act' if i % 2 == 0 else 'dve'
                    if lev < 5:
                        c.copy(e1, r32(Pn[:, :]), ps[:, 0:128], [psr], [('invP', ch, nxt)])
                        c.copy(e1, r32(Qn[:, :]), ps[:, 128:256], [psr], [('invQ', ch, nxt)])
                    else:
                        c.copy(e1, r32(Pn[:, :]), ps[:, 0:128], [psr], [('invP', ch, nxt)])
                for (ch, s, h, outs, outname), (ps, psr) in zip(grp, banks):
                    Pn, Tc = self.invP[ch][nxt], self.invT[ch][cur]
                    c.mm(ps[:, 256:384], R_(Pn[:, :]), R_(Tc[:, :]), True, True, [('invP', ch, nxt), ('invT', ch, cur)], [psr])
                for (ch, s, h, outs, outname), (ps, psr) in zip(grp, banks):
                    Tc, Tn = self.invT[ch][cur], self.invT[ch][nxt]
                    if lev < 5:
                        c.tt('dve', r32(Tn[:, :]), ps[:, 256:384], Tc[:, :], ALU.add, [psr, ('invT', ch, cur)], [('invT', ch, nxt)])
                    else:
                        c.tt('dve', outs[s][h][:, :], ps[:, 256:384], Tc[:, :], ALU.add, [psr, ('invT', ch, cur)],
                             [(outname, s, h)])
                cur = nxt


GN_EPS = 64e-5
DECAY_C = -float(np.exp(-0.5))


def _init_rwkv(self):
    c, NS, TT, T2 = self.c, self.NS, self.TT, self.T2
    cf = self.cf
    self.xr = c.dram("xr", [10 * 128, T2], BF16, "ExternalInput")
    self.rpar_d = c.dram("rpar", [128, 20], F32, "ExternalInput")
    self.rrep_d = c.dram("rrep", [128, 512], F32, "ExternalInput")
    self.rlw_d = c.dram("rlw", [128, 4, 256], F32, "ExternalInput")
    self.obT = c.dram("obT", [256, T2], BF16, "ExternalOutput")
    rp = self.rpar = c.sb("rpar_sb", [128, 20], F32)
    c.dma('sp', rp[:, :], self.rpar_d[:, :], [], ['rpar'], key='rpar')
    self.rrep = c.sb("rrep_sb", [128, 512], F32)
    c.dma('sp', self.rrep[:, :], self.rrep_d[:, :], [], ['rrep'], key='rrep')
    rlw_f = c.sb("rlw_f", [128, 4, 256], F32)
    c.dma('sp', rlw_f[:, :, :], self.rlw_d[:, :, :], [], ['rlw_f'], key='rlw_f')
    self.rlw = c.sb("rlw_b", [128, 4, 256], BF16)
    c.copy('dve', self.rlw[:, :, :], rlw_f[:, :, :], ['rlw_f'], ['rlw'])
    self.omka = c.sb("omka", [128, 2], F32)
    c.ts('dve', self.omka[:, :], rp[:, 16:18], -1.0, 1.0, ALU.mult, ALU.add, ['rpar'], ['omka'])
    self.bd_bf = c.sb("bd_bf", [128, 128], BF16)
    c.copy('dve', self.bd_bf[:, :], cf['bd'], ['cst'], ['cstb'])
    self.hsel = c.sb("hsel", [128, 2], BF16)
    c.copy('dve', self.hsel[:, 0:1], cf['c0'][:, 0:1], ['cst'], ['cstb'])
    c.copy('dve', self.hsel[:, 1:2], cf['c1'][:, 0:1], ['cst'], ['cstb'])
    self.gneps = c.sb("gneps", [128, 1], F32)
    c.memset('pool', self.gneps[:, :], GN_EPS, ['gneps'])
    self.rmask = c.sb("rmask", [128, TT], F32)
    c.memset('pool', self.rmask[:, :], 1.0, ['rmask'])
    for k in range(TT // 64):
        c.memset('pool', self.rmask[:, k * 64:k * 64 + 1], 0.0, ['rmask'])
    self.rraw = [c.sb("rraw%d" % j, [128, 1 + TT], BF16) for j in range(10)]
    self.rd = [c.sb("rd%d" % i, [128, TT], F32) for i in range(2)]
    self.xs = [c.sb("xs%d" % j, [128, TT], F32) for j in range(6)]
    self.lora_b = [c.sb("lorab%d" % j, [128, TT], BF16) for j in range(4)]
    self.lw = [c.sb("lw%d" % i, [128, TT], F32) for i in range(2)]
    self.av = [c.sb("av%d" % i, [128, TT], F32) for i in range(2)]
    self.g_tok = [c.sb("gtok%d" % s, [128, 256], F32) for s in range(NS)]
    tn = ['kkr', 'rkk', 'kk', 'tq', 'kp', 'ka', 'cs', 'Em', 'Ex', 'ktf', 'akf']
    self.rt = {n: c.sb("rt_" + n, [128, TT], F32) for n in tn}
    self.rsq = c.sb("rsq", [128, TT], BF16)
    self.Ep = [c.sb("Ep%d" % i, [128, TT], F32) for i in range(2)]
    self.br = [c.sb("br%d" % i, [128, NS, 2, 128], BF16) for i in range(2)]
    self.akT = [c.sb("akT%d" % i, [128, TT], BF16) for i in range(2)]
    self.ktT = [c.sb("ktT%d" % i, [128, TT], BF16) for i in range(2)]
    self.fm3 = [[c.sb("fm3_%d_%d" % (q, i), [128, TT], BF16) for i in range(2)] for q in range(3)]
    self.prod = c.sb("prod", [128, TT], BF16)
    self.tok3 = [[c.sb("tok3_%d_%d" % (q, s), [128, 256], BF16) for s in range(NS)] for q in range(3)]
    self.coef = [c.sb("coef%d" % s, [128, 4], F32) for s in range(NS)]
    self.AraT = [[c.sb("AraT%d_%d" % (s, h), [128, 128], BF16) for h in range(4)] for s in range(NS)]
    self.AbrkT = [[c.sb("AbrkT%d_%d" % (s, h), [128, 256], BF16) for h in range(4)] for s in range(NS)]
    self.TTr = [[c.sb("TTr%d_%d" % (s, h), [128, 128], BF16) for h in range(4)] for s in range(NS)]
    self.ZV = [c.sb("ZV%d" % s, [128, 256], F32) for s in range(NS)]
    self.YV = [c.sb("YV%d" % s, [128, 256], F32) for s in range(NS)]
    self.y_tok = [c.sb("ytok%d" % s, [128, 256], F32) for s in range(NS)]
    self.H_f = [c.sb("H_f%d" % i, [128, 128], F32) for i in range(2)]
    self.H_b = [c.sb("H_b%d" % i, [128, 128], BF16) for i in range(2)]
    for i in range(2):
        c.memset('pool', self.H_f[i][:, :], 0.0, [('H_f', i)])
        c.memset('pool', self.H_b[i][:, :], 0.0, [('H_b', i)])
    self.Z_bf = [c.sb("Z_bf%d" % i, [128, 128], BF16) for i in range(2)]
    self.U_bf = [c.sb("U_bf%d" % i, [128, 128], BF16) for i in range(2)]
    self.ysq = c.sb("ysq", [128, 256], F32)
    self.yn = c.sb("yn", [128, 256], F32)
    self.yfin = c.sb("yfin", [128, 256], BF16)
    self.gst = {n: c.sb("gst_" + n, [128, 4], F32) for n in ['s1', 's2', 'mean', 'msq', 'var']}
    self.obstage = [c.sb("obstage%d" % i, [128, TT], BF16) for i in range(2)]
    self.final_keys += ['obst']


def _rwkv_tile(self, tt):
    c, NS, TT = self.c, self.NS, self.TT
    cf = self.cf
    t0 = tt * TT
    rp = self.rpar
    RT = self.rt
    v3 = lambda ap: ap.rearrange("p (s t) -> p s t", t=128)
    for j in range(10):
        raw = self.rraw[j]
        rr = ('rraw', j)
        if tt == 0:
            c.memset('pool', raw[:, 0:1], 0.0, [rr])
            c.dma('sp', raw[:, 1:1 + TT], self.xr[j * 128:(j + 1) * 128, 0:TT], [], [rr], key='rraw%d' % j)
        else:
            c.dma('sp', raw[:, :], self.xr[j * 128:(j + 1) * 128, t0 - 1:t0 + TT], [], [rr], key='rraw%d' % j)
        d = self.rd[j % 2]
        dr = ('rd', j % 2)
        c.tt('pool', d[:, :], raw[:, 0:TT], raw[:, 1:1 + TT], ALU.subtract, [rr], [dr])
        if j < 6:
            c.stt(self.xs[j][:, :], d[:, :], rp[:, j:j + 1], raw[:, 1:1 + TT], ALU.mult, ALU.add, [dr, rr, 'rpar'], [('xs', j)])
        else:
            c.stt(d[:, :], d[:, :], rp[:, j:j + 1], raw[:, 1:1 + TT], ALU.mult, ALU.add, [dr, rr, 'rpar'], [dr])
            fn = AF.Tanh if j == 6 else (AF.Identity if j == 7 else AF.Sigmoid)
            c.act(self.lora_b[j - 6][:, :], d[:, :], fn, [dr], [('lorab', j - 6)])
    for cc in range(2):
        ps, psr = c.next_psum()
        c.mm(ps[:, :TT], self.rlw[:, 0, cc * 128:(cc + 1) * 128], self.lora_b[0][:, :], True, True, ['rlw', ('lorab', 0)], [psr])
        c.act(self.lw[cc][:, :], ps[:, :TT], AF.Sigmoid, [psr, 'rpar'], [('lw', cc)], bias=rp[:, 10 + cc:11 + cc])
        c.ts('dve', self.lw[cc][:, :], self.lw[cc][:, :], DECAY_C, None, ALU.mult, None, [('lw', cc)], [('lw', cc)])
        ps, psr = c.next_psum()
        c.mm(ps[:, :TT], self.rlw[:, 1, cc * 128:(cc + 1) * 128], self.lora_b[1][:, :], True, True, ['rlw', ('lorab', 1)], [psr])
        c.act(self.av[cc][:, :], ps[:, :TT], AF.Sigmoid, [psr, 'rpar'], [('av', cc)], bias=rp[:, 12 + cc:13 + cc])
    for s in range(NS):
        ps, psr = c.next_psum()
        for kc in range(2):
            c.mm(ps[:, 0:256], self.lora_b[2 + kc][:, s * 128:(s + 1) * 128], self.rlw[:, 2 + kc, :], kc == 0, kc == 1,
                 ['rlw', ('lorab', 2 + kc)], [psr])
        c.copy('act', self.g_tok[s][:, :], ps[:, 0:256], [psr], [('gtok', s)])
    for cc in range(2):
        xr_, xk_, xv_ = self.xs[cc], self.xs[2 + cc], self.xs[4 + cc]
        xrr, xkr, xvr = ('xs', cc), ('xs', 2 + cc), ('xs', 4 + cc)
        c.ts('dve', RT['kkr'][:, :], xk_[:, :], rp[:, 14 + cc:15 + cc], None, ALU.mult, None, [xkr, 'rpar'], ['kkr'])
        c.act(self.rsq[:, :], RT['kkr'][:, :], AF.Square, ['kkr'], ['rsq'])
        ps, psr = c.next_psum()
        c.mm(ps[:, :TT], self.bd_bf[:, :], self.rsq[:, :], True, True, ['cstb', 'rsq'], [psr])
        c.act(RT['rkk'][:, :], ps[:, :TT], AF.Ln, [psr, 'eps'], ['rkk'], bias=self.eps_sb[:, 0:1])
        c.act(RT['rkk'][:, :], RT['rkk'][:, :], AF.Exp, ['rkk'], ['rkk'], scale=-0.5)
        c.tt('dve', RT['kk'][:, :], RT['kkr'][:, :], RT['rkk'][:, :], ALU.mult, ['kkr', 'rkk'], ['kk'])
        c.ts('dve', RT['tq'][:, :], self.av[cc][:, :], rp[:, 16 + cc:17 + cc], self.omka[:, cc:cc + 1], ALU.mult, ALU.add,
             [('av', cc), 'rpar', 'omka'], ['tq'])
        c.tt('dve', RT['kp'][:, :], xk_[:, :], RT['tq'][:, :], ALU.mult, [xkr, 'tq'], ['kp'])
        c.tt('pool', RT['ka'][:, :], RT['kk'][:, :], self.av[cc][:, :], ALU.mult, ['kk', ('av', cc)], ['ka'])
        c.S.add('dve', lambda e, cc=cc: e.tensor_tensor_scan(out=RT['cs'][:, :], data0=self.rmask[:, :], data1=self.lw[cc][:, :],
                                                             initial=0.0, op0=ALU.mult, op1=ALU.add),
                ['rmask', ('lw', cc)], ['cs'])
        c.act(self.Ep[cc][:, :], RT['cs'][:, :], AF.Exp, ['cs'], [('Ep', cc)])
        c.act(RT['Em'][:, :], RT['cs'][:, :], AF.Exp, ['cs'], ['Em'], scale=-1.0)
        c.tt('pool', RT['Ex'][:, :], RT['cs'][:, :], self.lw[cc][:, :], ALU.subtract, ['cs', ('lw', cc)], ['Ex'])
        c.act(RT['Ex'][:, :], RT['Ex'][:, :], AF.Exp, ['Ex'], ['Ex'])
        brr = ('br', cc)
        c.tt('dve', self.br[cc][:, :, 0, :], v3(RT['kk'][:, :]), v3(RT['Ex'][:, :]), ALU.mult, ['kk', 'Ex'], [brr])
        c.tt('dve', self.br[cc][:, :, 1, :], v3(xr_[:, :]), v3(self.Ep[cc][:, :]), ALU.mult, [xrr, ('Ep', cc)], [brr])
        c.stt(RT['akf'][:, :], RT['ka'][:, :], -1.0, RT['Em'][:, :], ALU.mult, ALU.mult, ['ka', 'Em'], ['akf'])
        c.tt('dve', RT['ktf'][:, :], RT['kp'][:, :], RT['Em'][:, :], ALU.mult, ['kp', 'Em'], ['ktf'])
        c.copy('act', self.akT[cc][:, :], RT['akf'][:, :], ['akf'], [('akT', cc)])
        c.copy('act', self.ktT[cc][:, :], RT['ktf'][:, :], ['ktf'], [('ktT', cc)])
        for k8 in range(TT // 64):
            csl = slice(k8 * 64, (k8 + 1) * 64)
            epc = self.Ep[cc][:, k8 * 64 + 63:k8 * 64 + 64]
            c.ts('dve', self.fm3[0][cc][:, csl], RT['ktf'][:, csl], epc, None, ALU.mult, None, ['ktf', ('Ep', cc)], [('fm3', 0, cc)])
            c.ts('pool', self.fm3[1][cc][:, csl], RT['akf'][:, csl], epc, None, ALU.mult, None, ['akf', ('Ep', cc)], [('fm3', 1, cc)])
        c.copy('act', self.fm3[2][cc][:, :], xv_[:, :], [xvr], [('fm3', 2, cc)])
        c.stt(self.prod[:, :], xr_[:, :], rp[:, 18 + cc:19 + cc], RT['kp'][:, :], ALU.mult, ALU.mult, [xrr, 'rpar', 'kp'], ['prod'])
        ps, psr = c.next_psum()
        for s in range(NS):
            c.mm(ps[:, s * 2:s * 2 + 2], self.prod[:, s * 128:(s + 1) * 128], self.hsel[:, :], True, True, ['prod', 'cstb'], [psr])
        for s in range(NS):
            c.copy('dve', self.coef[s][:, cc * 2:cc * 2 + 2], ps[:, s * 2:s * 2 + 2], [psr], [('coef', s)])
        items = []
        for q in range(3):
            for s in range(NS):
                items.append((self.fm3[q][cc][:, s * 128:(s + 1) * 128], [('fm3', q, cc)],
                              lambda pt, res, q=q, s=s, cc=cc: c.copy('act' if (q + s) % 2 else 'dve',
                                                                       self.tok3[q][s][:, cc * 128:(cc + 1) * 128], pt, [res],
                                                                       [('tok3', q, s, cc)])))
        self.tr_batch(items)
    yield 'prep'
    chains = []
    for s in range(NS):
        ssl = slice(s * 128, (s + 1) * 128)
        for hd in range(4):
            cc, hr = hd // 2, slice((hd % 2) * 64, (hd % 2) * 64 + 64)
            ch = NS * 2 + len(chains)
            chains.append((ch, s, hd, self.TTr, 'TTr'))
            brf = self.br[cc][hr, s, :, :].rearrange("p a t -> p (a t)")
            psM, psMr = c.next_psum()
            c.mm(psM[:, 0:128], self.br[cc][hr, s, 0, :], self.akT[cc][hr, ssl], True, True, [('br', cc), ('akT', cc)], [psMr])
            c.mm(psM[:, 128:384], self.akT[cc][hr, ssl], brf, True, True, [('br', cc), ('akT', cc)], [psMr])
            psN, psNr = c.next_psum()
            c.mm(psN[:, 0:256], self.ktT[cc][hr, ssl], brf, True, True, [('br', cc), ('ktT', cc)], [psNr])
            P0, Q0, T0 = self.invP[ch][0], self.invQ[ch][0], self.invT[ch][0]
            c.tt('dve', r32(P0[:, :]), psM[:, 0:128], cf['m_s'], ALU.mult, [psMr, 'cst'], [('invP', ch, 0)])
            c.tt('dve', r32(Q0[:, :]), psM[:, 128:256], cf['mT_s'], ALU.mult, [psMr, 'cst'], [('invQ', ch, 0)])
            c.tt('dve', self.AraT[s][hd][:, :], psM[:, 256:384], cf['mT_i'], ALU.mult, [psMr, 'cst'], [('AraT', s, hd)])
            c.tt('dve', self.AbrkT[s][hd][:, :], psN[:, 0:256], self.cst[:, 10 * 128:12 * 128], ALU.mult, [psNr, 'cst'],
                 [('AbrkT', s, hd)])
            c.tt('dve', r32(T0[:, :]), Q0[:, :], cf['ident'], ALU.add, [('invQ', ch, 0), 'cst'], [('invT', ch, 0)])
    yield ('inv', chains)
    for s in range(NS):
        ps, psr = c.next_psum()
        for hd in range(4):
            vc = slice(hd * 64, hd * 64 + 64)
            cc = hd // 2
            c.mm(ps[:, hd * 64:hd * 64 + 64], self.AbrkT[s][hd][:, 0:128], self.tok3[2][s][:, vc], True, True,
                 [('AbrkT', s, hd), ('tok3', 2, s, cc)], [psr])
            c.mm(ps[:, 256 + hd * 64:256 + hd * 64 + 64], self.AbrkT[s][hd][:, 128:256], self.tok3[2][s][:, vc], True, True,
                 [('AbrkT', s, hd), ('tok3', 2, s, cc)], [psr])
        c.copy('act', self.ZV[s][:, :], ps[:, 0:256], [psr], [('ZV', s)])
        c.copy('act', self.YV[s][:, :], ps[:, 256:512], [psr], [('YV', s)])
    def rstep(ck, cc):
        s, half = ck // 2, ck % 2
        rows = slice(half * 64, half * 64 + 64)
        hb = slice(half * 64, half * 64 + 64)
        ccs = slice(cc * 128, (cc + 1) * 128)
        psZ, psZr = c.next_psum()
        c.mm(psZ[rows, 0:128], self.br[cc][:, s, 0, hb], self.H_b[cc][:, :], True, True, [('br', cc), ('H_b', cc)], [psZr])
        c.mm(psZ[rows, 128:256], self.br[cc][:, s, 1, hb], self.H_b[cc][:, :], True, True, [('br', cc), ('H_b', cc)], [psZr])
        yield
        c.tt('dve', self.Z_bf[cc][rows, :], psZ[rows, 0:128], self.ZV[s][rows, ccs], ALU.add, [psZr, ('ZV', s)], [('Z_bf', cc)])
        c.tt('dve', self.y_tok[s][rows, ccs], psZ[rows, 128:256], self.YV[s][rows, ccs], ALU.add, [psZr, ('YV', s)],
             [('ytok', s, cc)])
        yield
        psU, psUr = c.next_psum()
        for hh in range(2):
            hd = cc * 2 + hh
            hc = slice(hh * 64, hh * 64 + 64)
            c.mm(psU[rows, hc], self.TTr[s][hd][rows, hb], self.Z_bf[cc][rows, hc], True, True,
                 [('TTr', s, hd), ('Z_bf', cc)], [psUr])
        yield
        c.copy('act', self.U_bf[cc][rows, :], psU[rows, 0:128], [psUr], [('U_bf', cc)])
        yield
        psH, psHr = c.next_psum()
        for hh in range(2):
            hd = cc * 2 + hh
            hc = slice(hh * 64, hh * 64 + 64)
            hg = slice(cc * 128 + hh * 64, cc * 128 + hh * 64 + 64)
            c.mm(psH[hc, hc], self.tok3[1][s][rows, hg], self.U_bf[cc][rows, hc], True, False,
                 [('tok3', 1, s, cc), ('U_bf', cc)], [psHr])
            c.mm(psH[hc, hc], self.tok3[0][s][rows, hg], self.tok3[2][s][rows, hg], False, True,
                 [('tok3', 0, s, cc), ('tok3', 2, s, cc)], [psHr])
        for hh in range(2):
            hd = cc * 2 + hh
            hc = slice(hh * 64, hh * 64 + 64)
            c.mm(psU[rows, 128 + hh * 64:128 + hh * 64 + 64], self.AraT[s][hd][rows, hb], self.U_bf[cc][rows, hc], True, True,
                 [('AraT', s, hd), ('U_bf', cc)], [psUr])
        yield
        gcol = self.Ep[cc][:, ck * 64 + 63:ck * 64 + 64]
        for hh in range(2):
            hc = slice(hh * 64, hh * 64 + 64)
            c.stt(self.H_f[cc][hc, hc], self.H_f[cc][hc, hc], gcol[hc, :], psH[hc, hc], ALU.mult, ALU.add,
                  [('H_f', cc), ('Ep', cc), psHr], [('H_f', cc)])
        c.copy('act', self.H_b[cc][:, :], self.H_f[cc][:, :], [('H_f', cc)], [('H_b', cc)])
        c.tt('dve', self.y_tok[s][rows, ccs], psU[rows, 128:256], self.y_tok[s][rows, ccs], ALU.add, [psUr, ('ytok', s, cc)],
             [('ytok', s, cc)])

    for ck in range(2 * NS):
        yield ('seq', [rstep(ck, 0), rstep(ck, 1)])
    gs = self.gst
    for s in range(NS):
        y = self.y_tok[s]
        yr = [('ytok', s, 0), ('ytok', s, 1)]
        y3 = y[:, :].rearrange("p (h n) -> p h n", n=64)
        c.S.add('dve', lambda e, y3=y3: e.tensor_reduce(out=gs['s1'][:, :], in_=y3, axis=AX.X, op=ALU.add), yr, ['gs1'])
        c.act(self.ysq[:, :], y[:, :], AF.Square, yr, ['ysq'])
        c.S.add('dve', lambda e: e.tensor_reduce(out=gs['s2'][:, :], in_=self.ysq[:, :].rearrange("p (h n) -> p h n", n=64),
                                                 axis=AX.X, op=ALU.add), ['ysq'], ['gs2'])
        c.ts('dve', gs['mean'][:, :], gs['s1'][:, :], 1.0 / 64, None, ALU.mult, None, ['gs1'], ['gmean'])
        c.tt('dve', gs['msq'][:, :], gs['mean'][:, :], gs['mean'][:, :], ALU.mult, ['gmean'], ['gmsq'])
        c.stt(gs['var'][:, :], gs['s2'][:, :], 1.0 / 64, gs['msq'][:, :], ALU.mult, ALU.subtract, ['gs2', 'gmsq'], ['gvar'])
        c.act(gs['var'][:, :], gs['var'][:, :], AF.Ln, ['gvar', 'gneps'], ['gvar'], bias=self.gneps[:, 0:1])
        c.act(gs['var'][:, :], gs['var'][:, :], AF.Exp, ['gvar'], ['gvar'], scale=-0.5)
        for hd in range(4):
            hg = slice(hd * 64, hd * 64 + 64)
            c.ts('dve', self.yn[:, hg], y[:, hg], gs['mean'][:, hd:hd + 1], gs['var'][:, hd:hd + 1], ALU.subtract, ALU.mult,
                 yr + ['gmean', 'gvar'], [('yn', hd)])
        ynr = [('yn', hd) for hd in range(4)]
        c.tt('dve', self.yn[:, :], self.yn[:, :], self.rrep[:, 0:256], ALU.mult, ynr + ['rrep'], ynr)
        c.tt('pool', self.yn[:, :], self.yn[:, :], self.rrep[:, 256:512], ALU.add, ynr + ['rrep'], ynr)
        for hd in range(4):
            hg = slice(hd * 64, hd * 64 + 64)
            c.stt(self.yn[:, hg], self.tok3[2][s][:, hg], self.coef[s][:, hd:hd + 1], self.yn[:, hg], ALU.mult, ALU.add,
                  [('tok3', 2, s, hd // 2), ('coef', s), ('yn', hd)], [('yn', hd)])
        c.tt('dve', self.yfin[:, :], self.yn[:, :], self.g_tok[s][:, :], ALU.mult, ynr + [('gtok', s)], ['yfin'])
        items = []
        for cc in range(2):
            items.append((self.yfin[:, cc * 128:(cc + 1) * 128], ['yfin'],
                          lambda pt, res, s=s, cc=cc: c.copy('act', self.obstage[cc][:, s * 128:(s + 1) * 128], pt, [res],
                                                             [('obstage', cc)])))
        self.tr_batch(items)
    for cc in range(2):
        c.dma('sp', self.obT[cc * 128:(cc + 1) * 128, t0:t0 + TT], self.obstage[cc][:, :], [('obstage', cc)], ['obT'], key='obst')


K2.init_rwkv = _init_rwkv
K2.rwkv_tile = _rwkv_tile


def build_k3(D, DFF, VW, T, NT=512, final=False):
    c = Ctx()
    nc = c.nc
    KC = D // 128
    NF = DFF // 128
    VC = VW // 128
    h1T = c.dram("h1T", [D, T], F32, "ExternalInput")
    oa = c.dram("oa", [VW, T], BF16, "ExternalInput")
    ob = c.dram("ob", [VW, T], BF16, "ExternalInput")
    ploc = c.dram("ploc", [VW + 2 * D, T], BF16, "ExternalInput")
    wa_b = c.dram("wa", [VW, D], BF16, "ExternalInput")
    wb_b = c.dram("wb", [VW, D], BF16, "ExternalInput")
    wo_b = c.dram("wo", [D, D], BF16, "ExternalInput")
    g1 = c.dram("g1", [128, KC], F32, "ExternalInput")
    wg_b = c.dram("wg", [D, DFF], BF16, "ExternalInput")
    wu_b = c.dram("wu", [D, DFF], BF16, "ExternalInput")
    wd_b = c.dram("wd", [DFF, D], BF16, "ExternalInput")
    for n_ in ('wa_b', 'wb_b', 'wo_b', 'wg_b', 'wu_b', 'wd_b'):
        c.wres[n_] = []
    h3T = c.dram("h3T", [D, T], F32, "ExternalOutput")
    if final:
        gf = c.dram("gf", [128, KC], F32, "ExternalInput")
        outT = c.dram("outT", [D, T], F32, "ExternalOutput")
    c.init_psum(8)
    h_sb = c.sb("h_sb", [128, KC, NT], F32)
    n_sb = c.sb("n_sb", [128, KC, NT], BF16)
    hid_sb = c.sb("hid_sb", [128, max(NF, 2 * VC + KC), NT], BF16)
    g1_sb = c.sb("g1_sb", [128, KC], F32)
    ones_bf = c.sb("ones_bf", [128, 128], BF16)
    eps_sb = c.sb("eps_sb", [128, 1], F32)
    rstd_sb = c.sb("rstd_sb", [128, NT], F32)
    sq_bufs = [(c.sb("sq%d" % i, [128, NT], BF16), "sq%d" % i) for i in range(2)]
    sg_bufs = [(c.sb("sg%d" % i, [128, NT], F32), "sg%d" % i) for i in range(2)]
    wslots = [((c.sb("wga%d" % i, [128, KC, 512], BF16), "wga%d" % i),
               (c.sb("wua%d" % i, [128, KC, 512], BF16), "wua%d" % i)) for i in range(2)]
    wdslots = [(c.sb("wds%d" % i, [128, 11, 512], BF16), "wds%d" % i) for i in range(2)]
    gt_bufs = [(c.sb("gt%d" % i, [128, 2, NT], BF16), "gt%d" % i) for i in range(2)]
    t_bufs = [(c.sb("tb%d" % i, [128, NT], F32), "tb%d" % i) for i in range(2)]
    c.eps_ap = eps_sb[:, 0:1]
    c.memset('pool', ones_bf[:, :], 1.0, ['ones'])
    c.memset('pool', eps_sb[:, :], NORM_EPS, ['eps'])
    c.dma('sp', g1_sb[:, :], g1[:, :], [], ['g1'], key='g1')
    if final:
        gf_sb = c.sb("gf_sb", [128, KC], F32)
        c.dma('sp', gf_sb[:, :], gf[:, :], [], ['gf'], key='gf')
        fo_bufs = [(c.sb("fo%d" % i, [128, NT], F32), "fo%d" % i) for i in range(2)]
    hv = h1T.rearrange("(kc p) t -> p kc t", p=128)
    h3v = h3T.rearrange("(kc p) t -> p kc t", p=128)
    oav = oa.rearrange("(kc p) t -> p kc t", p=128)
    obv = ob.rearrange("(kc p) t -> p kc t", p=128)
    plv = ploc.rearrange("(kc p) t -> p kc t", p=128)
    wav = wa_b.rearrange("(kc p) f -> p kc f", p=128)
    wbv = wb_b.rearrange("(kc p) f -> p kc f", p=128)
    wov = wo_b.rearrange("(kc p) f -> p kc f", p=128)
    CW = 512 if D % 512 == 0 else 128
    NJ = CW // 128
    YO = 2 * VC
    for tt in range(T // NT):
        tsl = slice(tt * NT, (tt + 1) * NT)
        c.dma('sp', h_sb[:, :, :], hv[:, :, tsl], [], [('h', kc) for kc in range(KC)], key='hload')
        c.dma('sp', n_sb[:, 0:VC, :], oav[:, :, tsl], [], [('n', k) for k in range(VC)], key='oaload')
        c.dma('sp', n_sb[:, VC:2 * VC, :], plv[:, 0:VC, tsl], [], [('n', VC + k) for k in range(VC)], key='zload')
        c.dma('sp', hid_sb[:, VC:2 * VC, :], obv[:, :, tsl], [], [('hid', VC + k) for k in range(VC)], key='obload')
        for k in range(VC):
            sg, sgr = sg_bufs[k % 2]
            c.act(sg[:, :], n_sb[:, VC + k, :], AF.Silu, [('n', VC + k)], [sgr])
            c.tt('dve', hid_sb[:, k, :], sg[:, :], n_sb[:, k, :], ALU.mult, [sgr, ('n', k)], [('hid', k)])
        si = 0
        for cs in range(D // CW):
            (wa_sb, war), (wb_sb, wbr) = wslots[si % 2]
            si += 1
            c.dma('sp', wa_sb[:, :VC, :CW], wav[:, :, cs * CW:(cs + 1) * CW], c.wres['wa_b'], [war], key=war)
            c.dma('sp', wb_sb[:, :VC, :CW], wbv[:, :, cs * CW:(cs + 1) * CW], c.wres['wb_b'], [wbr], key=wbr)
            for j in range(NJ):
                dc = cs * NJ + j
                gt, gtr = gt_bufs[dc % 2]
                c.dma('act', gt[:, 0, :], plv[:, VC + dc, tsl], [], [gtr], key=gtr + 'a')
                c.dma('act', gt[:, 1, :], plv[:, VC + KC + dc, tsl], [], [gtr], key=gtr + 'b')
                pa, par_ = c.next_psum()
                pb, pbr = c.next_psum()
                for k in range(VC):
                    c.mm(pa[:, :NT], wa_sb[:, k, j * 128:(j + 1) * 128], hid_sb[:, k, :], k == 0, k == VC - 1,
                         [war, ('hid', k)], [par_])
                for k in range(VC):
                    c.mm(pb[:, :NT], wb_sb[:, k, j * 128:(j + 1) * 128], hid_sb[:, VC + k, :], k == 0, k == VC - 1,
                         [wbr, ('hid', VC + k)], [pbr])
                sg, sgr = sg_bufs[0]
                sg2, sgr2 = sg_bufs[1]
                c.act(sg[:, :], gt[:, 0, :], AF.Sigmoid, [gtr], [sgr])
                c.act(sg2[:, :], gt[:, 1, :], AF.Sigmoid, [gtr], [sgr2])
                t1, t1r = t_bufs[0]
                t2, t2r = t_bufs[1]
                c.tt('dve', t1[:, :], sg[:, :], pa[:, :NT], ALU.mult, [sgr, par_], [t1r])
                c.tt('dve', t2[:, :], sg2[:, :], pb[:, :NT], ALU.mult, [sgr2, pbr], [t2r])
                c.tt('pool', hid_sb[:, YO + dc, :], t1[:, :], t2[:, :], ALU.add, [t1r, t2r], [('hid', YO + dc)])
        for cs in range(D // CW):
            (wo_sb, wor), _ = wslots[si % 2]
            si += 1
            c.dma('sp', wo_sb[:, :, :CW], wov[:, :, cs * CW:(cs + 1) * CW], c.wres['wo_b'], [wor], key=wor)
            for j in range(NJ):
                dc = cs * NJ + j
                ps, psr = c.next_psum()
                for k in range(KC):
                    c.mm(ps[:, :NT], wo_sb[:, k, j * 128:(j + 1) * 128], hid_sb[:, YO + k, :], k == 0, k == KC - 1,
                         [wor, ('hid', YO + k)], [psr])
                c.tt('dve', h_sb[:, dc, :], ps[:, :NT], h_sb[:, dc, :], ALU.add, [psr, ('h', dc)], [('h', dc)])
        emit_rmsnorm(c, h_sb, 'h', g1_sb, 'g1', n_sb, 'n', KC, NT, D, ones_bf, sq_bufs, rstd_sb)
        emit_ffn(c, n_sb, 'n', h_sb, 'h', hid_sb, wg_b, wu_b, wd_b, ('wg_b', 'wu_b', 'wd_b'), KC, NT, DFF,
                 wslots, wdslots, sg_bufs)
        c.dma('sp', h3v[:, :, tsl], h_sb[:, :, :], [('h', kc) for kc in range(KC)], ['h3T'], key='hstore')
        if final:
            ps, psr = c.next_psum()
            for kc in range(KC):
                sq, sqr = sq_bufs[kc % 2]
                c.act(sq[:, :NT], h_sb[:, kc, :], AF.Square, [('h', kc)], [sqr])
                c.mm(ps[:, :NT], ones_bf[:, :], sq[:, :NT], kc == 0, kc == KC - 1, [sqr, 'ones'], [psr])
            c.act(rstd_sb[:, :NT], ps[:, :NT], AF.Ln, [psr], ['rstd'], bias=c.eps_ap, scale=1.0 / D)
            c.act(rstd_sb[:, :NT], rstd_sb[:, :NT], AF.Exp, ['rstd'], ['rstd'], scale=-0.5)
            for kc in range(KC):
                fo, fr = fo_bufs[kc % 2]
                c.stt(fo[:, :], h_sb[:, kc, :], gf_sb[:, kc:kc + 1], rstd_sb[:, :NT], ALU.mult, ALU.mult,
                      [('h', kc), 'rstd', 'gf'], [fr])
                c.dma('act', outT[kc * 128:(kc + 1) * 128, tsl], fo[:, :], [fr], ['outT'], key='fo%d' % (kc % 2))
    fk = ['hstore'] + (['fo0', 'fo1'] if final else [])
    info = c.S.emit(final_wait_keys=fk)
    c.es.close()
    return nc, info


K0_SPECS = [('wg1', 2048, 5632), ('wu1', 2048, 5632), ('wd1', 5632, 2048), ('wfm', 2048, 11776), ('wa', 1024, 2048),
            ('wb', 1024, 2048), ('wo', 2048, 2048), ('wg2', 2048, 5632), ('wu2', 2048, 5632), ('wd2', 5632, 2048)]


def build_k0(depth, ncores=8):
    c = Ctx()
    keys = []
    i = 0
    for l in range(depth):
        for (n, r, cl) in K0_SPECS:
            rs = r // ncores
            src = c.dram("%s_%d" % (n, l), [rs, cl], F32, "ExternalInput")
            dst = c.dram("%s_%db" % (n, l), [rs, cl], BF16, "ExternalOutput")
            k = 'cast%d' % (i % 8)
            i += 1
            c.dma('pool', dst[:, :], src[:, :], [], [("o", n, l)], key=k)
            if k not in keys:
                keys.append(k)
    c.S.emit(final_wait_keys=keys)
    c.es.close()
    return c.nc


D_MODEL, D_FF, DEPTH, BATCH, SEQ = 2048, 5632, 4, 2, 8192
NCORES = 8
TPC = BATCH * SEQ // NCORES
NFM = 11776
_PROGS = {}


def _prog(name):
    if name not in _PROGS:
        if name == 'k0':
            _PROGS[name] = build_k0(DEPTH)
        elif name == 'k1':
            _PROGS[name] = build_k1(D_MODEL, D_FF, NFM, 16, TPC)[0]
        elif name == 'k2':
            _PROGS[name] = K2(SEQ).c.nc
        elif name == 'k3':
            _PROGS[name] = build_k3(D_MODEL, D_FF, 1024, TPC, final=False)[0]
        elif name == 'k3f':
            _PROGS[name] = build_k3(D_MODEL, D_FF, 1024, TPC, final=True)[0]
    return _PROGS[name]


def _pk(g):
    g = np.asarray(g, np.float32)
    return np.ascontiguousarray(g.reshape(-1, 128).T)


def _pad128(x):
    o = np.zeros((128,) + x.shape[1:], x.dtype)
    o[:x.shape[0]] = x
    return o


def _run(nc, in_maps):
    res = run_bass_kernel_spmd(nc, in_maps, core_ids=list(range(NCORES)))
    return res.results


def kernel(**inp):
    f32 = lambda a: np.asarray(a, np.float32)
    x = f32(inp['x'])
    hT = []
    for cidx in range(NCORES):
        b, tq = cidx // 4, cidx % 4
        hT.append(np.ascontiguousarray(x[b, tq * TPC:(tq + 1) * TPC, :].T))
    cst = k2_consts()
    out = None
    wsrc = {}
    wabs = []
    for l in range(DEPTH):
        w_in = f32(inp['w_in'][l])
        wl_p = np.zeros((D_MODEL, 128), np.float32); wl_p[:, :96] = w_in[:, 4112 + 3072:4112 + 3168]
        al_p = np.zeros((D_MODEL, 128), np.float32); al_p[:, :96] = w_in[:, 4112 + 3168:4112 + 3264]
        wfm = np.concatenate([
            w_in[:, 0:3072], w_in[:, 4112:4112 + 3072], wl_p, al_p, w_in[:, 4112 + 3264:4112 + 3520],
            w_in[:, 3072:4096], w_in[:, 7632:9680], w_in[:, 9680:11728]], axis=1)
        wabs.append(np.ascontiguousarray(w_in[:, 4096:4112]))
        srcs = dict(wg1=inp['ffn1_w_gate'][l], wu1=inp['ffn1_w_up'][l], wd1=inp['ffn1_w_down'][l], wfm=wfm,
                    wa=inp['w_branch_a'][l], wb=inp['w_branch_b'][l], wo=inp['w_out'][l],
                    wg2=inp['ffn2_w_gate'][l], wu2=inp['ffn2_w_up'][l], wd2=inp['ffn2_w_down'][l])
        for n, r, cl in K0_SPECS:
            wsrc[(n, l)] = f32(srcs[n])
        del wfm, w_in
    in0 = []
    for cidx in range(NCORES):
        d = {}
        for (n, l), a in wsrc.items():
            rs = a.shape[0] // NCORES
            d["%s_%d" % (n, l)] = np.ascontiguousarray(a[cidx * rs:(cidx + 1) * rs])
        in0.append(d)
    r0 = _run(_prog('k0'), in0)
    del in0
    wbf = {}
    for (n, l) in list(wsrc.keys()):
        wbf[(n, l)] = np.ascontiguousarray(np.concatenate([r0[cidx]["%s_%db" % (n, l)] for cidx in range(NCORES)], axis=0))
    del r0, wsrc
    for l in range(DEPTH):
        common1 = dict(g1=_pk(inp['ffn1_norm'][l]), g2=_pk(inp['mix_norm'][l]), wg=wbf[('wg1', l)],
                       wu=wbf[('wu1', l)], wd=wbf[('wd1', l)], wfm=wbf[('wfm', l)], wab=wabs[l])
        r1 = _run(_prog('k1'), [dict(hT=hT[cidx], **common1) for cidx in range(NCORES)])
        del common1
        conv = f32(inp['gdn_conv'][l]); a_log = f32(inp['gdn_a_log'][l]); dtb = f32(inp['gdn_dt_bias'][l])
        ogain = f32(inp['gdn_out_norm'][l]); mu = f32(inp['rw_mu'][l])
        w0 = f32(inp['rw_w0'][l]); a0 = f32(inp['rw_a0'][l]); kk_ = f32(inp['rw_k_k'][l]); ka_ = f32(inp['rw_k_a'][l])
        rk_ = f32(inp['rw_r_k'][l]).reshape(-1); lnw = f32(inp['rw_ln_w'][l]); lnb = f32(inp['rw_ln_b'][l])
        w_up = f32(inp['rw_w_up'][l]); a_up = f32(inp['rw_a_up'][l]); g_up = f32(inp['rw_g_up'][l])
        in2 = []
        for m in range(NCORES):
            b, hg = m // 4, m % 4
            PTb = [r1[b * 4 + tq]['PT'] for tq in range(4)]
            rows = lambda r0, n: np.concatenate([p[r0:r0 + n] for p in PTb], axis=1)
            xg = np.concatenate([rows(0 + hg * 256, 256), rows(1024 + hg * 256, 256), rows(2048 + hg * 256, 256)], axis=0)
            xr = np.concatenate([rows(3072 + hg * 256, 256), rows(4096 + hg * 256, 256), rows(5120 + hg * 256, 256),
                                 rows(6144, 512)], axis=0)
            abf = np.concatenate([r1[b * 4 + tq]['ab'] for tq in range(4)], axis=0)
            abd = np.ascontiguousarray(np.concatenate([abf[:, hg * 2:hg * 2 + 2], abf[:, 8 + hg * 2:8 + hg * 2 + 2]], axis=1))
            gpar = np.zeros((128, 156), np.float32)
            for ci in range(6):
                kind, h = ci // 2, ci % 2
                ch0 = kind * 1024 + (hg * 2 + h) * 128
                gpar[:, ci * 4:(ci + 1) * 4] = conv[:, ch0:ch0 + 128].T
            gpar[:, 24:26] = a_log[None, hg * 2:hg * 2 + 2]
            gpar[:, 26:28] = dtb[None, hg * 2:hg * 2 + 2]
            gpar[:, 28:156] = ogain[None, :]
            c0 = hg * 256
            muT = np.concatenate([mu[c0:c0 + 256], mu[1024 + c0:1024 + c0 + 256], mu[2048 + c0:2048 + c0 + 256],
                                  _pad128(mu[3072:3168]), _pad128(mu[3168:3264]), mu[3264:3520]])
            rpar = np.zeros((128, 20), np.float32)
            rpar[:, 0:10] = muT.reshape(10, 128).T
            for cc in range(2):
                sl_ = slice(c0 + cc * 128, c0 + (cc + 1) * 128)
                rpar[:, 10 + cc] = w0[sl_]; rpar[:, 12 + cc] = a0[sl_]; rpar[:, 14 + cc] = kk_[sl_]
                rpar[:, 16 + cc] = ka_[sl_]; rpar[:, 18 + cc] = rk_[sl_]
            rrep = np.ascontiguousarray(np.concatenate([np.broadcast_to(lnw[c0:c0 + 256], (128, 256)),
                                                        np.broadcast_to(lnb[c0:c0 + 256], (128, 256))], axis=1))
            rlw = np.ascontiguousarray(np.stack([_pad128(w_up[:, c0:c0 + 256]), _pad128(a_up[:, c0:c0 + 256]),
                                                 g_up[0:128, c0:c0 + 256], g_up[128:256, c0:c0 + 256]], axis=1))
            in2.append(dict(cst=cst, xg=np.ascontiguousarray(xg), abd=abd, gpar=gpar, xr=np.ascontiguousarray(xr),
                            rpar=rpar, rrep=rrep, rlw=rlw))
        r2 = _run(_prog('k2'), in2)
        del in2
        last = (l == DEPTH - 1)
        common3 = dict(wa=wbf[('wa', l)], wb=wbf[('wb', l)], wo=wbf[('wo', l)],
                       g1=_pk(inp['ffn2_norm'][l]), wg=wbf[('wg2', l)], wu=wbf[('wu2', l)],
                       wd=wbf[('wd2', l)])
        if last:
            common3['gf'] = _pk(inp['final_norm'])
        in3 = []
        for cidx in range(NCORES):
            b, tq = cidx // 4, cidx % 4
            tsl = slice(tq * TPC, (tq + 1) * TPC)
            oa = np.ascontiguousarray(np.concatenate([r2[b * 4 + hg]['oaT'][:, tsl] for hg in range(4)], axis=0))
            ob = np.ascontiguousarray(np.concatenate([r2[b * 4 + hg]['obT'][:, tsl] for hg in range(4)], axis=0))
            ploc = np.ascontiguousarray(r1[cidx]['PT'][6656:11776])
            in3.append(dict(h1T=r1[cidx]['h1T'], oa=oa, ob=ob, ploc=ploc, **common3))
        del r2
        r3 = _run(_prog('k3f' if last else 'k3'), in3)
        del in3, r1
        hT = [r3[cidx]['h3T'] for cidx in range(NCORES)]
        if last:
            out = np.zeros((BATCH, SEQ, D_MODEL), np.float32)
            for cidx in range(NCORES):
                b, tq = cidx // 4, cidx % 4
                out[b, tq * TPC:(tq + 1) * TPC, :] = r3[cidx]['outT'].T
    return out
```
